# Optimizing a Trainium2 kernel written in Bass

```python
import math
import jax, jax.numpy as jnp
from jax import lax
import numpy as np

D_MODEL = 1024
BATCH = 8
SEQ = 4096
DEPTH = 1

GRID_W = 64
CTX_LEN = 256
S5_WIDTH = D_MODEL // 2
S5_GROUP = 16
S5_GROUPS = S5_WIDTH // S5_GROUP
S5_STATE = 64
CONV_WIDTH = D_MODEL - S5_WIDTH
CONV_K = 31
MIX_WIDTH = S5_WIDTH + CONV_WIDTH
IN_COLS = S5_WIDTH + 2 * CONV_WIDTH
D_FF = 4 * D_MODEL
EPS_RMS = 1e-6
EPS_LN = 1e-5
DT_MIN = 1e-3
DT_MAX = 1e-1

kernel_name = "hybrid_s5_conformer_prefix_dit_block"


def _rms_norm(x, g):
    xf = x.astype(jnp.float32)
    y = xf * lax.rsqrt(jnp.mean(xf * xf, axis=-1, keepdims=True) + EPS_RMS)
    return (y * g.astype(jnp.float32)).astype(x.dtype)


def _layer_norm(x, g, b):
    xf = x.astype(jnp.float32)
    mu = jnp.mean(xf, axis=-1, keepdims=True)
    var = jnp.mean(jnp.square(xf - mu), axis=-1, keepdims=True)
    return (xf - mu) * lax.rsqrt(var + EPS_LN) * g.astype(jnp.float32) + b.astype(jnp.float32)


def _modulate(x, shift, scale):
    return x * (1 + scale) + shift


def _s5_discretise(lam_re, lam_im, log_dt, b_re, b_im):
    lam_re = lam_re.astype(jnp.float32)
    lam_im = lam_im.astype(jnp.float32)
    dt = jnp.exp(log_dt.astype(jnp.float32))[..., None]
    mag = jnp.exp(lam_re * dt)
    abar_re = mag * jnp.cos(lam_im * dt)
    abar_im = mag * jnp.sin(lam_im * dt)
    den = lam_re * lam_re + lam_im * lam_im
    num_re = abar_re - 1
    f_re = (num_re * lam_re + abar_im * lam_im) / den
    f_im = (abar_im * lam_re - num_re * lam_im) / den
    b_re = b_re.astype(jnp.float32)
    b_im = b_im.astype(jnp.float32)
    bbar_re = f_re[..., None] * b_re - f_im[..., None] * b_im
    bbar_im = f_re[..., None] * b_im + f_im[..., None] * b_re
    return abar_re, abar_im, bbar_re, bbar_im


def _ccombine(e1, e2):
    a1r, a1i, b1r, b1i = e1
    a2r, a2i, b2r, b2i = e2
    return (a2r * a1r - a2i * a1i,
            a2r * a1i + a2i * a1r,
            a2r * b1r - a2i * b1i + b2r,
            a2r * b1i + a2i * b1r + b2i)


def _cscan(abar_re, abar_im, bu_re, bu_im, reverse):
    shp = (1, bu_re.shape[1]) + abar_re.shape
    a_re = jnp.broadcast_to(abar_re, shp)
    a_im = jnp.broadcast_to(abar_im, shp)
    return lax.associative_scan(_ccombine, (a_re, a_im, bu_re, bu_im), reverse=reverse, axis=1)


def _s5_readout(s_re, s_im, c_re, c_im):
    return (jnp.einsum('blgp,ghp->blgh', s_re, c_re.astype(jnp.float32))
            - jnp.einsum('blgp,ghp->blgh', s_im, c_im.astype(jnp.float32)))


def _s5_glu(y, w_glu):
    g = jax.nn.gelu(y)
    return g * jax.nn.sigmoid(g @ w_glu.astype(jnp.float32))


def _s5_mixer(u_lat, u_ctx, lam_re, lam_im, log_dt, b_re, b_im, c_re, c_im, d_skip, w_glu, need_ctx):
    abar_re, abar_im, bbar_re, bbar_im = _s5_discretise(lam_re, lam_im, log_dt, b_re, b_im)
    bsz, n_lat = u_lat.shape[0], u_lat.shape[1]
    n_ctx = u_ctx.shape[1]
    u4 = u_lat.astype(jnp.float32).reshape(bsz, n_lat, S5_GROUPS, S5_GROUP)
    uc4 = u_ctx.astype(jnp.float32).reshape(bsz, n_ctx, S5_GROUPS, S5_GROUP)
    d_f = d_skip.astype(jnp.float32)
    y = d_f * u4
    yc = d_f * uc4 if need_ctx else None
    for dirn, reverse in ((0, False), (1, True)):
        ar, ai = abar_re[dirn], abar_im[dirn]
        br, bi = bbar_re[dirn], bbar_im[dirn]
        cbu_re = jnp.einsum('bngh,gph->bngp', uc4, br)
        cbu_im = jnp.einsum('bngh,gph->bngp', uc4, bi)
        _, _, cs_re, cs_im = _cscan(ar, ai, cbu_re, cbu_im, reverse)
        end = 0 if reverse else -1
        h0_re = cs_re[:, end][:, None]
        h0_im = cs_im[:, end][:, None]
        bu_re = jnp.einsum('blgh,gph->blgp', u4, br)
        bu_im = jnp.einsum('blgh,gph->blgp', u4, bi)
        ap_re, ap_im, s_re, s_im = _cscan(ar, ai, bu_re, bu_im, reverse)
        s_re = s_re + ap_re * h0_re - ap_im * h0_im
        s_im = s_im + ap_re * h0_im + ap_im * h0_re
        y = y + _s5_readout(s_re, s_im, c_re[dirn], c_im[dirn])
        if need_ctx:
            yc = yc + _s5_readout(cs_re, cs_im, c_re[dirn], c_im[dirn])
    out = _s5_glu(y.reshape(bsz, n_lat, S5_WIDTH), w_glu).astype(u_lat.dtype)
    out_c = _s5_glu(yc.reshape(bsz, n_ctx, S5_WIDTH), w_glu).astype(u_ctx.dtype) if need_ctx else None
    return out, out_c


def _conv_module(v, gate, w_dw, b_dw, ln_g, ln_b, on_grid):
    bsz, n, ch = v.shape
    h = v * jax.nn.sigmoid(gate)
    pad = (CONV_K // 2, CONV_K // 2)
    if on_grid:
        rows = n // GRID_W
        h = lax.conv_general_dilated(
            h.reshape(bsz, rows, GRID_W, ch),
            w_dw.reshape(CONV_K, 1, 1, ch).astype(h.dtype),
            window_strides=(1, 1), padding=(pad, (0, 0)),
            dimension_numbers=('NHWC', 'HWIO', 'NHWC'), feature_group_count=ch)
        h = h.reshape(bsz, n, ch)
    else:
        h = lax.conv_general_dilated(
            h, w_dw.reshape(CONV_K, 1, ch).astype(h.dtype),
            window_strides=(1,), padding=(pad,),
            dimension_numbers=('NWC', 'WIO', 'NWC'), feature_group_count=ch)
    h = h + b_dw
    return jax.nn.silu(_layer_norm(h, ln_g, ln_b)).astype(v.dtype)


def _sq_relu_mlp(h, w1, w2):
    return jnp.square(jax.nn.relu(h @ w1)) @ w2


def setup_inputs(seed: int = 0) -> dict:
    key = jax.random.key(seed)
    ks = jax.random.split(key, 26)
    f32 = jnp.float32
    G, P, H = S5_GROUPS, S5_STATE, S5_GROUP

    def nrm(k, shape, scale):
        return jax.random.normal(k, shape, f32) * scale

    return {
        'x': nrm(ks[0], (BATCH, SEQ, D_MODEL), 1.0),
        'c': nrm(ks[1], (BATCH, D_MODEL), 1.0),
        'ctx': nrm(ks[2], (BATCH, CTX_LEN, D_MODEL), 1.0),
        'c_ctx': nrm(ks[3], (D_MODEL,), 1.0),
        'ada_w': nrm(ks[4], (DEPTH, D_MODEL, 6 * D_MODEL), 0.5 * D_MODEL ** -0.5),
        'ada_b': nrm(ks[5], (DEPTH, 6 * D_MODEL), 0.01),
        'norm1_g': 1.0 + nrm(ks[6], (DEPTH, D_MODEL), 0.02),
        'w_in': nrm(ks[7], (DEPTH, D_MODEL, IN_COLS), D_MODEL ** -0.5),
        's5_lam_re': -0.5 + nrm(ks[8], (DEPTH, 2, G, P), 0.01),
        's5_lam_im': jnp.pi * jnp.arange(P, dtype=f32) + nrm(ks[9], (DEPTH, 2, G, P), 0.01),
        's5_log_dt': jax.random.uniform(ks[10], (DEPTH, 2, G), f32, math.log(DT_MIN), math.log(DT_MAX)),
        's5_b_re': nrm(ks[11], (DEPTH, 2, G, P, H), (2 * H) ** -0.5),
        's5_b_im': nrm(ks[12], (DEPTH, 2, G, P, H), (2 * H) ** -0.5),
        's5_c_re': nrm(ks[13], (DEPTH, 2, G, H, P), P ** -0.5),
        's5_c_im': nrm(ks[14], (DEPTH, 2, G, H, P), P ** -0.5),
        's5_d': nrm(ks[15], (DEPTH, G, H), 1.0),
        's5_w_glu': nrm(ks[16], (DEPTH, S5_WIDTH, S5_WIDTH), S5_WIDTH ** -0.5),
        'conv_w': nrm(ks[17], (DEPTH, CONV_K, CONV_WIDTH), CONV_K ** -0.5),
        'conv_b': nrm(ks[18], (DEPTH, CONV_WIDTH), 0.01),
        'conv_ln_g': 1.0 + nrm(ks[19], (DEPTH, CONV_WIDTH), 0.02),
        'conv_ln_b': nrm(ks[20], (DEPTH, CONV_WIDTH), 0.01),
        'w_out': nrm(ks[21], (DEPTH, MIX_WIDTH, D_MODEL), MIX_WIDTH ** -0.5),
        'norm2_g': 1.0 + nrm(ks[22], (DEPTH, D_MODEL), 0.02),
        'mlp_w1': nrm(ks[23], (DEPTH, D_MODEL, D_FF), D_MODEL ** -0.5),
        'mlp_w2': nrm(ks[24], (DEPTH, D_FF, D_MODEL), D_FF ** -0.5),
        'final_g': 1.0 + nrm(ks[25], (D_MODEL,), 0.02),
    }


def reference(x, c, ctx, c_ctx, ada_w, ada_b, norm1_g, w_in, s5_lam_re, s5_lam_im, s5_log_dt,
              s5_b_re, s5_b_im, s5_c_re, s5_c_im, s5_d, s5_w_glu, conv_w, conv_b, conv_ln_g,
              conv_ln_b, w_out, norm2_g, mlp_w1, mlp_w2, final_g):
    h = x
    hc = ctx
    for layer in range(DEPTH):
        need_ctx = layer < DEPTH - 1
        mod = jax.nn.silu(c) @ ada_w[layer] + ada_b[layer]
        sh1, sc1, g1, sh2, sc2, g2 = jnp.split(mod[:, None, :], 6, axis=-1)
        n_cmod = 6 if need_ctx else 2
        modc = jax.nn.silu(c_ctx) @ ada_w[layer][:, :n_cmod * D_MODEL] + ada_b[layer][:n_cmod * D_MODEL]
        cmods = jnp.split(modc, n_cmod)

        a = _modulate(_rms_norm(h, norm1_g[layer]), sh1, sc1)
        ac = _modulate(_rms_norm(hc, norm1_g[layer]), cmods[1 - 1], cmods[1])
        z = a @ w_in[layer]
        zc = ac @ (w_in[layer] if need_ctx else w_in[layer][:, :S5_WIDTH])
        y_s5, yc_s5 = _s5_mixer(z[..., :S5_WIDTH], zc[..., :S5_WIDTH],
                                s5_lam_re[layer], s5_lam_im[layer], s5_log_dt[layer],
                                s5_b_re[layer], s5_b_im[layer], s5_c_re[layer], s5_c_im[layer],
                                s5_d[layer], s5_w_glu[layer], need_ctx)
        y_conv = _conv_module(z[..., S5_WIDTH:S5_WIDTH + CONV_WIDTH], z[..., S5_WIDTH + CONV_WIDTH:],
                              conv_w[layer], conv_b[layer], conv_ln_g[layer], conv_ln_b[layer], True)
        h = h + g1 * (jnp.concatenate([y_s5, y_conv], axis=-1) @ w_out[layer])

        h = h + g2 * _sq_relu_mlp(_modulate(_rms_norm(h, norm2_g[layer]), sh2, sc2),
                                  mlp_w1[layer], mlp_w2[layer])

        if need_ctx:
            csh1, csc1, cg1, csh2, csc2, cg2 = cmods
            yc_conv = _conv_module(zc[..., S5_WIDTH:S5_WIDTH + CONV_WIDTH], zc[..., S5_WIDTH + CONV_WIDTH:],
                                   conv_w[layer], conv_b[layer], conv_ln_g[layer], conv_ln_b[layer], False)
            hc = hc + cg1 * (jnp.concatenate([yc_s5, yc_conv], axis=-1) @ w_out[layer])
            hc = hc + cg2 * _sq_relu_mlp(_modulate(_rms_norm(hc, norm2_g[layer]), csh2, csc2),
                                         mlp_w1[layer], mlp_w2[layer])
    return _rms_norm(h, final_g)
```

```python
import contextlib
import math
import numpy as np
import concourse.bass as bass
import concourse.mybir as mybir
from concourse.bass_utils import run_bass_kernel_spmd

F32 = mybir.dt.float32
BF16 = mybir.dt.bfloat16
AF = mybir.ActivationFunctionType
ALU = mybir.AluOpType

L = 4096
NCTX = 256
LE = L + NCTX
D = 1024
TCH = 32
NCH = L // TCH
NCC = NCTX // TCH
NCE = NCH + NCC
MAGIC = 12582912.0
TWO_PI = 2.0 * math.pi


class _Op:
    __slots__ = ("eng", "fn", "deps", "sig", "count", "dma", "dsem", "dval")

    def __init__(self, eng, fn):
        self.eng = eng
        self.fn = fn
        self.deps = []
        self.sig = False
        self.count = None
        self.dma = False
        self.dsem = None
        self.dval = None


class Prog:
    ENGS = ("pe", "act", "dve", "pool", "sp")
    NDMA = {"sp": 8, "pool": 4}

    def __init__(self, nc):
        self.nc = nc
        self.q = {e: [] for e in self.ENGS}
        self.lastw = {}
        self.readers = {}
        self.ndma = {e: 0 for e in self.NDMA}
        self.dma_hist = {e: [] for e in self.NDMA}
        self.pending = {e: [] for e in self.ENGS}

    def _add_dep(self, op, d):
        if d is op:
            return
        for x in op.deps:
            if x is d:
                return
        op.deps.append(d)
        if not d.dma:
            d.sig = True

    def _hazards(self, op, reads, writes):
        deps = []
        for k in reads:
            w = self.lastw.get(k)
            if w is not None:
                deps.append(w)
        for k in writes:
            w = self.lastw.get(k)
            if w is not None:
                deps.append(w)
            deps.extend(self.readers.get(k, ()))
        for k in reads:
            self.readers.setdefault(k, []).append(op)
        for k in writes:
            self.lastw[k] = op
            self.readers[k] = []
        for d in deps:
            self._add_dep(op, d)
        pend = self.pending[op.eng]
        if pend:
            for d in pend:
                self._add_dep(op, d)
            self.pending[op.eng] = []

    def op(self, eng, fn, reads=(), writes=()):
        o = _Op(eng, fn)
        self._hazards(o, reads, writes)
        self.q[eng].append(o)
        return o

    def dma(self, eng, out, in_, reads=(), writes=()):
        o = _Op(eng, lambda e: e.dma_start(out=out, in_=in_))
        o.dma = True
        n = self.ndma[eng]
        k = self.NDMA[eng]
        o.dsem = (eng, n % k)
        o.dval = 16 * (n // k + 1)
        self.ndma[eng] = n + 1
        hist = self.dma_hist[eng]
        if n >= k:
            o.deps.append(hist[n - k])
        hist.append(o)
        self._hazards(o, reads, writes)
        self.q[eng].append(o)
        return o

    def barrier(self):
        lasts = []
        for e in self.ENGS:
            for o in reversed(self.q[e]):
                if not o.dma:
                    lasts.append(o)
                    break
        for e, hist in self.dma_hist.items():
            lasts.extend(hist[-self.NDMA[e]:])
        for e in self.ENGS:
            self.pending[e] = list(lasts)
        self.lastw = {}
        self.readers = {}

    def emit(self, final_waits):
        nc = self.nc
        for e in self.ENGS:
            c = 0
            for o in self.q[e]:
                if o.sig and not o.dma:
                    c += 1
                    o.count = c
        with contextlib.ExitStack() as st:
            esem = {e: st.enter_context(nc.semaphore("s_" + e)) for e in ("pe", "act", "dve", "pool")}
            dsem = {}
            for e, k in self.NDMA.items():
                for i in range(k):
                    dsem[(e, i)] = st.enter_context(nc.semaphore("d_%s%d" % (e, i)))
            block = st.enter_context(nc.Block())

            def run(eng_name, engine):
                waited = {}

                def wait_for(d):
                    if d.dma:
                        key, sem, val = d.dsem, dsem[d.dsem], d.dval
                    else:
                        key, sem, val = d.eng, esem[d.eng], d.count
                    if waited.get(key, 0) >= val:
                        return
                    waited[key] = val
                    engine.wait_ge(sem, val)

                for o in self.q[eng_name]:
                    for d in o.deps:
                        wait_for(d)
                    ins = o.fn(engine)
                    if o.dma:
                        ins.then_inc(dsem[o.dsem], 16)
                    elif o.sig:
                        ins.then_inc(esem[eng_name], 1)
                if eng_name == "sp":
                    for d in final_waits:
                        wait_for(d)

            @block.tensor
            def _(eng):
                run("pe", eng)

            @block.scalar
            def _(eng):
                run("act", eng)

            @block.vector
            def _(eng):
                run("dve", eng)

            @block.gpsimd
            def _(eng):
                run("pool", eng)

            @block.sync
            def _(eng):
                run("sp", eng)


NCST = 128 + 128 + 128 + 98 + 5
C_ID, C_ONE, C_BD, C_KE, C_MK = 0, 128, 256, 384, 482


def build_program(stop=None):
    nc = bass.Bass("TRN2", target_bir_lowering=False)

    def din(name, shape, dt=F32):
        return nc.dram_tensor(name, list(shape), dt, kind="ExternalInput").ap()

    x_d = din("x", [L, D])
    ctx_d = din("ctx", [NCTX, D])
    cv_d = din("cvec", [128, 8, 2])
    adaw_d = din("ada_w", [D, 6 * D])
    adab_d = din("ada_b", [128, 48])
    n1g_d = din("n1g", [128, 8])
    n2g_d = din("n2g", [128, 8])
    fg_d = din("fgb", [128, D])
    win_d = din("w_in", [D, 1536])
    wout_d = din("w_out", [D, D])
    w1_d = din("w1", [D, 4 * D])
    w2_d = din("w2", [4 * D, D])
    wglu_d = din("w_glu", [512, 512])
    lre_d = din("lam_re", [128, 32])
    lim_d = din("lam_im", [128, 32])
    ldt_d = din("log_dt", [128, 32])
    bre_d = din("b_re", [128, 32, 16])
    bim_d = din("b_im", [128, 32, 16])
    cre_d = din("c_re", [128, 32, 16])
    cim_d = din("c_im", [128, 32, 16])
    dsk_d = din("d_skip", [128, 4])
    cw_d = din("conv_w", [128, 4, 31])
    cb_d = din("conv_b", [128, 4])
    lng_d = din("ln_g", [128, 4])
    lnb_d = din("ln_b", [128, 4])
    cst_d = din("cst", [128, NCST])
    out_d = nc.dram_tensor("out", [L, D], F32, kind="ExternalOutput").ap()
    mix_d = nc.dram_tensor("mixd", [D, L], BF16, kind=("ExternalOutput" if stop else "Internal")).ap()

    P = Prog(nc)

    def sb(st, name, shape, dt):
        return st.enter_context(nc.sbuf_tensor("sb_" + name, list(shape), dt))

    def pst(st, name, shape, dt):
        full = 512 if dt == F32 else 1024
        nfree = 1
        for d_ in shape[1:]:
            nfree *= d_
        assert nfree <= full
        t = st.enter_context(nc.psum_tensor("ps_" + name, [128, full], dt))
        ap = t[:, 0:nfree]
        if len(shape) == 3:
            ap = ap.rearrange("p (a b) -> p a b", b=shape[2])
        return ap

    dbg_fins = []

    def finish(dumps):
        P.barrier()
        fins = []
        for name, ap, shape, dt in dumps:
            d_ = nc.dram_tensor("dbg_" + name, list(shape), dt, kind="ExternalOutput").ap()
            fins.append(P.dma("sp", d_, ap))
        P.emit(fins)
        return nc

    with contextlib.ExitStack() as glob:
        cst = sb(glob, "cst", [128, NCST], F32)
        identb = sb(glob, "identb", [128, 128], BF16)
        onesb = sb(glob, "onesb", [128, 128], BF16)
        modT = sb(glob, "modT", [128, 48, 2], F32)
        s1 = sb(glob, "s1", [128, 8], F32)
        s1c = sb(glob, "s1c", [128, 8], F32)
        s2 = sb(glob, "s2", [128, 8], F32)
        n1g = sb(glob, "n1g", [128, 8], F32)
        n2g = sb(glob, "n2g", [128, 8], F32)
        dsk = sb(glob, "dsk", [128, 4], F32)
        cw = sb(glob, "cw", [128, 4, 31], F32)
        cb = sb(glob, "cb", [128, 4], F32)
        lng = sb(glob, "lng", [128, 4], F32)
        lnb = sb(glob, "lnb", [128, 4], F32)

        ident = cst[:, C_ID:C_ID + 128]
        onesf = cst[:, C_ONE:C_ONE + 128]
        bdmask = cst[:, C_BD:C_BD + 128]
        kexp = cst[:, C_KE:C_KE + 98]
        mk_even = cst[:, C_MK:C_MK + 1]
        mk_odd = cst[:, C_MK + 1:C_MK + 2]
        mk_hi = cst[:, C_MK + 2:C_MK + 3]

        P.dma("sp", cst[:], cst_d[:, :], writes=["cst"])
        P.dma("sp", n1g[:], n1g_d[:, :], writes=["n1g"])
        P.dma("sp", n2g[:], n2g_d[:, :], writes=["n2g"])
        P.dma("sp", dsk[:], dsk_d[:, :], writes=["dsk"])
        P.dma("sp", cw[:], cw_d[:, :, :], writes=["cw"])
        P.dma("sp", cb[:], cb_d[:, :], writes=["cb"])
        P.dma("sp", lng[:], lng_d[:, :], writes=["lng"])
        P.dma("sp", lnb[:], lnb_d[:, :], writes=["lnb"])
        P.op("dve", lambda e: e.tensor_copy(out=identb[:], in_=ident), reads=["cst"], writes=["identb"])
        P.op("dve", lambda e: e.tensor_copy(out=onesb[:], in_=onesf), reads=["cst"], writes=["onesb"])

        with contextlib.ExitStack() as stB:
            uT = sb(stB, "uT", [128, 4, LE], BF16)
            Sall = sb(stB, "Sall", [128, 32, 2, NCE], BF16)
            Ebf = Sall
            with contextlib.ExitStack() as stB2:
                pw_r = sb(stB2, "pw_r", [128, 32, 98], F32)
                pw_i = sb(stB2, "pw_i", [128, 32, 98], F32)
                bbr = sb(stB2, "bbr", [128, 32, 16], F32)
                bbi = sb(stB2, "bbi", [128, 32, 16], F32)
                Cr = sb(stB2, "Cr", [128, 32, 16], F32)
                Ci = sb(stB2, "Ci", [128, 32, 16], F32)
                nCr = sb(stB2, "nCr", [128, 32, 16], F32)
                nCi = sb(stB2, "nCi", [128, 32, 16], F32)
                Crb = sb(stB2, "Crb", [128, 32, 16], BF16)
                nCib = sb(stB2, "nCib", [128, 32, 16], BF16)
                CA = sb(stB2, "CA", [128, 2, 32], F32)
                CBn = sb(stB2, "CBn", [128, 32], F32)
                CBp = sb(stB2, "CBp", [128, 32], F32)
                with contextlib.ExitStack() as stH:
                    hcT = sb(stH, "hcT", [128, 4, L], BF16)
                    stP = contextlib.ExitStack()
                    cvs = sb(stP, "cvs", [128, 8, 2], F32)
                    csl = sb(stP, "csl", [128, 8, 2], F32)
                    adb = sb(stP, "adb", [128, 48], F32)
                    awb = [sb(stP, "awb%d" % i, [128, 8, 256], F32) for i in range(2)]
                    rowb = [sb(stP, "rowb%d" % i, [128, 256], F32) for i in range(2)]
                    ps_mod = pst(stP, "ps_mod", [128, 48, 2], F32)
                    psR = [pst(stP, "psR%d" % i, [128, 512], F32) for i in range(1)]
                    with contextlib.ExitStack() as st:
                        lre = sb(st, "lre", [128, 32], F32)
                        lim = sb(st, "lim", [128, 32], F32)
                        ldt = sb(st, "ldt", [128, 32], F32)
                        bre = sb(st, "bre", [128, 32, 16], F32)
                        bim = sb(st, "bim", [128, 32, 16], F32)
                        dtt = sb(st, "dtt", [128, 32], F32)
                        lr = sb(st, "lr", [128, 32], F32)
                        li = sb(st, "li", [128, 32], F32)
                        arg = sb(st, "arg", [128, 32, 98], F32)
                        mg = sb(st, "mg", [128, 32, 98], F32)
                        tn = sb(st, "tn", [128, 32, 98], F32)
                        tr_ = sb(st, "tr_", [128, 32, 98], F32)
                        fa = sb(st, "fa", [128, 32], F32)
                        fb = sb(st, "fb", [128, 32], F32)
                        fc = sb(st, "fc", [128, 32], F32)
                        den = sb(st, "den", [128, 32], F32)
                        fre = sb(st, "fre", [128, 32], F32)
                        fim = sb(st, "fim", [128, 32], F32)
                        tb1 = sb(st, "tb1", [128, 32, 16], F32)
                        tb2 = sb(st, "tb2", [128, 32, 16], F32)

                        P.dma("sp", cvs[:], cv_d[:, :, :], writes=["cvs"])
                        P.dma("sp", adb[:], adab_d[:, :], writes=["adb"])
                        P.dma("sp", lre[:], lre_d[:, :], writes=["lre"])
                        P.dma("sp", lim[:], lim_d[:, :], writes=["lim"])
                        P.dma("sp", ldt[:], ldt_d[:, :], writes=["ldt"])
                        P.dma("sp", bre[:], bre_d[:, :, :], writes=["bre"])
                        P.dma("sp", bim[:], bim_d[:, :, :], writes=["bim"])
                        P.dma("sp", Cr[:], cre_d[:, :, :], writes=["Cr"])
                        P.dma("sp", Ci[:], cim_d[:, :, :], writes=["Ci"])
                        P.op("act", lambda e: e.activation(out=csl[:], in_=cvs[:], func=AF.Silu), reads=["cvs"], writes=["csl"])

                        def emit_modT_tr(blk):
                            rb, rbk = rowb[blk % 2], "rowb%d" % (blk % 2)

                            def trm(e):
                                ins = None
                                for ctl in range(2):
                                    ins = e.transpose(out=ps_mod[:, blk * 2 + ctl, :], in_=rb[0:2, ctl * 128:(ctl + 1) * 128], identity=ident[0:2, 0:2])
                                return ins
                            P.op("pe", trm, reads=[rbk, "cst"], writes=["ps_mod"])
                        awv = adaw_d.rearrange("(k p) n -> p k n", p=128)
                        def ada_dma(blk):
                            buf = awb[blk % 2]
                            key = "awb%d" % (blk % 2)
                            P.dma("sp", buf[:], awv[:, :, blk * 256:(blk + 1) * 256], writes=[key])

                        def ada_mm(blk):
                            buf = awb[blk % 2]
                            key = "awb%d" % (blk % 2)
                            pr, prk = psR[0], "psR0"
                            rb, rbk = rowb[blk % 2], "rowb%d" % (blk % 2)

                            def mm(e):
                                ins = None
                                for k in range(8):
                                    ins = e.matmul(pr[0:2, 0:256], lhsT=csl[:, k, :], rhs=buf[:, k, :], start=(k == 0), stop=(k == 7))
                                return ins
                            P.op("pe", mm, reads=[key, "csl"], writes=[prk])
                            P.op("act", lambda e: e.activation(out=rb[0:2, :], in_=pr[0:2, 0:256], func=AF.Copy), reads=[prk], writes=[rbk])
                        ada_dma(0)
                        for blk in range(8):
                            if blk + 1 < 8:
                                ada_dma(blk + 1)
                            ada_mm(blk)
                            if blk >= 1:
                                emit_modT_tr(blk - 1)
                        emit_modT_tr(7)


                        V = lambda fn, r, w: P.op("dve", fn, reads=r, writes=w)
                        A_ = lambda fn, r, w: P.op("act", fn, reads=r, writes=w)
                        A_(lambda e: e.activation(out=dtt[:], in_=ldt[:], func=AF.Exp), ["ldt"], ["dtt"])
                        V(lambda e: e.tensor_tensor(out=lr[:], in0=lre[:], in1=dtt[:], op=ALU.mult), ["lre", "dtt"], ["lr"])
                        V(lambda e: e.tensor_tensor(out=li[:], in0=lim[:], in1=dtt[:], op=ALU.mult), ["lim", "dtt"], ["li"])
                        kb3 = kexp.unsqueeze(1).to_broadcast([128, 32, 98])
                        V(lambda e: e.tensor_tensor(out=mg[:], in0=kb3, in1=lr[:].unsqueeze(2).to_broadcast([128, 32, 98]), op=ALU.mult), ["cst", "lr"], ["mg"])
                        A_(lambda e: e.activation(out=mg[:], in_=mg[:], func=AF.Exp), ["mg"], ["mg"])
                        V(lambda e: e.tensor_tensor(out=arg[:], in0=kb3, in1=li[:].unsqueeze(2).to_broadcast([128, 32, 98]), op=ALU.mult), ["cst", "li"], ["arg"])

                        def trig(dst, shift, key):
                            if shift != 0.0:
                                V(lambda e: e.tensor_scalar(out=tr_[:], in0=arg[:], scalar1=shift, scalar2=None, op0=ALU.add), ["arg"], ["tr_"])
                                srcv = tr_
                            else:
                                V(lambda e: e.tensor_copy(out=tr_[:], in_=arg[:]), ["arg"], ["tr_"])
                                srcv = tr_
                            V(lambda e: e.tensor_scalar(out=tn[:], in0=srcv[:], scalar1=1.0 / TWO_PI, scalar2=MAGIC, op0=ALU.mult, op1=ALU.add), ["tr_"], ["tn"])
                            V(lambda e: e.tensor_scalar(out=tn[:], in0=tn[:], scalar1=-MAGIC, scalar2=None, op0=ALU.add), ["tn"], ["tn"])
                            V(lambda e: e.scalar_tensor_tensor(out=tr_[:], in0=tn[:], scalar=-TWO_PI, in1=srcv[:], op0=ALU.mult, op1=ALU.add), ["tn", "tr_"], ["tr_"])
                            V(lambda e: e.tensor_scalar(out=tr_[:], in0=tr_[:], scalar1=-3.1415925, scalar2=3.1415925, op0=ALU.max, op1=ALU.min), ["tr_"], ["tr_"])
                            A_(lambda e: e.activation(out=tr_[:], in_=tr_[:], func=AF.Sin), ["tr_"], ["tr_"])
                            V(lambda e: e.tensor_tensor(out=dst[:], in0=tr_[:], in1=mg[:], op=ALU.mult), ["tr_", "mg"], [key])
                        trig(pw_i, 0.0, "pw_i")
                        trig(pw_r, math.pi / 2.0, "pw_r")
                        V(lambda e: e.tensor_scalar(out=fa[:], in0=pw_r[:, :, 65], scalar1=-1.0, scalar2=None, op0=ALU.add), ["pw_r"], ["fa"])
                        V(lambda e: e.tensor_tensor(out=den[:], in0=lre[:], in1=lre[:], op=ALU.mult), ["lre"], ["den"])
                        V(lambda e: e.tensor_tensor(out=fb[:], in0=lim[:], in1=lim[:], op=ALU.mult), ["lim"], ["fb"])
                        V(lambda e: e.tensor_tensor(out=den[:], in0=den[:], in1=fb[:], op=ALU.add), ["den", "fb"], ["den"])
                        V(lambda e: e.reciprocal(out=den[:], in_=den[:]), ["den"], ["den"])
                        V(lambda e: e.tensor_tensor(out=fb[:], in0=fa[:], in1=lre[:], op=ALU.mult), ["fa", "lre"], ["fb"])
                        V(lambda e: e.tensor_tensor(out=fc[:], in0=pw_i[:, :, 65], in1=lim[:], op=ALU.mult), ["pw_i", "lim"], ["fc"])
                        V(lambda e: e.tensor_tensor(out=fb[:], in0=fb[:], in1=fc[:], op=ALU.add), ["fb", "fc"], ["fb"])
                        V(lambda e: e.tensor_tensor(out=fre[:], in0=fb[:], in1=den[:], op=ALU.mult), ["fb", "den"], ["fre"])
                        V(lambda e: e.tensor_tensor(out=fb[:], in0=pw_i[:, :, 65], in1=lre[:], op=ALU.mult), ["pw_i", "lre", "fre"], ["fb"])
                        V(lambda e: e.tensor_tensor(out=fc[:], in0=fa[:], in1=lim[:], op=ALU.mult), ["fa", "lim"], ["fc"])
                        V(lambda e: e.tensor_tensor(out=fb[:], in0=fb[:], in1=fc[:], op=ALU.subtract), ["fb", "fc"], ["fb"])
                        V(lambda e: e.tensor_tensor(out=fim[:], in0=fb[:], in1=den[:], op=ALU.mult), ["fb", "den"], ["fim"])
                        frb = fre[:].unsqueeze(2).to_broadcast([128, 32, 16])
                        fib = fim[:].unsqueeze(2).to_broadcast([128, 32, 16])
                        V(lambda e: e.tensor_tensor(out=tb1[:], in0=bre[:], in1=frb, op=ALU.mult), ["bre", "fre"], ["tb1"])
                        V(lambda e: e.tensor_tensor(out=tb2[:], in0=bim[:], in1=fib, op=ALU.mult), ["bim", "fim"], ["tb2"])
                        V(lambda e: e.tensor_tensor(out=bbr[:], in0=tb1[:], in1=tb2[:], op=ALU.subtract), ["tb1", "tb2"], ["bbr"])
                        V(lambda e: e.tensor_tensor(out=tb1[:], in0=bim[:], in1=frb, op=ALU.mult), ["bim", "fre", "bbr"], ["tb1"])
                        V(lambda e: e.tensor_tensor(out=tb2[:], in0=bre[:], in1=fib, op=ALU.mult), ["bre", "fim", "bbr"], ["tb2"])
                        V(lambda e: e.tensor_tensor(out=bbi[:], in0=tb1[:], in1=tb2[:], op=ALU.add), ["tb1", "tb2"], ["bbi"])
                        V(lambda e: e.tensor_scalar(out=nCr[:], in0=Cr[:], scalar1=-1.0, scalar2=None, op0=ALU.mult), ["Cr"], ["nCr"])
                        V(lambda e: e.tensor_scalar(out=nCi[:], in0=Ci[:], scalar1=-1.0, scalar2=None, op0=ALU.mult), ["Ci"], ["nCi"])
                        V(lambda e: e.tensor_copy(out=Crb[:], in_=Cr[:]), ["Cr"], ["Crb"])
                        V(lambda e: e.tensor_copy(out=nCib[:], in_=nCi[:]), ["nCi"], ["nCib"])
                        V(lambda e: e.tensor_copy(out=CA[:, 0, :], in_=pw_r[:, :, 64]), ["pw_r"], ["CA"])
                        V(lambda e: e.tensor_copy(out=CA[:, 1, :], in_=pw_r[:, :, 64]), ["pw_r", "CA"], ["CA"])
                        V(lambda e: e.tensor_copy(out=CBp[:], in_=pw_i[:, :, 64]), ["pw_i"], ["CBp"])
                        V(lambda e: e.tensor_scalar(out=CBn[:], in0=pw_i[:, :, 64], scalar1=-1.0, scalar2=None, op0=ALU.mult), ["pw_i"], ["CBn"])

                        P.op("dve", lambda e: e.tensor_tensor(out=modT[:, 0:16, :], in0=ps_mod[:, 0:16, :], in1=adb[:, 0:16].unsqueeze(2).to_broadcast([128, 16, 2]), op=ALU.add),
                             reads=["ps_mod", "adb"], writes=["modT"])
                        P.op("dve", lambda e: e.scalar_tensor_tensor(out=s1[:], in0=modT[:, 8:16, 0], scalar=1.0, in1=n1g[:], op0=ALU.add, op1=ALU.mult),
                             reads=["modT", "n1g"], writes=["s1"])
                        P.op("dve", lambda e: e.scalar_tensor_tensor(out=s1c[:], in0=modT[:, 8:16, 1], scalar=1.0, in1=n1g[:], op0=ALU.add, op1=ALU.mult),
                             reads=["modT", "n1g"], writes=["s1c"])
                    P.barrier()
                    if stop == "B1":
                        return finish([("pw_r", pw_r[:], [128, 32, 98], F32), ("pw_i", pw_i[:], [128, 32, 98], F32), ("bbr", bbr[:], [128, 32, 16], F32), ("bbi", bbi[:], [128, 32, 16], F32)])
                    with contextlib.ExitStack() as st:
                        winb = sb(st, "winb", [128, 8, 1536], BF16)
                        xt = [sb(st, "xt%d" % i, [128, D], F32) for i in range(3)]
                        junk = sb(st, "junk", [128, D], BF16)
                        ssq4 = [sb(st, "ssqa%d" % i, [128, 4], F32) for i in range(4)]
                        xs = [sb(st, "xs%d" % i, [128, D], BF16) for i in range(2)]
                        aT = [sb(st, "aT%d" % i, [128, 8, 512], BF16) for i in range(2)]
                        ssq = [sb(st, "ssq%d" % i, [128, 4], F32) for i in range(3)]
                        sig = [sb(st, "sig%d" % i, [128, 512], F32) for i in range(2)]
                        psT = [pst(st, "psTA%d" % i, [128, 8, 128], BF16) for i in range(2)]
                        psz = [pst(st, "psz%d" % i, [128, 512], F32) for i in range(4)]

                        for k in range(8):
                            P.dma("pool", winb[:, k, :], win_d[k * 128:(k + 1) * 128, :], writes=["winb%d" % k])
                        winkeys = ["winb%d" % k for k in range(8)]

                        tiles = []
                        for kb in range(L // 512):
                            for tt in range(4):
                                tiles.append((kb, tt, False))
                        for tt in range(2):
                            tiles.append((L // 512, tt, True))
                        NT = len(tiles)
                        xt4 = xt
                        zc = [0]

                        def a_s1a(t):
                            kb, tt, is_ctx = tiles[t]
                            xi = t % 3
                            xb, xk = xt4[xi], "xt%d" % xi
                            sq, sk = ssq4[t % 4], "ssq%d" % (t % 4)
                            src = ctx_d[tt * 128:(tt + 1) * 128, :] if is_ctx else x_d[kb * 512 + tt * 128: kb * 512 + (tt + 1) * 128, :]
                            P.dma("sp", xb[:], src, writes=[xk])
                            P.op("act", lambda e: e.activation(out=junk[:], in_=xb[:], func=AF.Square, accum_out=sq[:, 0:1]), reads=[xk], writes=["junk", sk])
                            P.op("act", lambda e: e.activation(out=sq[:, 2:3], in_=sq[:, 0:1], func=AF.Sqrt, bias=cst[:, C_MK + 3:C_MK + 4], scale=1.0 / D), reads=[sk], writes=[sk])

                        def a_s1b(t):
                            xi = t % 3
                            xb, xk = xt4[xi], "xt%d" % xi
                            sq, sk = ssq4[t % 4], "ssq%d" % (t % 4)
                            xsb, xsk = xs[t % 2], "xs%d" % (t % 2)
                            P.op("dve", lambda e: e.reciprocal(out=sq[:, 3:4], in_=sq[:, 2:3]), reads=[sk], writes=[sk])
                            P.op("dve", lambda e: e.tensor_scalar(out=xsb[:], in0=xb[:], scalar1=sq[:, 3:4], scalar2=None, op0=ALU.mult), reads=[xk, sk], writes=[xsk])

                        def a_s2(t):
                            kb, tt, is_ctx = tiles[t]
                            xsb, xsk = xs[t % 2], "xs%d" % (t % 2)
                            pt, ptk = psT[t % 2], "psTA%d" % (t % 2)
                            a, ak = aT[kb % 2], "aT%d" % (kb % 2)
                            sc_t = s1c if is_ctx else s1
                            shcol = 1 if is_ctx else 0

                            def tr(e):
                                ins = None
                                for j in range(8):
                                    ins = e.transpose(out=pt[:, j, :], in_=xsb[:, j * 128:(j + 1) * 128], identity=identb[:])
                                return ins
                            P.op("pe", tr, reads=[xsk], writes=[ptk])

                            def ev(e):
                                ins = None
                                for j in range(8):
                                    ins = e.activation(out=a[:, j, tt * 128:(tt + 1) * 128], in_=pt[:, j, :], func=AF.Identity,
                                                       bias=modT[:, j, shcol:shcol + 1], scale=sc_t[:, j:j + 1])
                                return ins
                            P.op("act", ev, reads=[ptk], writes=[ak + "_%d" % tt])

                        def a_block(kb):
                            is_ctx = kb == L // 512
                            ntile = 2 if is_ctx else 4
                            ntok = ntile * 128
                            a, ak = aT[kb % 2], "aT%d" % (kb % 2)
                            akeys = [ak + "_%d" % tt for tt in range(ntile)]
                            tok0 = L if is_ctx else kb * 512

                            def mm_cols(pz, c0):
                                def f_(e):
                                    ins = None
                                    for k in range(8):
                                        ins = e.matmul(pz[:, 0:ntok], lhsT=winb[:, k, c0:c0 + 128], rhs=a[:, k, 0:ntok], start=(k == 0), stop=(k == 7))
                                    return ins
                                return f_
                            for ct in range(4):
                                pz, pzk = psz[zc[0] % 4], "psz%d" % (zc[0] % 4)
                                zc[0] += 1
                                P.op("pe", mm_cols(pz, ct * 128), reads=akeys + winkeys, writes=[pzk])
                                P.op("act", lambda e, pz=pz, ct=ct: e.activation(out=uT[:, ct, tok0:tok0 + ntok], in_=pz[:, 0:ntok], func=AF.Copy),
                                     reads=[pzk], writes=["uT%d_%d" % (ct, kb)])
                            if is_ctx:
                                return
                            for ct in range(4):
                                pv, pvk = psz[zc[0] % 4], "psz%d" % (zc[0] % 4)
                                zc[0] += 1
                                pg, pgk = psz[zc[0] % 4], "psz%d" % (zc[0] % 4)
                                zc[0] += 1
                                P.op("pe", mm_cols(pg, 1024 + ct * 128), reads=akeys + winkeys, writes=[pgk])
                                P.op("pe", mm_cols(pv, 512 + ct * 128), reads=akeys + winkeys, writes=[pvk])
                                sg, sgk = sig[ct % 2], "sig%d" % (ct % 2)
                                P.op("act", lambda e, sg=sg, pg=pg: e.activation(out=sg[:], in_=pg[:], func=AF.Sigmoid), reads=[pgk], writes=[sgk])
                                P.op("dve", lambda e, sg=sg, pv=pv, ct=ct: e.tensor_tensor(out=hcT[:, ct, kb * 512:(kb + 1) * 512], in0=pv[:], in1=sg[:], op=ALU.mult),
                                     reads=[pvk, sgk], writes=["hcT%d_%d" % (ct, kb)])

                        ada_dma(8)
                        a_s1a(0)
                        a_s1a(1)
                        a_s1b(0)
                        for t in range(NT):
                            if t + 2 < NT:
                                a_s1a(t + 2)
                            if t + 1 < NT:
                                a_s1b(t + 1)
                            a_s2(t)
                            if t < 16:
                                if t + 1 < 16:
                                    ada_dma(8 + t + 1)
                                ada_mm(8 + t)
                                if t >= 1:
                                    emit_modT_tr(8 + t - 1)
                            if t == 16:
                                emit_modT_tr(23)
                                P.op("dve", lambda e: e.tensor_tensor(out=modT[:, 16:48, :], in0=ps_mod[:, 16:48, :], in1=adb[:, 16:48].unsqueeze(2).to_broadcast([128, 32, 2]), op=ALU.add),
                                     reads=["ps_mod", "adb"], writes=["modT2"])
                                P.op("dve", lambda e: e.scalar_tensor_tensor(out=s2[:], in0=modT[:, 32:40, 0], scalar=1.0, in1=n2g[:], op0=ALU.add, op1=ALU.mult),
                                     reads=["modT2", "n2g"], writes=["s2"])
                            kb, tt, is_ctx = tiles[t]
                            if tt == (1 if is_ctx else 3):
                                a_block(kb)
                    P.barrier()
                    stP.close()
                    if stop == "A":
                        return finish([("uT", uT[:], [128, 4, LE], BF16), ("hcT", hcT[:], [128, 4, L], BF16), ("modT", modT[:], [128, 48, 2], F32), ("G1b", G1b[:], [128, D], F32)])
                    with contextlib.ExitStack() as st:
                        ABr = sb(st, "ABr", [128, 8, 32, 16], BF16)
                        ABi = sb(st, "ABi", [128, 8, 32, 16], BF16)
                        CAr = sb(st, "CAr", [128, 8, 32, 16], BF16)
                        CAi = sb(st, "CAi", [128, 8, 32, 16], BF16)
                        tA = [sb(st, "tA%d" % i, [128, 8, 2, 16], F32) for i in range(2)]
                        tB = [sb(st, "tB%d" % i, [128, 8, 2, 16], F32) for i in range(2)]
                        Wp = sb(st, "Wp", [128, 8, 4, 2, 128], BF16)
                        W2 = sb(st, "W2", [128, 2, 32, 128], BF16)
                        Zc = sb(st, "Zc", [128, 1, 8, 8, 16], BF16)
                        Up = sb(st, "Up", [128, 8, LE // 8], BF16)
                        Bbrb = sb(st, "Bbrb", [128, 32, 16], BF16)
                        nBbib = sb(st, "nBbib", [128, 32, 16], BF16)
                        psT1 = pst(st, "psT1", [128, 8, 128], BF16)
                        psS = [pst(st, "psS%d" % i, [128, 2, NCE], F32) for i in range(4)]
                        psK = pst(st, "psK", [128, 4, 128], F32)
                        psY = [pst(st, "psYt%d" % i, [128, 16, 32], F32) for i in range(2)]
                        P.op("dve", lambda e: e.tensor_copy(out=Bbrb[:], in_=bbr[:]), reads=["bb"], writes=["Bbrb"])
                        P.op("dve", lambda e: e.tensor_scalar(out=nBbib[:], in0=bbi[:], scalar1=-1.0, scalar2=None, op0=ALU.mult), reads=["bb"], writes=["nBbib"])
                        yc = 0
                        sc_ = [0]
                        ycc = [0]

                        def prep_up(j, sp_list):
                            g0 = 8 * j
                            ukeys = ["uT%d_%d" % (j, kb) for kb in range(9)]
                            upkeys = ["Up%d" % sp for sp in range(5)]
                            abkeys = ["ABr%d" % q for q in range(16)] + ["ABi%d" % q for q in range(16)]
                            cakeys = ["CAr%d" % q for q in range(16)] + ["CAi%d" % q for q in range(16)]
                            for sp in sp_list:
                                ncol = 128 if sp < 4 else 32

                                def tr1(e, sp=sp, ncol=ncol, j=j):
                                    ins = None
                                    for s8 in range(8):
                                        ins = e.transpose(out=psT1[0:ncol, s8, :], in_=uT[:, j, 1024 * sp + s8:1024 * sp + 8 * ncol:8], identity=identb[:])
                                    return ins
                                P.op("pe", tr1, reads=ukeys, writes=["psT1"])
                                P.op("act", lambda e, sp=sp, ncol=ncol: e.activation(out=Zc[0:ncol, 0, :, :, :].rearrange("p g s h -> p s g h"),
                                                                                   in_=psT1[0:ncol].rearrange("p s (g h) -> p s g h", h=16), func=AF.Copy),
                                     reads=["psT1"], writes=["Zc0"])

                                def tr2(e, sp=sp, ncol=ncol):
                                    ins = None
                                    for gl in range(8):
                                        ins = e.transpose(out=psT1[:, gl, 0:ncol], in_=Zc[0:ncol, 0, gl, :, :].rearrange("p s h -> p (s h)"), identity=identb[0:ncol, 0:ncol])
                                    return ins
                                P.op("pe", tr2, reads=["Zc0"], writes=["psT1"])
                                P.op("dve", lambda e, sp=sp, ncol=ncol: e.tensor_copy(out=Up[:, :, 128 * sp:128 * sp + ncol], in_=psT1[:, :, 0:ncol]),
                                     reads=["psT1"], writes=["Up%d" % sp])

                        def prep_ab(j, sh_list, which="all"):
                            g0 = 8 * j
                            ukeys = ["uT%d_%d" % (j, kb) for kb in range(9)]
                            upkeys = ["Up%d" % sp for sp in range(5)]
                            abkeys = ["ABr%d" % q for q in range(16)] + ["ABi%d" % q for q in range(16)]
                            cakeys = ["CAr%d" % q for q in range(16)] + ["CAi%d" % q for q in range(16)]
                            for sh in sh_list:
                                s0 = sh * 2
                                bsh = [128, 8, 2, 16]
                                p1r = pw_r[:, g0:g0 + 8, s0:s0 + 2].unsqueeze(3).to_broadcast(bsh)
                                p1i = pw_i[:, g0:g0 + 8, s0:s0 + 2].unsqueeze(3).to_broadcast(bsh)
                                ptr = pw_r[:, g0:g0 + 8, 66 + s0:66 + s0 + 2].unsqueeze(3).to_broadcast(bsh)
                                pti = pw_i[:, g0:g0 + 8, 66 + s0:66 + s0 + 2].unsqueeze(3).to_broadcast(bsh)
                                br_ = bbr[:, g0:g0 + 8, :].unsqueeze(2).to_broadcast(bsh)
                                bi_ = bbi[:, g0:g0 + 8, :].unsqueeze(2).to_broadcast(bsh)
                                cr_ = Cr[:, g0:g0 + 8, :].unsqueeze(2).to_broadcast(bsh)
                                ci_ = Ci[:, g0:g0 + 8, :].unsqueeze(2).to_broadcast(bsh)

                                def cplx(eng, ta, tb, kA, kB, a1, b1, a2, b2, op, dst, dkey):
                                    P.op(eng, lambda e: e.tensor_tensor(out=ta[:], in0=a1, in1=b1, op=ALU.mult), reads=["pw", "bb"], writes=[kA])
                                    P.op(eng, lambda e: e.tensor_tensor(out=tb[:], in0=a2, in1=b2, op=ALU.mult), reads=["pw", "bb"], writes=[kB])
                                    P.op(eng, lambda e: e.tensor_tensor(out=dst, in0=ta[:], in1=tb[:], op=op), reads=[kA, kB], writes=[dkey])
                                if which in ("all", "ab"):
                                    cplx("dve", tA[0], tA[1], "tA0", "tA1", p1r, br_, p1i, bi_, ALU.subtract, ABr[:, :, s0:s0 + 2, :], "ABr%d" % sh)
                                    cplx("pool", tB[0], tB[1], "tB0", "tB1", p1r, bi_, p1i, br_, ALU.add, ABi[:, :, s0:s0 + 2, :], "ABi%d" % sh)
                                if which in ("all", "ca"):
                                    cplx("dve", tA[0], tA[1], "tA0", "tA1", ptr, cr_, pti, ci_, ALU.subtract, CAr[:, :, s0:s0 + 2, :], "CAr%d" % sh)
                                    cplx("pool", tB[0], tB[1], "tB0", "tB1", pti, cr_, ptr, ci_, ALU.add, CAi[:, :, s0:s0 + 2, :], "CAi%d" % sh)

                        def main_pre(j):
                            g0 = 8 * j
                            ukeys = ["uT%d_%d" % (j, kb) for kb in range(9)]
                            upkeys = ["Up%d" % sp for sp in range(5)]
                            abkeys = ["ABr%d" % q for q in range(16)] + ["ABi%d" % q for q in range(16)]
                            cakeys = ["CAr%d" % q for q in range(16)] + ["CAi%d" % q for q in range(16)]
                            for gl in range(8):
                                def trw(e, gl=gl):
                                    ins = None
                                    for q in range(4):
                                        for ri, AB in enumerate((ABr, ABi)):
                                            ins = e.transpose(out=psT1[:, q * 2 + ri, :], in_=AB[:, gl, 8 * q:8 * q + 8, :].rearrange("p s h -> p (s h)"), identity=identb[:])
                                    return ins
                                P.op("pe", trw, reads=abkeys, writes=["psT1"])

                                def evw(e, gl=gl):
                                    o_ = Wp[:, gl, :, :, :].rearrange("p q r m -> p (q r m)")
                                    i_ = psT1[:].rearrange("p a m -> p (a m)")
                                    e.activation(out=o_[:, 0:512], in_=i_[:, 0:512], func=AF.Copy)
                                    return e.activation(out=o_[:, 512:1024], in_=i_[:, 512:1024], func=AF.Copy)
                                P.op("act", evw, reads=["psT1"], writes=["Wp%d" % gl])
                            for gl in range(8):
                                for ri in range(2):
                                    ps_, psk = psS[sc_[0] % 4], "psS%d" % (sc_[0] % 4)
                                    sc_[0] += 1

                                    def mms(e, gl=gl, ri=ri, ps_=ps_):
                                        ins = None
                                        for q in range(4):
                                            ins = e.matmul(ps_[:, 0, :], lhsT=Wp[:, gl, q, ri, :], rhs=Up[:, gl, q:LE // 8:4], start=(q == 0), stop=(q == 3))
                                        return ins
                                    P.op("pe", mms, reads=["Wp%d" % gl] + upkeys, writes=[psk])
                                    P.op("act", lambda e, gl=gl, ri=ri, ps_=ps_, g0=g0: e.activation(out=Sall[:, g0 + gl, ri, :], in_=ps_[:, 0, :], func=AF.Copy),
                                         reads=[psk], writes=["Sall"])
                            for dr in range(2):
                                rows = slice(64 * dr, 64 * dr + 64)
                                for t4 in range(8):
                                    def mmk(e, rows=rows, t4=t4, g0=g0):
                                        ins = None
                                        for q in range(4):
                                            tau = t4 * 4 + q
                                            o_ = psK[:, q, :].rearrange("p (g h) -> p g h", h=16)
                                            e.matmul(o_, lhsT=Bbrb[rows, g0:g0 + 8, :].rearrange("p g h -> p (g h)"), rhs=CAr[rows, :, tau, :], start=True, stop=False)
                                            ins = e.matmul(o_, lhsT=nBbib[rows, g0:g0 + 8, :].rearrange("p g h -> p (g h)"), rhs=CAi[rows, :, tau, :], start=False, stop=True)
                                        return ins
                                    P.op("pe", mmk, reads=cakeys + ["Bbrb", "nBbib"], writes=["psK"])
                                    P.op("dve", lambda e, dr=dr, t4=t4: e.tensor_tensor(out=W2[:, dr, t4 * 4:(t4 + 1) * 4, :], in0=psK[:], in1=bdmask.unsqueeze(1).to_broadcast([128, 4, 128]), op=ALU.mult),
                                         reads=["psK", "cst"], writes=["W2_%d" % dr])

                        def taps(j, kb):
                            g0 = 8 * j
                            ukeys = ["uT%d_%d" % (j, kb) for kb in range(9)]
                            upkeys = ["Up%d" % sp for sp in range(5)]
                            abkeys = ["ABr%d" % q for q in range(16)] + ["ABi%d" % q for q in range(16)]
                            cakeys = ["CAr%d" % q for q in range(16)] + ["CAi%d" % q for q in range(16)]
                            if True:
                                py, pyk = psY[ycc[0] % 2], "psYt%d" % (ycc[0] % 2)
                                ycc[0] += 1
                                u3 = uT[:, j, kb * 512:(kb + 1) * 512].rearrange("p (c r) -> p c r", r=32)

                                def mmt(e, py=py, u3=u3):
                                    ins = None
                                    first = True
                                    for dr in range(2):
                                        for tau in range(32):
                                            if dr == 0:
                                                o_ = py[:, :, tau:32]
                                                r_ = u3[:, :, 0:32 - tau]
                                            else:
                                                o_ = py[:, :, 0:32 - tau]
                                                r_ = u3[:, :, tau:32]
                                            ins = e.matmul(o_, lhsT=W2[:, dr, tau, :], rhs=r_, start=first, stop=(dr == 1 and tau == 31))
                                            first = False
                                    return ins
                                ukey = "uT%d_%d" % (j, kb)
                                P.op("pe", mmt, reads=["W2_0", "W2_1", ukey], writes=[pyk])
                                P.op("dve", lambda e, py=py, u3=u3, j=j: e.scalar_tensor_tensor(out=u3, in0=u3, scalar=dsk[:, j:j + 1], in1=py[:], op0=ALU.mult, op1=ALU.add),
                                     reads=[pyk, "dsk"], writes=[ukey])

                        prep_ab(0, range(16), "ab")
                        prep_up(0, range(5))
                        prep_ab(0, range(16), "ca")
                        for j in range(4):
                            if j > 0:
                                prep_up(j, range(5))
                            main_pre(j)
                            for kb in range(8):
                                taps(j, kb)
                                if j + 1 < 4:
                                    prep_ab(j + 1, [2 * kb, 2 * kb + 1])
                    P.barrier()
                    if stop == "B2":
                        return finish([("uT", uT[:], [128, 4, LE], BF16), ("Sall", Sall[:], [128, 32, 2, NCE], BF16)])
                    with contextlib.ExitStack() as st:
                        E32 = sb(st, "E32", [128, 2, 32, NCH + 1], F32)
                        Ec32 = sb(st, "Ec32", [128, 2, 32, NCC + 1], F32)
                        sp1 = sb(st, "sp1", [128, 2, 32], F32)
                        sp2 = sb(st, "sp2", [128, 2, 32], F32)
                        sq_ = sb(st, "sq_", [128, 2, 32], F32)
                        def scan_steps():
                            steps = []
                            Sv = lambda rows, c: Sall[rows, :, :, c].rearrange("p g r -> p r g")
                            for half, eng in ((0, "dve"), (1, "pool")):
                                rows = slice(64 * half, 64 * half + 64)
                                hk = "scan%d" % half
                                seq = []
                                c_init = 0 if half == 0 else NCC
                                seq.append(("init", None))
                                order_c = list(range(NCC)) if half == 0 else list(range(NCC - 1, -1, -1))
                                for c in order_c:
                                    seq.append(("ctx", c))
                                seq.append(("seed", None))
                                order_m = list(range(NCH)) if half == 0 else list(range(NCH - 1, -1, -1))
                                for c in order_m:
                                    seq.append(("main", c))
                                steps.append((half, eng, rows, hk, seq))
                            return steps

                        def emit_scan_step(half, eng, rows, hk, item):
                            kind, c = item
                            if kind == "init":
                                ci = 0 if half == 0 else NCC
                                P.op(eng, lambda e: e.memset(Ec32[rows, :, :, ci], 0.0), reads=[], writes=[hk + "X"])
                                return
                            if kind == "seed":
                                src = Ec32[rows, :, :, NCC] if half == 0 else Ec32[rows, :, :, 0]
                                dst = E32[rows, :, :, 0] if half == 0 else E32[rows, :, :, NCH]
                                P.op(eng, lambda e: e.tensor_copy(out=dst, in_=src), reads=[hk + "X"], writes=[hk + "X"])
                                return
                            Ebuf = Ec32 if kind == "ctx" else E32
                            scol = NCH + c if kind == "ctx" else c
                            if half == 0:
                                cin, cout = c, c + 1
                            else:
                                cin, cout = c + 1, c
                            X = Ebuf[rows, :, :, cin]
                            Xo = Ebuf[rows, :, :, cout]
                            S_ = Sall[rows, :, :, scol].rearrange("p g r -> p r g")
                            P.op(eng, lambda e: e.tensor_tensor(out=sp1[rows], in0=X, in1=CA[rows], op=ALU.mult), reads=[hk + "X", "CA"], writes=[hk + "p1"])
                            P.op(eng, lambda e: e.tensor_tensor(out=sp2[rows, 0, :], in0=Ebuf[rows, 1, :, cin], in1=CBn[rows], op=ALU.mult), reads=[hk + "X", "CB"], writes=[hk + "p2a"])
                            P.op(eng, lambda e: e.tensor_tensor(out=sp2[rows, 1, :], in0=Ebuf[rows, 0, :, cin], in1=CBp[rows], op=ALU.mult), reads=[hk + "X", "CB"], writes=[hk + "p2b"])
                            P.op(eng, lambda e: e.tensor_tensor(out=sq_[rows], in0=sp1[rows], in1=S_, op=ALU.add), reads=[hk + "p1", "Sall"], writes=[hk + "q"])
                            P.op(eng, lambda e: e.tensor_tensor(out=Xo, in0=sq_[rows], in1=sp2[rows], op=ALU.add), reads=[hk + "q", hk + "p2a", hk + "p2b"], writes=[hk + "X"])

                        steps = scan_steps()
                        pos = [0, 0]

                        def advance_scan(n):
                            for (half, eng, rows, hk, seq) in steps:
                                for _ in range(n):
                                    if pos[half] < len(seq):
                                        emit_scan_step(half, eng, rows, hk, seq[pos[half]])
                                        pos[half] += 1

                        cvb = [sb(st, "cvb%d" % i, [128, 4, 512], BF16) for i in range(2)]
                        csq = [sb(st, "csq%d" % i, [128, 512], BF16) for i in range(1)]
                        mean = sb(st, "mean", [128, 512], F32)
                        rsd = sb(st, "rsd", [128, 512], F32)
                        ctm = [sb(st, "ctm%d" % i, [128, 512], F32) for i in range(1)]
                        cob = [sb(st, "cob%d" % i, [128, 512], BF16) for i in range(1)]
                        DgAll = sb(st, "DgAll", [128, 4, 31, 128], BF16)
                        psC = [pst(st, "psC%d" % i, [128, 512], F32) for i in range(2)]
                        psM = [pst(st, "psM%d" % i, [128, 512], F32) for i in range(2)]
                        psQ = [pst(st, "psQ%d" % i, [128, 512], F32) for i in range(2)]
                        def build_dg(j):
                            def f_(e):
                                ins = None
                                for k in range(31):
                                    ins = e.activation(out=DgAll[:, j, k, :], in_=ident, func=AF.Copy, scale=cw[:, j, k:k + 1])
                                return ins
                            P.op("act", f_, reads=["cst", "cw"], writes=["Dg%d" % j])
                        cc = [0]

                        def conv_X(kb):
                            cvt, cvk = cvb[kb % 2], "cvb%d" % (kb % 2)
                            pm, pmk = psM[kb % 2], "psM%d" % (kb % 2)
                            pq, pqk = psQ[kb % 2], "psQ%d" % (kb % 2)

                            def stats(j):
                                cs_, csk = csq[0], "csq0"
                                P.op("pe", lambda e: e.matmul(pm[:], lhsT=onesb[:], rhs=cvt[:, j, :], start=(j == 0), stop=(j == 3)),
                                     reads=[cvk + "_%d" % j, "onesb"], writes=[pmk])
                                P.op("pe", lambda e: e.matmul(pq[:], lhsT=onesb[:], rhs=cs_[:], start=(j == 0), stop=(j == 3)),
                                     reads=[csk, "onesb"], writes=[pqk])

                            def mm_part(j):
                                pc, pck = psC[cc[0] % 2], "psC%d" % (cc[0] % 2)
                                cc[0] += 1

                                def mmc(e):
                                    ins = None
                                    taps = [15] + [k for k in range(31) if k != 15]
                                    todo = []
                                    for k in taps:
                                        dl = 64 * (k - 15)
                                        lo = max(512 * kb, -dl)
                                        hi = min(512 * kb + 512, L - dl)
                                        if lo < hi:
                                            todo.append((k, lo, hi, dl))
                                    for n_, (k, lo, hi, dl) in enumerate(todo):
                                        ins = e.matmul(pc[:, lo - 512 * kb:hi - 512 * kb], lhsT=DgAll[:, j, k, :], rhs=hcT[:, j, lo + dl:hi + dl],
                                                       start=(n_ == 0), stop=(n_ == len(todo) - 1))
                                    return ins
                                P.op("pe", mmc, reads=["Dg%d" % j], writes=[pck])
                                return pc, pck

                            def act_part(j, pc, pck):
                                P.op("act", lambda e: e.activation(out=cvt[:, j, :], in_=pc[:], func=AF.Identity, bias=cb[:, j:j + 1], scale=1.0),
                                     reads=[pck], writes=[cvk + "_%d" % j])
                                cs_, csk = csq[0], "csq0"
                                P.op("act", lambda e: e.activation(out=cs_[:], in_=cvt[:, j, :], func=AF.Square), reads=[cvk + "_%d" % j], writes=[csk])
                            for j in range(4):
                                if kb == 0:
                                    build_dg(j)
                                pc, pck = mm_part(j)
                                if j >= 1:
                                    stats(j - 1)
                                act_part(j, pc, pck)
                                advance_scan(3)
                            stats(3)

                        def conv_Y(kb):
                            cvt, cvk = cvb[kb % 2], "cvb%d" % (kb % 2)
                            pm, pmk = psM[kb % 2], "psM%d" % (kb % 2)
                            pq, pqk = psQ[kb % 2], "psQ%d" % (kb % 2)
                            P.op("dve", lambda e: e.tensor_scalar(out=mean[:], in0=pm[:], scalar1=1.0 / 512.0, scalar2=None, op0=ALU.mult), reads=[pmk], writes=["mean"])
                            P.op("dve", lambda e: e.tensor_tensor(out=rsd[:], in0=mean[:], in1=mean[:], op=ALU.mult), reads=["mean"], writes=["rsd"])
                            P.op("dve", lambda e: e.scalar_tensor_tensor(out=rsd[:], in0=pq[:], scalar=1.0 / 512.0, in1=rsd[:], op0=ALU.mult, op1=ALU.subtract),
                                 reads=[pqk, "rsd"], writes=["rsd"])
                            P.op("act", lambda e: e.activation(out=rsd[:], in_=rsd[:], func=AF.Sqrt, bias=cst[:, C_MK + 4:C_MK + 5], scale=1.0), reads=["rsd"], writes=["rsd"])
                            P.op("dve", lambda e: e.reciprocal(out=rsd[:], in_=rsd[:]), reads=["rsd"], writes=["rsd"])

                            def ln(j):
                                ct_, ctk = ctm[0], "ctm0"
                                co_, cok = cob[0], "cob0"
                                P.op("dve", lambda e: e.tensor_tensor(out=ct_[:], in0=cvt[:, j, :], in1=mean[:], op=ALU.subtract),
                                     reads=[cvk + "_%d" % j, "mean"], writes=[ctk])
                                P.op("dve", lambda e: e.tensor_tensor(out=ct_[:], in0=ct_[:], in1=rsd[:], op=ALU.mult), reads=[ctk, "rsd"], writes=[ctk])
                                P.op("act", lambda e: e.activation(out=co_[:], in_=ct_[:], func=AF.Silu, bias=lnb[:, j:j + 1], scale=lng[:, j:j + 1]),
                                     reads=[ctk], writes=[cok])
                                P.dma("sp", mix_d[512 + j * 128:512 + (j + 1) * 128, kb * 512:(kb + 1) * 512], co_[:], reads=[cok], writes=["mixd_c%d_%d" % (j, kb)])
                            for j in range(4):
                                ln(j)
                                advance_scan(3)

                        conv_X(0)
                        for kb in range(8):
                            if kb + 1 < 8:
                                conv_X(kb + 1)
                            conv_Y(kb)
                        advance_scan(10000)
                        for ri in range(2):
                            P.op("act", lambda e, ri=ri: e.activation(out=Ebf[0:64, :, ri, 0:NCH], in_=E32[0:64, ri, :, 0:NCH], func=AF.Copy), reads=["scan0X"], writes=["Sall"])
                            P.op("act", lambda e, ri=ri: e.activation(out=Ebf[64:128, :, ri, 0:NCH], in_=E32[64:128, ri, :, 1:NCH + 1], func=AF.Copy), reads=["scan1X"], writes=["Sall"])
                    P.barrier()
                    if stop == "B3":
                        return finish([("Sall", Sall[:], [128, 32, 2, NCE], BF16)])
                with contextlib.ExitStack() as st:
                    W3 = [sb(st, "W3_%d" % i, [128, 8, 2, 32, 16], BF16) for i in range(2)]
                    wglu = sb(st, "wglu", [128, 4, 512], BF16)
                    P.dma("pool", wglu[:], wglu_d.rearrange("(k p) n -> p k n", p=128), writes=["wglu"])
                    wa = [sb(st, "wa%d" % i, [128, 4, 32, 16], F32) for i in range(2)]
                    wb_ = [sb(st, "wb%d" % i, [128, 4, 32, 16], F32) for i in range(2)]
                    Yc = [sb(st, "Yc%d" % i, [128, 32, 8, 16], BF16) for i in range(2)]
                    ytm = [sb(st, "ytm%d" % i, [128, 8, 128], F32) for i in range(2)]
                    sgl = [sb(st, "sgl%d" % i, [128, 512], BF16) for i in range(2)]
                    msb = [sb(st, "msb%d" % i, [128, 512], BF16) for i in range(2)]
                    psY = [pst(st, "psYr%d" % i, [128, 512], F32) for i in range(2)]
                    psT = [pst(st, "psTr%d" % i, [128, 8, 128], BF16) for i in range(2)]
                    psL = [pst(st, "psL%d" % i, [128, 512], F32) for i in range(2)]
                    yc = 0
                    tcn = 0
                    def build_w3(j):
                        g0 = 8 * j
                        W3j, w3k = W3[j % 2], "W3_%d" % (j % 2)
                        for ri in range(2):
                            for hf in range(2):
                                eng = "pool" if (ri == 1 and hf == 1) else "dve"
                                t1, t2 = (wb_ if eng == "pool" else wa)
                                k1, k2 = ("wb0", "wb1") if eng == "pool" else ("wa0", "wa1")
                                gs = slice(g0 + 4 * hf, g0 + 4 * hf + 4)
                                p3r = pw_r[:, gs, 32:64].unsqueeze(3).to_broadcast([128, 4, 32, 16])
                                p3i = pw_i[:, gs, 32:64].unsqueeze(3).to_broadcast([128, 4, 32, 16])
                                if ri == 0:
                                    c1 = Cr[:, gs, :].unsqueeze(2).to_broadcast([128, 4, 32, 16])
                                    c2 = nCi[:, gs, :].unsqueeze(2).to_broadcast([128, 4, 32, 16])
                                    pa_, pb_ = p3r, p3i
                                else:
                                    c1 = nCr[:, gs, :].unsqueeze(2).to_broadcast([128, 4, 32, 16])
                                    c2 = nCi[:, gs, :].unsqueeze(2).to_broadcast([128, 4, 32, 16])
                                    pa_, pb_ = p3i, p3r
                                P.op(eng, lambda e, t1=t1, c1=c1, pa_=pa_: e.tensor_tensor(out=t1[:], in0=c1, in1=pa_, op=ALU.mult), reads=["pw", "C"], writes=[k1])
                                P.op(eng, lambda e, t2=t2, c2=c2, pb_=pb_: e.tensor_tensor(out=t2[:], in0=c2, in1=pb_, op=ALU.mult), reads=["pw", "C"], writes=[k2])
                                P.op(eng, lambda e, t1=t1, t2=t2, W3j=W3j, hf=hf, ri=ri: e.tensor_tensor(out=W3j[:, 4 * hf:4 * hf + 4, ri, :, :], in0=t1[:], in1=t2[:], op=ALU.add),
                                     reads=[k1, k2], writes=[w3k + "_%d%d" % (ri, hf)])

                    build_w3(0)
                    for j in range(4):
                        g0 = 8 * j
                        W3j, w3k = W3[j % 2], "W3_%d" % (j % 2)
                        Ycj, yck = Yc[j % 2], "Yc%d" % (j % 2)
                        if j + 1 < 4:
                            build_w3(j + 1)
                        w3keys = [w3k + "_%d%d" % (ri, hf) for ri in range(2) for hf in range(2)]
                        for gl in range(8):
                            py, pyk = psY[yc % 2], "psYr%d" % (yc % 2)
                            yc += 1

                            def mmr(e, py=py, g=g0 + gl, gl=gl, W3j=W3j):
                                e.matmul(py[:], lhsT=Ebf[:, g, 0, 0:NCH], rhs=W3j[:, gl, 0, :, :].rearrange("p r h -> p (r h)"), start=True, stop=False)
                                return e.matmul(py[:], lhsT=Ebf[:, g, 1, 0:NCH], rhs=W3j[:, gl, 1, :, :].rearrange("p r h -> p (r h)"), start=False, stop=True)
                            P.op("pe", mmr, reads=["Ebf"] + w3keys, writes=[pyk])
                            P.op("act", lambda e, py=py, Ycj=Ycj, gl=gl: e.activation(out=Ycj[:, :, gl, :], in_=py[:].rearrange("p (r h) -> p r h", h=16), func=AF.Copy), reads=[pyk], writes=[yck + "_%d" % gl])
                        ykeys = [yck + "_%d" % gl for gl in range(8)]
                        uv = uT[:, j, 0:L].rearrange("p (c r) -> p r c", r=32)
                        for r8 in range(4):
                            pt, ptk = psT[tcn % 2], "psTr%d" % (tcn % 2)
                            ym, ymk = ytm[tcn % 2], "ytm%d" % (tcn % 2)
                            tcn += 1

                            def trr(e, pt=pt, Ycj=Ycj, r8=r8):
                                ins = None
                                for rr in range(8):
                                    r = r8 * 8 + rr
                                    ins = e.transpose(out=pt[:, rr, :], in_=Ycj[:, r, :, :].rearrange("p g h -> p (g h)"), identity=identb[:])
                                return ins
                            P.op("pe", trr, reads=ykeys + ["identb"], writes=[ptk])
                            ukeys = ["uT%d_%d" % (j, kb) for kb in range(8)]
                            P.op("dve", lambda e, pt=pt, ym=ym, uv=uv, r8=r8: e.tensor_tensor(out=ym[:], in0=pt[:], in1=uv[:, r8 * 8:(r8 + 1) * 8, :], op=ALU.add),
                                 reads=[ptk] + ukeys, writes=[ymk])
                            P.op("act", lambda e, ym=ym, uv=uv, r8=r8: e.activation(out=uv[:, r8 * 8:(r8 + 1) * 8, :], in_=ym[:], func=AF.Gelu_apprx_tanh),
                                 reads=[ymk], writes=ukeys)
                    lc = 0
                    for kb in range(8):
                        for jo in range(4):
                            pl, plk = psL[lc % 2], "psL%d" % (lc % 2)
                            sg, sgk = sgl[lc % 2], "sgl%d" % (lc % 2)
                            ms, msk = msb[lc % 2], "msb%d" % (lc % 2)
                            lc += 1

                            def mml(e, pl=pl, jo=jo, kb=kb):
                                ins = None
                                for k in range(4):
                                    ins = e.matmul(pl[:], lhsT=wglu[:, k, jo * 128:(jo + 1) * 128], rhs=uT[:, k, kb * 512:(kb + 1) * 512], start=(k == 0), stop=(k == 3))
                                return ins
                            P.op("pe", mml, reads=["wglu"] + ["uT%d_%d" % (k, kb) for k in range(4)], writes=[plk])
                            P.op("act", lambda e, pl=pl, sg=sg: e.activation(out=sg[:], in_=pl[:], func=AF.Sigmoid), reads=[plk], writes=[sgk])
                            P.op("dve", lambda e, sg=sg, ms=ms, jo=jo, kb=kb: e.tensor_tensor(out=ms[:], in0=uT[:, jo, kb * 512:(kb + 1) * 512], in1=sg[:], op=ALU.mult),
                                 reads=[sgk, "uT%d_%d" % (jo, kb)], writes=[msk])
                            P.dma("sp", mix_d[jo * 128:(jo + 1) * 128, kb * 512:(kb + 1) * 512], ms[:], reads=[msk], writes=["mixd_s%d_%d" % (jo, kb)])
                P.barrier()
        if stop == "B4":
            return finish([("modT", modT[:], [128, 48, 2], F32)])
        with contextlib.ExitStack() as st:
            w1b = sb(st, "w1b", [128, 8, 4 * D], BF16)
            G1b = sb(st, "G1b", [128, D], F32)
            G2b = sb(st, "G2b", [128, D], F32)
            FGb = sb(st, "FGb", [128, D], F32)
            dg = [sb(st, "dg%d" % i, [128, 128], F32) for i in range(2)]
            if stop != "C00b":
                P.dma("sp", FGb[:], fg_d[:, :], writes=["FGb"])
            wout = sb(st, "wout", [128, 8, D], BF16)
            if stop != "C00a":
                P.dma("pool", wout[:], wout_d.rearrange("(k p) n -> p k n", p=128), writes=["wout"])
            w2b = sb(st, "w2b", [128, 32, D], BF16)
            mixb = [sb(st, "mixb%d" % i, [128, 8, 256], BF16) for i in range(2)]
            xh = [sb(st, "xh%d" % i, [128, D], F32) for i in range(4)]
            tmpG = [sb(st, "tmpG%d" % i, [128, 512], F32) for i in range(1)]
            xs = [sb(st, "xsc%d" % i, [128, D], BF16) for i in range(1)]
            junkc = xs[0]
            a2T = [sb(st, "a2T%d" % i, [128, 8, 256], BF16) for i in range(2)]
            ssq = [sb(st, "ssc%d" % i, [128, 8], F32) for i in range(4)]
            rT = [sb(st, "rT%d" % i, [128, 256], BF16) for i in range(3)]
            hT = [sb(st, "hT%d" % i, [128, 256], BF16) for i in range(3)]
            h2 = [sb(st, "h2_%d" % i, [128, D], F32) for i in range(2)]
            pso = [pst(st, "pso%d" % i, [128, 512], F32) for i in range(4)]
            psf = [pst(st, "psf%d" % i, [128, 256], F32) for i in range(3)]
            psx = pst(st, "psx", [128, 512], F32)
            psxT = psx.bitcast(BF16).rearrange("p (a m) -> p a m", m=128)

            if stop in ("C00", "C00a", "C00b"):
                return finish([("modT", modT[:], [128, 48, 2], F32)])
            for gi, (base, Gb) in enumerate(((16, G1b), (40, G2b))):
                for j in range(8):
                    d_ = dg[j % 2]
                    dk = "dg%d" % (j % 2)
                    P.op("dve", lambda e, d_=d_, base=base, j=j: e.tensor_scalar(out=d_[:], in0=ident, scalar1=modT[:, base + j, 0:1], scalar2=None, op0=ALU.mult),
                         reads=["modT", "cst"], writes=[dk])
                    pg = pso[gi * 2 + j // 4]
                    pk = "pso%d" % (gi * 2 + j // 4)
                    P.op("pe", lambda e, pg=pg, d_=d_, j=j: e.matmul(pg[:, (j % 4) * 128:(j % 4 + 1) * 128], lhsT=onesf, rhs=d_[:], start=True, stop=True),
                         reads=[dk, "cst"], writes=[pk + "_%d" % (j % 4)])
                for h in range(2):
                    pg = pso[gi * 2 + h]
                    pk = "pso%d" % (gi * 2 + h)
                    P.op("act", lambda e, pg=pg, Gb=Gb, h=h: e.activation(out=Gb[:, h * 512:(h + 1) * 512], in_=pg[:], func=AF.Copy),
                         reads=[pk + "_%d" % q for q in range(4)], writes=["G%d" % gi])
            P.barrier()
            if stop == "C0":
                return finish([("G1b", G1b[:], [128, D], F32)])
            w1v = w1_d.rearrange("(k p) n -> p k n", p=128)
            for k in range(8):
                P.dma("pool", w1b[:, k, :], w1v[:, k, :], writes=["w1b%d" % k])
            w2v = w2_d.rearrange("(f p) n -> p f n", p=128)
            for f4 in range(8):
                P.dma("pool", w2b[:, f4 * 4:(f4 + 1) * 4, :], w2v[:, f4 * 4:(f4 + 1) * 4, :], writes=["w2b%d" % f4])
            w1keys = ["w1b%d" % k for k in range(8)]
            finals = []
            NBLK = L // 256
            mixv = mix_d.rearrange("(k p) t -> p k t", p=128)

            def c_load(blk):
                t0 = blk * 256
                mb, mbk = mixb[blk % 2], "mixb%d" % (blk % 2)
                P.dma("sp", mb[:], mixv[:, :, t0:t0 + 256], writes=[mbk])
                for tt in range(2):
                    xi = (blk * 2 + tt) % 4
                    P.dma("sp", xh[xi][:], x_d[t0 + tt * 128:t0 + (tt + 1) * 128, :], writes=["xh%d" % xi])

            def c_wout(blk, tt, hf):
                mb, mbk = mixb[blk % 2], "mixb%d" % (blk % 2)
                xi = (blk * 2 + tt) % 4
                xb, xk = xh[xi], "xh%d" % xi
                cs = slice(hf * 512, (hf + 1) * 512)
                tg, tgk = tmpG[0], "tmpG0"

                def mmo(e):
                    ins = None
                    for k in range(8):
                        ins = e.matmul(psx[:], lhsT=mb[:, k, tt * 128:(tt + 1) * 128], rhs=wout[:, k, cs], start=(k == 0), stop=(k == 7))
                    return ins
                P.op("pe", mmo, reads=[mbk, "wout"], writes=["psx"])
                P.op("dve", lambda e: e.tensor_tensor(out=tg[:], in0=psx[:], in1=G1b[:, cs], op=ALU.mult), reads=["psx"], writes=[tgk])
                P.op("dve", lambda e: e.tensor_tensor(out=xb[:, cs], in0=xb[:, cs], in1=tg[:], op=ALU.add), reads=[tgk, xk], writes=[xk])

            def c_rms_act(blk, tt):
                xi = (blk * 2 + tt) % 4
                xb, xk = xh[xi], "xh%d" % xi
                sq, sk = ssq[xi], "ssc%d" % xi
                P.op("act", lambda e: e.activation(out=junkc[:], in_=xb[:], func=AF.Square, accum_out=sq[:, 0:1]), reads=[xk], writes=["xsc0", sk])
                P.op("act", lambda e: e.activation(out=sq[:, 2:3], in_=sq[:, 0:1], func=AF.Sqrt, bias=cst[:, C_MK + 3:C_MK + 4], scale=1.0 / D), reads=[sk], writes=[sk])

            def c_rms_dve(blk, tt):
                xi = (blk * 2 + tt) % 4
                xb, xk = xh[xi], "xh%d" % xi
                sq, sk = ssq[xi], "ssc%d" % xi
                xsb, xsk = xs[0], "xsc0"
                P.op("dve", lambda e: e.reciprocal(out=sq[:, 3:4], in_=sq[:, 2:3]), reads=[sk], writes=[sk])
                P.op("dve", lambda e: e.tensor_scalar(out=xsb[:], in0=xb[:], scalar1=sq[:, 3:4], scalar2=None, op0=ALU.mult), reads=[xk, sk], writes=[xsk])

            def c_tr(blk, tt):
                xsb, xsk = xs[0], "xsc0"

                def tr(e):
                    ins = None
                    for j in range(8):
                        ins = e.transpose(out=psxT[:, j, :], in_=xsb[:, j * 128:(j + 1) * 128], identity=identb[:])
                    return ins
                P.op("pe", tr, reads=[xsk], writes=["psx"])

            def c_ev(blk, tt):
                a2, a2k = a2T[blk % 2], "a2T%d" % (blk % 2)

                def ev(e):
                    ins = None
                    for j in range(8):
                        ins = e.activation(out=a2[:, j, tt * 128:(tt + 1) * 128], in_=psxT[:, j, :], func=AF.Identity,
                                           bias=modT[:, 24 + j, 0:1], scale=s2[:, j:j + 1])
                    return ins
                P.op("act", ev, reads=["psx"], writes=[a2k + "_%d" % tt])

            def prologue_steps(blk):
                st_ = []
                for tt in range(2):
                    st_.append(lambda tt=tt: c_wout(blk, tt, 0))
                    st_.append(lambda tt=tt: c_wout(blk, tt, 1))
                    st_.append(lambda tt=tt: c_rms_act(blk, tt))
                    st_.append(lambda tt=tt: c_rms_dve(blk, tt))
                    st_.append(lambda tt=tt: c_tr(blk, tt))
                    st_.append(lambda tt=tt: c_ev(blk, tt))
                return st_

            fcn = [0]

            def emit_w1(blk, f):
                a2, a2k = a2T[blk % 2], "a2T%d" % (blk % 2)
                i_ = fcn[0] % 3
                fcn[0] += 1
                pf, pfk = psf[i_], "psf%d" % i_
                r_, rk = rT[i_], "rT%d" % i_
                h_, hk_ = hT[i_], "hT%d" % i_

                def mm1(e):
                    ins = None
                    for k in range(8):
                        ins = e.matmul(pf[:], lhsT=w1b[:, k, f * 128:(f + 1) * 128], rhs=a2[:, k, :], start=(k == 0), stop=(k == 7))
                    return ins
                P.op("pe", mm1, reads=[a2k + "_0", a2k + "_1"] + w1keys, writes=[pfk])
                P.op("act", lambda e: e.activation(out=r_[:], in_=pf[:], func=AF.Relu), reads=[pfk], writes=[rk])
                P.op("pool", lambda e: e.tensor_tensor(out=h_[:], in0=r_[:], in1=r_[:], op=ALU.mult), reads=[rk], writes=[hk_])
                return h_, hk_

            def emit_w2(f, h_, hk_):
                def mm2(e):
                    ins = None
                    for tt in range(2):
                        for hf in range(2):
                            ins = e.matmul(pso[tt * 2 + hf][:], lhsT=h_[:, tt * 128:(tt + 1) * 128], rhs=w2b[:, f, hf * 512:(hf + 1) * 512], start=(f == 0), stop=(f == 31))
                    return ins
                P.op("pe", mm2, reads=[hk_, "w2b%d" % (f // 4)], writes=["pso%d" % i for i in range(4)])

            def epilogue(blk):
                for tt in range(2):
                    epi_mult(blk, tt)
                for tt in range(2):
                    epi_tile(blk, tt)

            def epi_mult(blk, tt):
                h2b, h2k = h2[tt], "h2_%d" % tt
                for hf in range(2):
                    cs = slice(hf * 512, (hf + 1) * 512)
                    P.op("dve", lambda e, hf=hf, cs=cs: e.tensor_tensor(out=h2b[:, cs], in0=pso[tt * 2 + hf][:], in1=G2b[:, cs], op=ALU.mult),
                         reads=["pso%d" % (tt * 2 + hf)], writes=[h2k + "_%d" % hf])

            def epi_tile(blk, tt):
                t0 = blk * 256
                if True:
                    xi = (blk * 2 + tt) % 4
                    xb, xk = xh[xi], "xh%d" % xi
                    h2b, h2k = h2[tt], "h2_%d" % tt
                    sq, sk = ssq[xi], "ssc%d" % xi
                    for hf in range(2):
                        cs = slice(hf * 512, (hf + 1) * 512)
                        P.op("dve", lambda e, cs=cs: e.tensor_tensor(out=h2b[:, cs], in0=h2b[:, cs], in1=xb[:, cs], op=ALU.add),
                             reads=[xk, h2k + "_%d" % hf], writes=[h2k + "_%d" % hf])
                    hk3 = [h2k + "_0", h2k + "_1"]
                    P.op("act", lambda e: e.activation(out=junkc[:], in_=h2b[:], func=AF.Square, accum_out=sq[:, 4:5]), reads=hk3, writes=["xsc0", sk])
                    P.op("act", lambda e: e.activation(out=sq[:, 6:7], in_=sq[:, 4:5], func=AF.Sqrt, bias=cst[:, C_MK + 3:C_MK + 4], scale=1.0 / D), reads=[sk], writes=[sk])
                    P.op("dve", lambda e: e.reciprocal(out=sq[:, 7:8], in_=sq[:, 6:7]), reads=[sk], writes=[sk])
                    P.op("dve", lambda e: e.scalar_tensor_tensor(out=h2b[:], in0=h2b[:], scalar=sq[:, 7:8], in1=FGb[:], op0=ALU.mult, op1=ALU.mult),
                         reads=hk3 + [sk], writes=hk3)
                    finals.append(P.dma("sp", out_d[t0 + tt * 128:t0 + (tt + 1) * 128, :], h2b[:], reads=hk3, writes=["out%d_%d" % (blk, tt)]))

            c_load(0)
            for stp in prologue_steps(0):
                stp()
            for blk in range(NBLK):
                nxt_steps = []
                if blk + 1 < NBLK:
                    c_load(blk + 1)
                    nxt_steps = prologue_steps(blk + 1)
                sched = {}
                for n_, stp in enumerate(nxt_steps):
                    sched[4 + 2 * n_] = stp
                pend = [emit_w1(blk, 0), emit_w1(blk, 1)]
                for f in range(32):
                    if f + 2 < 32:
                        pend.append(emit_w1(blk, f + 2))
                    h_, hk_ = pend.pop(0)
                    emit_w2(f, h_, hk_)
                    if f in sched:
                        sched[f]()
                epilogue(blk)
            P.emit(finals)
    return nc


_PROG = None


def _consts():
    c = np.zeros((128, NCST), np.float32)
    c[:, C_ID:C_ID + 128] = np.eye(128, dtype=np.float32)
    c[:, C_ONE:C_ONE + 128] = 1.0
    p = np.arange(128)
    c[:, C_BD:C_BD + 128] = (p[:, None] // 16 == p[None, :] // 16).astype(np.float32)
    s = np.arange(32, dtype=np.float32)
    ke = np.zeros((128, 98), np.float32)
    ke[:, 66:98] = np.arange(32, dtype=np.float32)[None, :]
    ke[:64, 0:32] = 31.0 - s
    ke[64:, 0:32] = s
    ke[:64, 32:64] = s + 1.0
    ke[64:, 32:64] = 32.0 - s
    ke[:, 64] = 32.0
    ke[:, 65] = 1.0
    c[:, C_KE:C_KE + 98] = ke
    c[:, C_MK] = ((p // 16) % 2 == 0).astype(np.float32)
    c[:, C_MK + 1] = ((p // 16) % 2 == 1).astype(np.float32)
    c[:, C_MK + 2] = (p >= 96).astype(np.float32)
    c[:, C_MK + 3] = 1e-6
    c[:, C_MK + 4] = 1e-5
    return c


def kernel(x, c, ctx, c_ctx, ada_w, ada_b, norm1_g, w_in, s5_lam_re, s5_lam_im, s5_log_dt,
           s5_b_re, s5_b_im, s5_c_re, s5_c_im, s5_d, s5_w_glu, conv_w, conv_b, conv_ln_g,
           conv_ln_b, w_out, norm2_g, mlp_w1, mlp_w2, final_g):
    global _PROG
    f = lambda a: np.ascontiguousarray(np.asarray(a, dtype=np.float32))
    x, c, ctx, c_ctx = f(x), f(c), f(ctx), f(c_ctx)
    nb = x.shape[0]
    if _PROG is None:
        _PROG = build_program()
    nc = _PROG
    col8 = lambda v: f(np.asarray(v).reshape(8, 128).T)
    col4 = lambda v: f(np.asarray(v).reshape(4, 128).T)
    shared = {
        "ada_w": f(ada_w[0]),
        "ada_b": f(np.asarray(ada_b[0]).reshape(48, 128).T),
        "n1g": col8(norm1_g[0]), "n2g": col8(norm2_g[0]),
        "fgb": f(np.broadcast_to(np.asarray(final_g)[None, :], (128, D))),
        "w_in": f(w_in[0]), "w_out": f(w_out[0]), "w1": f(mlp_w1[0]), "w2": f(mlp_w2[0]),
        "w_glu": f(s5_w_glu[0]),
        "lam_re": f(np.asarray(s5_lam_re[0]).transpose(0, 2, 1).reshape(128, 32)),
        "lam_im": f(np.asarray(s5_lam_im[0]).transpose(0, 2, 1).reshape(128, 32)),
        "log_dt": f(np.repeat(np.asarray(s5_log_dt[0])[:, None, :], 64, axis=1).reshape(128, 32)),
        "b_re": f(np.asarray(s5_b_re[0]).transpose(0, 2, 1, 3).reshape(128, 32, 16)),
        "b_im": f(np.asarray(s5_b_im[0]).transpose(0, 2, 1, 3).reshape(128, 32, 16)),
        "c_re": f(np.asarray(s5_c_re[0]).transpose(0, 3, 1, 2).reshape(128, 32, 16)),
        "c_im": f(np.asarray(s5_c_im[0]).transpose(0, 3, 1, 2).reshape(128, 32, 16)),
        "d_skip": col4(np.asarray(s5_d[0]).reshape(512)),
        "conv_w": f(np.asarray(conv_w[0]).T.reshape(4, 128, 31).transpose(1, 0, 2)),
        "conv_b": col4(conv_b[0]), "ln_g": col4(conv_ln_g[0]), "ln_b": col4(conv_ln_b[0]),
        "cst": _consts(),
    }
    in_maps = []
    for b in range(nb):
        m = dict(shared)
        m["x"] = x[b]
        m["ctx"] = ctx[b]
        m["cvec"] = f(np.stack([c[b].reshape(8, 128).T, c_ctx.reshape(8, 128).T], axis=-1))
        in_maps.append(m)
    res = run_bass_kernel_spmd(nc, in_maps, core_ids=list(range(nb)))
    return np.stack([np.asarray(r["out"], dtype=np.float32) for r in res.results], axis=0)
```

```python
import contextlib
import math
import numpy as np
import concourse.bass as bass
import concourse.mybir as mybir
from concourse.bass_utils import run_bass_kernel_spmd

F32 = mybir.dt.float32
BF16 = mybir.dt.bfloat16
AF = mybir.ActivationFunctionType
ALU = mybir.AluOpType

L = 4096
NCTX = 256
LE = L + NCTX
D = 1024
TCH = 32
NCH = L // TCH
NCC = NCTX // TCH
NCE = NCH + NCC
MAGIC = 12582912.0
TWO_PI = 2.0 * math.pi


class _Op:
    __slots__ = ("eng", "fn", "deps", "sig", "count", "dma", "dsem", "dval")

    def __init__(self, eng, fn):
        self.eng = eng
        self.fn = fn
        self.deps = []
        self.sig = False
        self.count = None
        self.dma = False
        self.dsem = None
        self.dval = None


class Prog:
    ENGS = ("pe", "act", "dve", "pool", "sp")
    NDMA = {"sp": 8, "pool": 4}

    def __init__(self, nc):
        self.nc = nc
        self.q = {e: [] for e in self.ENGS}
        self.lastw = {}
        self.readers = {}
        self.ndma = {e: 0 for e in self.NDMA}
        self.dma_hist = {e: [] for e in self.NDMA}
        self.pending = {e: [] for e in self.ENGS}

    def _add_dep(self, op, d):
        if d is op:
            return
        for x in op.deps:
            if x is d:
                return
        op.deps.append(d)
        if not d.dma:
            d.sig = True

    def _hazards(self, op, reads, writes):
        deps = []
        for k in reads:
            w = self.lastw.get(k)
            if w is not None:
                deps.append(w)
        for k in writes:
            w = self.lastw.get(k)
            if w is not None:
                deps.append(w)
            deps.extend(self.readers.get(k, ()))
        for k in reads:
            self.readers.setdefault(k, []).append(op)
        for k in writes:
            self.lastw[k] = op
            self.readers[k] = []
        for d in deps:
            self._add_dep(op, d)
        pend = self.pending[op.eng]
        if pend:
            for d in pend:
                self._add_dep(op, d)
            self.pending[op.eng] = []

    def op(self, eng, fn, reads=(), writes=()):
        o = _Op(eng, fn)
        self._hazards(o, reads, writes)
        self.q[eng].append(o)
        return o

    def dma(self, eng, out, in_, reads=(), writes=()):
        o = _Op(eng, lambda e: e.dma_start(out=out, in_=in_))
        o.dma = True
        n = self.ndma[eng]
        k = self.NDMA[eng]
        o.dsem = (eng, n % k)
        o.dval = 16 * (n // k + 1)
        self.ndma[eng] = n + 1
        hist = self.dma_hist[eng]
        if n >= k:
            o.deps.append(hist[n - k])
        hist.append(o)
        self._hazards(o, reads, writes)
        self.q[eng].append(o)
        return o

    def barrier(self):
        lasts = []
        for e in self.ENGS:
            for o in reversed(self.q[e]):
                if not o.dma:
                    lasts.append(o)
                    break
        for e, hist in self.dma_hist.items():
            lasts.extend(hist[-self.NDMA[e]:])
        for e in self.ENGS:
            self.pending[e] = list(lasts)
        self.lastw = {}
        self.readers = {}

    def emit(self, final_waits):
        nc = self.nc
        for e in self.ENGS:
            c = 0
            for o in self.q[e]:
                if o.sig and not o.dma:
                    c += 1
                    o.count = c
        with contextlib.ExitStack() as st:
            esem = {e: st.enter_context(nc.semaphore("s_" + e)) for e in ("pe", "act", "dve", "pool")}
            dsem = {}
            for e, k in self.NDMA.items():
                for i in range(k):
                    dsem[(e, i)] = st.enter_context(nc.semaphore("d_%s%d" % (e, i)))
            block = st.enter_context(nc.Block())

            def run(eng_name, engine):
                waited = {}

                def wait_for(d):
                    if d.dma:
                        key, sem, val = d.dsem, dsem[d.dsem], d.dval
                    else:
                        key, sem, val = d.eng, esem[d.eng], d.count
                    if waited.get(key, 0) >= val:
                        return
                    waited[key] = val
                    engine.wait_ge(sem, val)

                for o in self.q[eng_name]:
                    for d in o.deps:
                        wait_for(d)
                    ins = o.fn(engine)
                    if o.dma:
                        ins.then_inc(dsem[o.dsem], 16)
                    elif o.sig:
                        ins.then_inc(esem[eng_name], 1)
                if eng_name == "sp":
                    for d in final_waits:
                        wait_for(d)

            @block.tensor
            def _(eng):
                run("pe", eng)

            @block.scalar
            def _(eng):
                run("act", eng)

            @block.vector
            def _(eng):
                run("dve", eng)

            @block.gpsimd
            def _(eng):
                run("pool", eng)

            @block.sync
            def _(eng):
                run("sp", eng)


NCST = 128 + 128 + 128 + 98 + 5
C_ID, C_ONE, C_BD, C_KE, C_MK = 0, 128, 256, 384, 482


def build_program(stop=None):
    nc = bass.Bass("TRN2", target_bir_lowering=False)

    def din(name, shape, dt=F32):
        return nc.dram_tensor(name, list(shape), dt, kind="ExternalInput").ap()

    x_d = din("x", [L, D])
    ctx_d = din("ctx", [NCTX, D])
    cv_d = din("cvec", [128, 8, 2])
    adaw_d = din("ada_w", [D, 6 * D])
    adab_d = din("ada_b", [128, 48])
    n1g_d = din("n1g", [128, 8])
    n2g_d = din("n2g", [128, 8])
    fg_d = din("fgb", [128, D])
    win_d = din("w_in", [D, 1536])
    wout_d = din("w_out", [D, D])
    w1_d = din("w1", [D, 4 * D])
    w2_d = din("w2", [4 * D, D])
    wglu_d = din("w_glu", [512, 512])
    lre_d = din("lam_re", [128, 32])
    lim_d = din("lam_im", [128, 32])
    ldt_d = din("log_dt", [128, 32])
    bre_d = din("b_re", [128, 32, 16])
    bim_d = din("b_im", [128, 32, 16])
    cre_d = din("c_re", [128, 32, 16])
    cim_d = din("c_im", [128, 32, 16])
    dsk_d = din("d_skip", [128, 4])
    cw_d = din("conv_w", [128, 4, 31])
    cb_d = din("conv_b", [128, 4])
    lng_d = din("ln_g", [128, 4])
    lnb_d = din("ln_b", [128, 4])
    cst_d = din("cst", [128, NCST])
    out_d = nc.dram_tensor("out", [L, D], F32, kind="ExternalOutput").ap()
    mix_d = nc.dram_tensor("mixd", [D, L], BF16, kind=("ExternalOutput" if stop else "Internal")).ap()

    P = Prog(nc)

    def sb(st, name, shape, dt):
        return st.enter_context(nc.sbuf_tensor("sb_" + name, list(shape), dt))

    def pst(st, name, shape, dt):
        full = 512 if dt == F32 else 1024
        nfree = 1
        for d_ in shape[1:]:
            nfree *= d_
        assert nfree <= full
        t = st.enter_context(nc.psum_tensor("ps_" + name, [128, full], dt))
        ap = t[:, 0:nfree]
        if len(shape) == 3:
            ap = ap.rearrange("p (a b) -> p a b", b=shape[2])
        return ap

    dbg_fins = []

    def finish(dumps):
        P.barrier()
        fins = []
        for name, ap, shape, dt in dumps:
            d_ = nc.dram_tensor("dbg_" + name, list(shape), dt, kind="ExternalOutput").ap()
            fins.append(P.dma("sp", d_, ap))
        P.emit(fins)
        return nc

    with contextlib.ExitStack() as glob:
        cst = sb(glob, "cst", [128, NCST], F32)
        identb = sb(glob, "identb", [128, 128], BF16)
        onesb = sb(glob, "onesb", [128, 128], BF16)
        modT = sb(glob, "modT", [128, 48, 2], F32)
        s1 = sb(glob, "s1", [128, 8], F32)
        s1c = sb(glob, "s1c", [128, 8], F32)
        s2 = sb(glob, "s2", [128, 8], F32)
        n1g = sb(glob, "n1g", [128, 8], F32)
        n2g = sb(glob, "n2g", [128, 8], F32)
        dsk = sb(glob, "dsk", [128, 4], F32)
        cw = sb(glob, "cw", [128, 4, 31], F32)
        cb = sb(glob, "cb", [128, 4], F32)
        lng = sb(glob, "lng", [128, 4], F32)
        lnb = sb(glob, "lnb", [128, 4], F32)

        ident = cst[:, C_ID:C_ID + 128]
        onesf = cst[:, C_ONE:C_ONE + 128]
        bdmask = cst[:, C_BD:C_BD + 128]
        kexp = cst[:, C_KE:C_KE + 98]
        mk_even = cst[:, C_MK:C_MK + 1]
        mk_odd = cst[:, C_MK + 1:C_MK + 2]
        mk_hi = cst[:, C_MK + 2:C_MK + 3]

        P.dma("sp", cst[:], cst_d[:, :], writes=["cst"])
        P.dma("sp", n1g[:], n1g_d[:, :], writes=["n1g"])
        P.dma("sp", n2g[:], n2g_d[:, :], writes=["n2g"])
        P.dma("sp", dsk[:], dsk_d[:, :], writes=["dsk"])
        P.dma("sp", cw[:], cw_d[:, :, :], writes=["cw"])
        P.dma("sp", cb[:], cb_d[:, :], writes=["cb"])
        P.dma("sp", lng[:], lng_d[:, :], writes=["lng"])
        P.dma("sp", lnb[:], lnb_d[:, :], writes=["lnb"])
        P.op("dve", lambda e: e.tensor_copy(out=identb[:], in_=ident), reads=["cst"], writes=["identb"])
        P.op("dve", lambda e: e.tensor_copy(out=onesb[:], in_=onesf), reads=["cst"], writes=["onesb"])

        with contextlib.ExitStack() as stB:
            uT = sb(stB, "uT", [128, 4, LE], BF16)
            Sall = sb(stB, "Sall", [128, 32, 2, NCE], BF16)
            Ebf = Sall
            with contextlib.ExitStack() as stB2:
                pw_r = sb(stB2, "pw_r", [128, 32, 98], F32)
                pw_i = sb(stB2, "pw_i", [128, 32, 98], F32)
                bbr = sb(stB2, "bbr", [128, 32, 16], F32)
                bbi = sb(stB2, "bbi", [128, 32, 16], F32)
                Cr = sb(stB2, "Cr", [128, 32, 16], F32)
                Ci = sb(stB2, "Ci", [128, 32, 16], F32)
                nCr = sb(stB2, "nCr", [128, 32, 16], F32)
                nCi = sb(stB2, "nCi", [128, 32, 16], F32)
                Crb = sb(stB2, "Crb", [128, 32, 16], BF16)
                nCib = sb(stB2, "nCib", [128, 32, 16], BF16)
                CA = sb(stB2, "CA", [128, 2, 32], F32)
                CBn = sb(stB2, "CBn", [128, 32], F32)
                CBp = sb(stB2, "CBp", [128, 32], F32)
                with contextlib.ExitStack() as stH:
                    hcT = sb(stH, "hcT", [128, 4, L], BF16)
                    stP = contextlib.ExitStack()
                    cvs = sb(stP, "cvs", [128, 8, 2], F32)
                    csl = sb(stP, "csl", [128, 8, 2], F32)
                    adb = sb(stP, "adb", [128, 48], F32)
                    awb = [sb(stP, "awb%d" % i, [128, 8, 256], F32) for i in range(2)]
                    rowb = [sb(stP, "rowb%d" % i, [128, 256], F32) for i in range(2)]
                    ps_mod = pst(stP, "ps_mod", [128, 48, 2], F32)
                    psR = [pst(stP, "psR%d" % i, [128, 512], F32) for i in range(1)]
                    with contextlib.ExitStack() as st:
                        lre = sb(st, "lre", [128, 32], F32)
                        lim = sb(st, "lim", [128, 32], F32)
                        ldt = sb(st, "ldt", [128, 32], F32)
                        bre = sb(st, "bre", [128, 32, 16], F32)
                        bim = sb(st, "bim", [128, 32, 16], F32)
                        dtt = sb(st, "dtt", [128, 32], F32)
                        lr = sb(st, "lr", [128, 32], F32)
                        li = sb(st, "li", [128, 32], F32)
                        arg = sb(st, "arg", [128, 32, 98], F32)
                        mg = sb(st, "mg", [128, 32, 98], F32)
                        tn = sb(st, "tn", [128, 32, 98], F32)
                        tr_ = sb(st, "tr_", [128, 32, 98], F32)
                        fa = sb(st, "fa", [128, 32], F32)
                        fb = sb(st, "fb", [128, 32], F32)
                        fc = sb(st, "fc", [128, 32], F32)
                        den = sb(st, "den", [128, 32], F32)
                        fre = sb(st, "fre", [128, 32], F32)
                        fim = sb(st, "fim", [128, 32], F32)
                        tb1 = sb(st, "tb1", [128, 32, 16], F32)
                        tb2 = sb(st, "tb2", [128, 32, 16], F32)

                        P.dma("sp", cvs[:], cv_d[:, :, :], writes=["cvs"])
                        P.dma("sp", adb[:], adab_d[:, :], writes=["adb"])
                        P.dma("sp", lre[:], lre_d[:, :], writes=["lre"])
                        P.dma("sp", lim[:], lim_d[:, :], writes=["lim"])
                        P.dma("sp", ldt[:], ldt_d[:, :], writes=["ldt"])
                        P.dma("sp", bre[:], bre_d[:, :, :], writes=["bre"])
                        P.dma("sp", bim[:], bim_d[:, :, :], writes=["bim"])
                        P.dma("sp", Cr[:], cre_d[:, :, :], writes=["Cr"])
                        P.dma("sp", Ci[:], cim_d[:, :, :], writes=["Ci"])
                        P.op("act", lambda e: e.activation(out=csl[:], in_=cvs[:], func=AF.Silu), reads=["cvs"], writes=["csl"])

                        def emit_modT_tr(blk):
                            rb, rbk = rowb[blk % 2], "rowb%d" % (blk % 2)

                            def trm(e):
                                ins = None
                                for ctl in range(2):
                                    ins = e.transpose(out=ps_mod[:, blk * 2 + ctl, :], in_=rb[0:2, ctl * 128:(ctl + 1) * 128], identity=ident[0:2, 0:2])
                                return ins
                            P.op("pe", trm, reads=[rbk, "cst"], writes=["ps_mod"])
                        awv = adaw_d.rearrange("(k p) n -> p k n", p=128)
                        def ada_dma(blk):
                            buf = awb[blk % 2]
                            key = "awb%d" % (blk % 2)
                            P.dma("sp", buf[:], awv[:, :, blk * 256:(blk + 1) * 256], writes=[key])

                        def ada_mm(blk):
                            buf = awb[blk % 2]
                            key = "awb%d" % (blk % 2)
                            pr, prk = psR[0], "psR0"
                            rb, rbk = rowb[blk % 2], "rowb%d" % (blk % 2)

                            def mm(e):
                                ins = None
                                for k in range(8):
                                    ins = e.matmul(pr[0:2, 0:256], lhsT=csl[:, k, :], rhs=buf[:, k, :], start=(k == 0), stop=(k == 7))
                                return ins
                            P.op("pe", mm, reads=[key, "csl"], writes=[prk])
                            P.op("act", lambda e: e.activation(out=rb[0:2, :], in_=pr[0:2, 0:256], func=AF.Copy), reads=[prk], writes=[rbk])
                        ada_dma(0)
                        for blk in range(8):
                            if blk + 1 < 8:
                                ada_dma(blk + 1)
                            ada_mm(blk)
                            if blk >= 1:
                                emit_modT_tr(blk - 1)
                        emit_modT_tr(7)


                        V = lambda fn, r, w: P.op("dve", fn, reads=r, writes=w)
                        A_ = lambda fn, r, w: P.op("act", fn, reads=r, writes=w)
                        A_(lambda e: e.activation(out=dtt[:], in_=ldt[:], func=AF.Exp), ["ldt"], ["dtt"])
                        V(lambda e: e.tensor_tensor(out=lr[:], in0=lre[:], in1=dtt[:], op=ALU.mult), ["lre", "dtt"], ["lr"])
                        V(lambda e: e.tensor_tensor(out=li[:], in0=lim[:], in1=dtt[:], op=ALU.mult), ["lim", "dtt"], ["li"])
                        kb3 = kexp.unsqueeze(1).to_broadcast([128, 32, 98])
                        V(lambda e: e.tensor_tensor(out=mg[:], in0=kb3, in1=lr[:].unsqueeze(2).to_broadcast([128, 32, 98]), op=ALU.mult), ["cst", "lr"], ["mg"])
                        A_(lambda e: e.activation(out=mg[:], in_=mg[:], func=AF.Exp), ["mg"], ["mg"])
                        V(lambda e: e.tensor_tensor(out=arg[:], in0=kb3, in1=li[:].unsqueeze(2).to_broadcast([128, 32, 98]), op=ALU.mult), ["cst", "li"], ["arg"])

                        def trig(dst, shift, key):
                            if shift != 0.0:
                                V(lambda e: e.tensor_scalar(out=tr_[:], in0=arg[:], scalar1=shift, scalar2=None, op0=ALU.add), ["arg"], ["tr_"])
                                srcv = tr_
                            else:
                                V(lambda e: e.tensor_copy(out=tr_[:], in_=arg[:]), ["arg"], ["tr_"])
                                srcv = tr_
                            V(lambda e: e.tensor_scalar(out=tn[:], in0=srcv[:], scalar1=1.0 / TWO_PI, scalar2=MAGIC, op0=ALU.mult, op1=ALU.add), ["tr_"], ["tn"])
                            V(lambda e: e.tensor_scalar(out=tn[:], in0=tn[:], scalar1=-MAGIC, scalar2=None, op0=ALU.add), ["tn"], ["tn"])
                            V(lambda e: e.scalar_tensor_tensor(out=tr_[:], in0=tn[:], scalar=-TWO_PI, in1=srcv[:], op0=ALU.mult, op1=ALU.add), ["tn", "tr_"], ["tr_"])
                            V(lambda e: e.tensor_scalar(out=tr_[:], in0=tr_[:], scalar1=-3.1415925, scalar2=3.1415925, op0=ALU.max, op1=ALU.min), ["tr_"], ["tr_"])
                            A_(lambda e: e.activation(out=tr_[:], in_=tr_[:], func=AF.Sin), ["tr_"], ["tr_"])
                            V(lambda e: e.tensor_tensor(out=dst[:], in0=tr_[:], in1=mg[:], op=ALU.mult), ["tr_", "mg"], [key])
                        trig(pw_i, 0.0, "pw_i")
                        trig(pw_r, math.pi / 2.0, "pw_r")
                        V(lambda e: e.tensor_scalar(out=fa[:], in0=pw_r[:, :, 65], scalar1=-1.0, scalar2=None, op0=ALU.add), ["pw_r"], ["fa"])
                        V(lambda e: e.tensor_tensor(out=den[:], in0=lre[:], in1=lre[:], op=ALU.mult), ["lre"], ["den"])
                        V(lambda e: e.tensor_tensor(out=fb[:], in0=lim[:], in1=lim[:], op=ALU.mult), ["lim"], ["fb"])
                        V(lambda e: e.tensor_tensor(out=den[:], in0=den[:], in1=fb[:], op=ALU.add), ["den", "fb"], ["den"])
                        V(lambda e: e.reciprocal(out=den[:], in_=den[:]), ["den"], ["den"])
                        V(lambda e: e.tensor_tensor(out=fb[:], in0=fa[:], in1=lre[:], op=ALU.mult), ["fa", "lre"], ["fb"])
                        V(lambda e: e.tensor_tensor(out=fc[:], in0=pw_i[:, :, 65], in1=lim[:], op=ALU.mult), ["pw_i", "lim"], ["fc"])
                        V(lambda e: e.tensor_tensor(out=fb[:], in0=fb[:], in1=fc[:], op=ALU.add), ["fb", "fc"], ["fb"])
                        V(lambda e: e.tensor_tensor(out=fre[:], in0=fb[:], in1=den[:], op=ALU.mult), ["fb", "den"], ["fre"])
                        V(lambda e: e.tensor_tensor(out=fb[:], in0=pw_i[:, :, 65], in1=lre[:], op=ALU.mult), ["pw_i", "lre", "fre"], ["fb"])
                        V(lambda e: e.tensor_tensor(out=fc[:], in0=fa[:], in1=lim[:], op=ALU.mult), ["fa", "lim"], ["fc"])
                        V(lambda e: e.tensor_tensor(out=fb[:], in0=fb[:], in1=fc[:], op=ALU.subtract), ["fb", "fc"], ["fb"])
                        V(lambda e: e.tensor_tensor(out=fim[:], in0=fb[:], in1=den[:], op=ALU.mult), ["fb", "den"], ["fim"])
                        frb = fre[:].unsqueeze(2).to_broadcast([128, 32, 16])
                        fib = fim[:].unsqueeze(2).to_broadcast([128, 32, 16])
                        V(lambda e: e.tensor_tensor(out=tb1[:], in0=bre[:], in1=frb, op=ALU.mult), ["bre", "fre"], ["tb1"])
                        V(lambda e: e.tensor_tensor(out=tb2[:], in0=bim[:], in1=fib, op=ALU.mult), ["bim", "fim"], ["tb2"])
                        V(lambda e: e.tensor_tensor(out=bbr[:], in0=tb1[:], in1=tb2[:], op=ALU.subtract), ["tb1", "tb2"], ["bbr"])
                        V(lambda e: e.tensor_tensor(out=tb1[:], in0=bim[:], in1=frb, op=ALU.mult), ["bim", "fre", "bbr"], ["tb1"])
                        V(lambda e: e.tensor_tensor(out=tb2[:], in0=bre[:], in1=fib, op=ALU.mult), ["bre", "fim", "bbr"], ["tb2"])
                        V(lambda e: e.tensor_tensor(out=bbi[:], in0=tb1[:], in1=tb2[:], op=ALU.add), ["tb1", "tb2"], ["bbi"])
                        V(lambda e: e.tensor_scalar(out=nCr[:], in0=Cr[:], scalar1=-1.0, scalar2=None, op0=ALU.mult), ["Cr"], ["nCr"])
                        V(lambda e: e.tensor_scalar(out=nCi[:], in0=Ci[:], scalar1=-1.0, scalar2=None, op0=ALU.mult), ["Ci"], ["nCi"])
                        V(lambda e: e.tensor_copy(out=Crb[:], in_=Cr[:]), ["Cr"], ["Crb"])
                        V(lambda e: e.tensor_copy(out=nCib[:], in_=nCi[:]), ["nCi"], ["nCib"])
                        V(lambda e: e.tensor_copy(out=CA[:, 0, :], in_=pw_r[:, :, 64]), ["pw_r"], ["CA"])
                        V(lambda e: e.tensor_copy(out=CA[:, 1, :], in_=pw_r[:, :, 64]), ["pw_r", "CA"], ["CA"])
                        V(lambda e: e.tensor_copy(out=CBp[:], in_=pw_i[:, :, 64]), ["pw_i"], ["CBp"])
                        V(lambda e: e.tensor_scalar(out=CBn[:], in0=pw_i[:, :, 64], scalar1=-1.0, scalar2=None, op0=ALU.mult), ["pw_i"], ["CBn"])

                        P.op("dve", lambda e: e.tensor_tensor(out=modT[:, 0:16, :], in0=ps_mod[:, 0:16, :], in1=adb[:, 0:16].unsqueeze(2).to_broadcast([128, 16, 2]), op=ALU.add),
                             reads=["ps_mod", "adb"], writes=["modT"])
                        P.op("dve", lambda e: e.scalar_tensor_tensor(out=s1[:], in0=modT[:, 8:16, 0], scalar=1.0, in1=n1g[:], op0=ALU.add, op1=ALU.mult),
                             reads=["modT", "n1g"], writes=["s1"])
                        P.op("dve", lambda e: e.scalar_tensor_tensor(out=s1c[:], in0=modT[:, 8:16, 1], scalar=1.0, in1=n1g[:], op0=ALU.add, op1=ALU.mult),
                             reads=["modT", "n1g"], writes=["s1c"])
                    P.barrier()
                    if stop == "B1":
                        return finish([("pw_r", pw_r[:], [128, 32, 98], F32), ("pw_i", pw_i[:], [128, 32, 98], F32), ("bbr", bbr[:], [128, 32, 16], F32), ("bbi", bbi[:], [128, 32, 16], F32)])
                    with contextlib.ExitStack() as st:
                        winb = sb(st, "winb", [128, 8, 1536], BF16)
                        xt = [sb(st, "xt%d" % i, [128, D], F32) for i in range(3)]
                        junk = sb(st, "junk", [128, D], BF16)
                        ssq4 = [sb(st, "ssqa%d" % i, [128, 4], F32) for i in range(4)]
                        xs = [sb(st, "xs%d" % i, [128, D], BF16) for i in range(2)]
                        aT = [sb(st, "aT%d" % i, [128, 8, 512], BF16) for i in range(2)]
                        ssq = [sb(st, "ssq%d" % i, [128, 4], F32) for i in range(3)]
                        sig = [sb(st, "sig%d" % i, [128, 512], F32) for i in range(2)]
                        psT = [pst(st, "psTA%d" % i, [128, 8, 128], BF16) for i in range(2)]
                        psz = [pst(st, "psz%d" % i, [128, 512], F32) for i in range(4)]

                        for k in range(8):
                            P.dma("pool", winb[:, k, :], win_d[k * 128:(k + 1) * 128, :], writes=["winb%d" % k])
                        winkeys = ["winb%d" % k for k in range(8)]

                        tiles = []
                        for kb in range(L // 512):
                            for tt in range(4):
                                tiles.append((kb, tt, False))
                        for tt in range(2):
                            tiles.append((L // 512, tt, True))
                        NT = len(tiles)
                        xt4 = xt
                        zc = [0]

                        def a_s1a(t):
                            kb, tt, is_ctx = tiles[t]
                            xi = t % 3
                            xb, xk = xt4[xi], "xt%d" % xi
                            sq, sk = ssq4[t % 4], "ssq%d" % (t % 4)
                            src = ctx_d[tt * 128:(tt + 1) * 128, :] if is_ctx else x_d[kb * 512 + tt * 128: kb * 512 + (tt + 1) * 128, :]
                            P.dma("sp", xb[:], src, writes=[xk])
                            P.op("act", lambda e: e.activation(out=junk[:], in_=xb[:], func=AF.Square, accum_out=sq[:, 0:1]), reads=[xk], writes=["junk", sk])
                            P.op("act", lambda e: e.activation(out=sq[:, 2:3], in_=sq[:, 0:1], func=AF.Sqrt, bias=cst[:, C_MK + 3:C_MK + 4], scale=1.0 / D), reads=[sk], writes=[sk])

                        def a_s1b(t):
                            xi = t % 3
                            xb, xk = xt4[xi], "xt%d" % xi
                            sq, sk = ssq4[t % 4], "ssq%d" % (t % 4)
                            xsb, xsk = xs[t % 2], "xs%d" % (t % 2)
                            P.op("dve", lambda e: e.reciprocal(out=sq[:, 3:4], in_=sq[:, 2:3]), reads=[sk], writes=[sk])
                            P.op("dve", lambda e: e.tensor_scalar(out=xsb[:], in0=xb[:], scalar1=sq[:, 3:4], scalar2=None, op0=ALU.mult), reads=[xk, sk], writes=[xsk])

                        def a_s2(t):
                            kb, tt, is_ctx = tiles[t]
                            xsb, xsk = xs[t % 2], "xs%d" % (t % 2)
                            pt, ptk = psT[t % 2], "psTA%d" % (t % 2)
                            a, ak = aT[kb % 2], "aT%d" % (kb % 2)
                            sc_t = s1c if is_ctx else s1
                            shcol = 1 if is_ctx else 0

                            def tr(e):
                                ins = None
                                for j in range(8):
                                    ins = e.transpose(out=pt[:, j, :], in_=xsb[:, j * 128:(j + 1) * 128], identity=identb[:])
                                return ins
                            P.op("pe", tr, reads=[xsk], writes=[ptk])

                            def ev(e):
                                ins = None
                                for j in range(8):
                                    ins = e.activation(out=a[:, j, tt * 128:(tt + 1) * 128], in_=pt[:, j, :], func=AF.Identity,
                                                       bias=modT[:, j, shcol:shcol + 1], scale=sc_t[:, j:j + 1])
                                return ins
                            P.op("act", ev, reads=[ptk], writes=[ak + "_%d" % tt])

                        def a_block(kb):
                            is_ctx = kb == L // 512
                            ntile = 2 if is_ctx else 4
                            ntok = ntile * 128
                            a, ak = aT[kb % 2], "aT%d" % (kb % 2)
                            akeys = [ak + "_%d" % tt for tt in range(ntile)]
                            tok0 = L if is_ctx else kb * 512

                            def mm_cols(pz, c0):
                                def f_(e):
                                    ins = None
                                    for k in range(8):
                                        ins = e.matmul(pz[:, 0:ntok], lhsT=winb[:, k, c0:c0 + 128], rhs=a[:, k, 0:ntok], start=(k == 0), stop=(k == 7))
                                    return ins
                                return f_
                            for ct in range(4):
                                pz, pzk = psz[zc[0] % 4], "psz%d" % (zc[0] % 4)
                                zc[0] += 1
                                P.op("pe", mm_cols(pz, ct * 128), reads=akeys + winkeys, writes=[pzk])
                                P.op("act", lambda e, pz=pz, ct=ct: e.activation(out=uT[:, ct, tok0:tok0 + ntok], in_=pz[:, 0:ntok], func=AF.Copy),
                                     reads=[pzk], writes=["uT%d_%d" % (ct, kb)])
                            if is_ctx:
                                return
                            for ct in range(4):
                                pv, pvk = psz[zc[0] % 4], "psz%d" % (zc[0] % 4)
                                zc[0] += 1
                                pg, pgk = psz[zc[0] % 4], "psz%d" % (zc[0] % 4)
                                zc[0] += 1
                                P.op("pe", mm_cols(pg, 1024 + ct * 128), reads=akeys + winkeys, writes=[pgk])
                                P.op("pe", mm_cols(pv, 512 + ct * 128), reads=akeys + winkeys, writes=[pvk])
                                sg, sgk = sig[ct % 2], "sig%d" % (ct % 2)
                                P.op("act", lambda e, sg=sg, pg=pg: e.activation(out=sg[:], in_=pg[:], func=AF.Sigmoid), reads=[pgk], writes=[sgk])
                                P.op("dve", lambda e, sg=sg, pv=pv, ct=ct: e.tensor_tensor(out=hcT[:, ct, kb * 512:(kb + 1) * 512], in0=pv[:], in1=sg[:], op=ALU.mult),
                                     reads=[pvk, sgk], writes=["hcT%d_%d" % (ct, kb)])

                        ada_dma(8)
                        a_s1a(0)
                        a_s1a(1)
                        a_s1b(0)
                        for t in range(NT):
                            if t + 2 < NT:
                                a_s1a(t + 2)
                            if t + 1 < NT:
                                a_s1b(t + 1)
                            a_s2(t)
                            if t < 16:
                                if t + 1 < 16:
                                    ada_dma(8 + t + 1)
                                ada_mm(8 + t)
                                if t >= 1:
                                    emit_modT_tr(8 + t - 1)
                            if t == 16:
                                emit_modT_tr(23)
                                P.op("dve", lambda e: e.tensor_tensor(out=modT[:, 16:48, :], in0=ps_mod[:, 16:48, :], in1=adb[:, 16:48].unsqueeze(2).to_broadcast([128, 32, 2]), op=ALU.add),
                                     reads=["ps_mod", "adb"], writes=["modT2"])
                                P.op("dve", lambda e: e.scalar_tensor_tensor(out=s2[:], in0=modT[:, 32:40, 0], scalar=1.0, in1=n2g[:], op0=ALU.add, op1=ALU.mult),
                                     reads=["modT2", "n2g"], writes=["s2"])
                            kb, tt, is_ctx = tiles[t]
                            if tt == (1 if is_ctx else 3):
                                a_block(kb)
                    P.barrier()
                    stP.close()
                    if stop == "A":
                        return finish([("uT", uT[:], [128, 4, LE], BF16), ("hcT", hcT[:], [128, 4, L], BF16), ("modT", modT[:], [128, 48, 2], F32), ("G1b", G1b[:], [128, D], F32)])
                    with contextlib.ExitStack() as st:
                        ABr = sb(st, "ABr", [128, 8, 32, 16], BF16)
                        ABi = sb(st, "ABi", [128, 8, 32, 16], BF16)
                        CAr = sb(st, "CAr", [128, 8, 32, 16], BF16)
                        CAi = sb(st, "CAi", [128, 8, 32, 16], BF16)
                        tA = [sb(st, "tA%d" % i, [128, 8, 2, 16], F32) for i in range(2)]
                        tB = [sb(st, "tB%d" % i, [128, 8, 2, 16], F32) for i in range(2)]
                        Wp = sb(st, "Wp", [128, 8, 4, 2, 128], BF16)
                        W2 = sb(st, "W2", [128, 2, 32, 128], BF16)
                        Zc = sb(st, "Zc", [128, 1, 8, 8, 16], BF16)
                        Up = sb(st, "Up", [128, 8, LE // 8], BF16)
                        Bbrb = sb(st, "Bbrb", [128, 32, 16], BF16)
                        nBbib = sb(st, "nBbib", [128, 32, 16], BF16)
                        psT1 = pst(st, "psT1", [128, 8, 128], BF16)
                        psS = [pst(st, "psS%d" % i, [128, 2, NCE], F32) for i in range(3)]
                        psKs = [pst(st, "psK%d" % i, [128, 4, 128], F32) for i in range(2)]
                        psY = [pst(st, "psYt%d" % i, [128, 16, 32], F32) for i in range(2)]
                        P.op("dve", lambda e: e.tensor_copy(out=Bbrb[:], in_=bbr[:]), reads=["bb"], writes=["Bbrb"])
                        P.op("dve", lambda e: e.tensor_scalar(out=nBbib[:], in0=bbi[:], scalar1=-1.0, scalar2=None, op0=ALU.mult), reads=["bb"], writes=["nBbib"])
                        yc = 0
                        sc_ = [0]
                        ycc = [0]

                        def prep_up(j, sp_list):
                            g0 = 8 * j
                            ukeys = ["uT%d_%d" % (j, kb) for kb in range(9)]
                            upkeys = ["Up%d" % sp for sp in range(5)]
                            abkeys = ["ABr%d" % q for q in range(16)] + ["ABi%d" % q for q in range(16)]
                            cakeys = ["CAr%d" % q for q in range(16)] + ["CAi%d" % q for q in range(16)]
                            for sp in sp_list:
                                ncol = 128 if sp < 4 else 32

                                def tr1(e, sp=sp, ncol=ncol, j=j):
                                    ins = None
                                    for s8 in range(8):
                                        ins = e.transpose(out=psT1[0:ncol, s8, :], in_=uT[:, j, 1024 * sp + s8:1024 * sp + 8 * ncol:8], identity=identb[:])
                                    return ins
                                P.op("pe", tr1, reads=ukeys, writes=["psT1"])
                                P.op("act", lambda e, sp=sp, ncol=ncol: e.activation(out=Zc[0:ncol, 0, :, :, :].rearrange("p g s h -> p s g h"),
                                                                                   in_=psT1[0:ncol].rearrange("p s (g h) -> p s g h", h=16), func=AF.Copy),
                                     reads=["psT1"], writes=["Zc0"])

                                def tr2(e, sp=sp, ncol=ncol):
                                    ins = None
                                    for gl in range(8):
                                        ins = e.transpose(out=psT1[:, gl, 0:ncol], in_=Zc[0:ncol, 0, gl, :, :].rearrange("p s h -> p (s h)"), identity=identb[0:ncol, 0:ncol])
                                    return ins
                                P.op("pe", tr2, reads=["Zc0"], writes=["psT1"])
                                P.op("dve", lambda e, sp=sp, ncol=ncol: e.tensor_copy(out=Up[:, :, 128 * sp:128 * sp + ncol], in_=psT1[:, :, 0:ncol]),
                                     reads=["psT1"], writes=["Up%d" % sp])

                        def prep_ab(j, sh_list, which="all"):
                            g0 = 8 * j
                            ukeys = ["uT%d_%d" % (j, kb) for kb in range(9)]
                            upkeys = ["Up%d" % sp for sp in range(5)]
                            abkeys = ["ABr%d" % q for q in range(16)] + ["ABi%d" % q for q in range(16)]
                            cakeys = ["CAr%d" % q for q in range(16)] + ["CAi%d" % q for q in range(16)]
                            for sh in sh_list:
                                s0 = sh * 2
                                bsh = [128, 8, 2, 16]
                                p1r = pw_r[:, g0:g0 + 8, s0:s0 + 2].unsqueeze(3).to_broadcast(bsh)
                                p1i = pw_i[:, g0:g0 + 8, s0:s0 + 2].unsqueeze(3).to_broadcast(bsh)
                                ptr = pw_r[:, g0:g0 + 8, 66 + s0:66 + s0 + 2].unsqueeze(3).to_broadcast(bsh)
                                pti = pw_i[:, g0:g0 + 8, 66 + s0:66 + s0 + 2].unsqueeze(3).to_broadcast(bsh)
                                br_ = bbr[:, g0:g0 + 8, :].unsqueeze(2).to_broadcast(bsh)
                                bi_ = bbi[:, g0:g0 + 8, :].unsqueeze(2).to_broadcast(bsh)
                                cr_ = Cr[:, g0:g0 + 8, :].unsqueeze(2).to_broadcast(bsh)
                                ci_ = Ci[:, g0:g0 + 8, :].unsqueeze(2).to_broadcast(bsh)

                                def cplx(eng, ta, tb, kA, kB, a1, b1, a2, b2, op, dst, dkey):
                                    P.op(eng, lambda e: e.tensor_tensor(out=ta[:], in0=a1, in1=b1, op=ALU.mult), reads=["pw", "bb"], writes=[kA])
                                    P.op(eng, lambda e: e.tensor_tensor(out=tb[:], in0=a2, in1=b2, op=ALU.mult), reads=["pw", "bb"], writes=[kB])
                                    P.op(eng, lambda e: e.tensor_tensor(out=dst, in0=ta[:], in1=tb[:], op=op), reads=[kA, kB], writes=[dkey])
                                if which in ("all", "ab"):
                                    cplx("dve", tA[0], tA[1], "tA0", "tA1", p1r, br_, p1i, bi_, ALU.subtract, ABr[:, :, s0:s0 + 2, :], "ABr%d" % sh)
                                    cplx("pool", tB[0], tB[1], "tB0", "tB1", p1r, bi_, p1i, br_, ALU.add, ABi[:, :, s0:s0 + 2, :], "ABi%d" % sh)
                                if which in ("all", "ca"):
                                    cplx("dve", tA[0], tA[1], "tA0", "tA1", ptr, cr_, pti, ci_, ALU.subtract, CAr[:, :, s0:s0 + 2, :], "CAr%d" % sh)
                                    cplx("pool", tB[0], tB[1], "tB0", "tB1", pti, cr_, ptr, ci_, ALU.add, CAi[:, :, s0:s0 + 2, :], "CAi%d" % sh)

                        def main_pre(j):
                            g0 = 8 * j
                            ukeys = ["uT%d_%d" % (j, kb) for kb in range(9)]
                            upkeys = ["Up%d" % sp for sp in range(5)]
                            abkeys = ["ABr%d" % q for q in range(16)] + ["ABi%d" % q for q in range(16)]
                            cakeys = ["CAr%d" % q for q in range(16)] + ["CAi%d" % q for q in range(16)]
                            for gl in range(8):
                                def trw(e, gl=gl):
                                    ins = None
                                    for q in range(4):
                                        for ri, AB in enumerate((ABr, ABi)):
                                            ins = e.transpose(out=psT1[:, q * 2 + ri, :], in_=AB[:, gl, 8 * q:8 * q + 8, :].rearrange("p s h -> p (s h)"), identity=identb[:])
                                    return ins
                                P.op("pe", trw, reads=abkeys, writes=["psT1"])

                                def evw(e, gl=gl):
                                    o_ = Wp[:, gl, :, :, :].rearrange("p q r m -> p (q r m)")
                                    i_ = psT1[:].rearrange("p a m -> p (a m)")
                                    e.activation(out=o_[:, 0:512], in_=i_[:, 0:512], func=AF.Copy)
                                    return e.activation(out=o_[:, 512:1024], in_=i_[:, 512:1024], func=AF.Copy)
                                P.op("act", evw, reads=["psT1"], writes=["Wp%d" % gl])
                            for gl in range(8):
                                for ri in range(2):
                                    ps_, psk = psS[sc_[0] % 3], "psS%d" % (sc_[0] % 3)
                                    sc_[0] += 1

                                    def mms(e, gl=gl, ri=ri, ps_=ps_):
                                        ins = None
                                        for q in range(4):
                                            ins = e.matmul(ps_[:, 0, :], lhsT=Wp[:, gl, q, ri, :], rhs=Up[:, gl, q:LE // 8:4], start=(q == 0), stop=(q == 3))
                                        return ins
                                    P.op("pe", mms, reads=["Wp%d" % gl] + upkeys, writes=[psk])
                                    P.op("act", lambda e, gl=gl, ri=ri, ps_=ps_, g0=g0: e.activation(out=Sall[:, g0 + gl, ri, :], in_=ps_[:, 0, :], func=AF.Copy),
                                         reads=[psk], writes=["Sall"])
                            for dr in range(2):
                                rows = slice(64 * dr, 64 * dr + 64)
                                for t4 in range(8):
                                    psK, pkk = psKs[(dr * 8 + t4) % 2], "psK%d" % ((dr * 8 + t4) % 2)

                                    def mmk(e, rows=rows, t4=t4, g0=g0, psK=psK):
                                        ins = None
                                        for q in range(4):
                                            tau = t4 * 4 + q
                                            o_ = psK[:, q, :].rearrange("p (g h) -> p g h", h=16)
                                            e.matmul(o_, lhsT=Bbrb[rows, g0:g0 + 8, :].rearrange("p g h -> p (g h)"), rhs=CAr[rows, :, tau, :], start=True, stop=False)
                                            ins = e.matmul(o_, lhsT=nBbib[rows, g0:g0 + 8, :].rearrange("p g h -> p (g h)"), rhs=CAi[rows, :, tau, :], start=False, stop=True)
                                        return ins
                                    P.op("pe", mmk, reads=cakeys + ["Bbrb", "nBbib"], writes=[pkk])
                                    P.op("dve", lambda e, dr=dr, t4=t4, psK=psK: e.tensor_tensor(out=W2[:, dr, t4 * 4:(t4 + 1) * 4, :], in0=psK[:], in1=bdmask.unsqueeze(1).to_broadcast([128, 4, 128]), op=ALU.mult),
                                         reads=[pkk, "cst"], writes=["W2_%d_%d" % (dr, t4)])

                        def taps(j, kb):
                            g0 = 8 * j
                            ukeys = ["uT%d_%d" % (j, kb) for kb in range(9)]
                            upkeys = ["Up%d" % sp for sp in range(5)]
                            abkeys = ["ABr%d" % q for q in range(16)] + ["ABi%d" % q for q in range(16)]
                            cakeys = ["CAr%d" % q for q in range(16)] + ["CAi%d" % q for q in range(16)]
                            if True:
                                py, pyk = psY[ycc[0] % 2], "psYt%d" % (ycc[0] % 2)
                                ycc[0] += 1
                                u3 = uT[:, j, kb * 512:(kb + 1) * 512].rearrange("p (c r) -> p c r", r=32)

                                def mmt(e, py=py, u3=u3):
                                    ins = None
                                    first = True
                                    for dr in range(2):
                                        for tau in range(32):
                                            if dr == 0:
                                                o_ = py[:, :, tau:32]
                                                r_ = u3[:, :, 0:32 - tau]
                                            else:
                                                o_ = py[:, :, 0:32 - tau]
                                                r_ = u3[:, :, tau:32]
                                            ins = e.matmul(o_, lhsT=W2[:, dr, tau, :], rhs=r_, start=first, stop=(dr == 1 and tau == 31))
                                            first = False
                                    return ins
                                ukey = "uT%d_%d" % (j, kb)
                                P.op("pe", mmt, reads=["W2_%d_%d" % (d_, t_) for d_ in range(2) for t_ in range(8)] + [ukey], writes=[pyk])
                                P.op("dve", lambda e, py=py, u3=u3, j=j: e.scalar_tensor_tensor(out=u3, in0=u3, scalar=dsk[:, j:j + 1], in1=py[:], op0=ALU.mult, op1=ALU.add),
                                     reads=[pyk, "dsk"], writes=[ukey])

                        prep_ab(0, range(16), "ab")
                        prep_up(0, range(5))
                        prep_ab(0, range(16), "ca")
                        for j in range(4):
                            if j > 0:
                                prep_up(j, range(5))
                            main_pre(j)
                            for kb in range(8):
                                taps(j, kb)
                                if j + 1 < 4:
                                    prep_ab(j + 1, [2 * kb, 2 * kb + 1])
                    P.barrier()
                    if stop == "B2":
                        return finish([("uT", uT[:], [128, 4, LE], BF16), ("Sall", Sall[:], [128, 32, 2, NCE], BF16)])
                    with contextlib.ExitStack() as st:
                        E32 = sb(st, "E32", [128, 2, 32, NCH + 1], F32)
                        Ec32 = sb(st, "Ec32", [128, 2, 32, NCC + 1], F32)
                        sp1 = sb(st, "sp1", [128, 2, 32], F32)
                        sp2 = sb(st, "sp2", [128, 2, 32], F32)
                        sq_ = sb(st, "sq_", [128, 2, 32], F32)
                        def scan_steps():
                            steps = []
                            Sv = lambda rows, c: Sall[rows, :, :, c].rearrange("p g r -> p r g")
                            for half, eng in ((0, "dve"), (1, "pool")):
                                rows = slice(64 * half, 64 * half + 64)
                                hk = "scan%d" % half
                                seq = []
                                c_init = 0 if half == 0 else NCC
                                seq.append(("init", None))
                                order_c = list(range(NCC)) if half == 0 else list(range(NCC - 1, -1, -1))
                                for c in order_c:
                                    seq.append(("ctx", c))
                                seq.append(("seed", None))
                                order_m = list(range(NCH)) if half == 0 else list(range(NCH - 1, -1, -1))
                                for c in order_m:
                                    seq.append(("main", c))
                                steps.append((half, eng, rows, hk, seq))
                            return steps

                        def emit_scan_step(half, eng, rows, hk, item):
                            kind, c = item
                            if kind == "init":
                                ci = 0 if half == 0 else NCC
                                P.op(eng, lambda e: e.memset(Ec32[rows, :, :, ci], 0.0), reads=[], writes=[hk + "X"])
                                return
                            if kind == "seed":
                                src = Ec32[rows, :, :, NCC] if half == 0 else Ec32[rows, :, :, 0]
                                dst = E32[rows, :, :, 0] if half == 0 else E32[rows, :, :, NCH]
                                P.op(eng, lambda e: e.tensor_copy(out=dst, in_=src), reads=[hk + "X"], writes=[hk + "X"])
                                return
                            Ebuf = Ec32 if kind == "ctx" else E32
                            scol = NCH + c if kind == "ctx" else c
                            if half == 0:
                                cin, cout = c, c + 1
                            else:
                                cin, cout = c + 1, c
                            X = Ebuf[rows, :, :, cin]
                            Xo = Ebuf[rows, :, :, cout]
                            S_ = Sall[rows, :, :, scol].rearrange("p g r -> p r g")
                            P.op(eng, lambda e: e.tensor_tensor(out=sp1[rows], in0=X, in1=CA[rows], op=ALU.mult), reads=[hk + "X", "CA"], writes=[hk + "p1"])
                            P.op(eng, lambda e: e.tensor_tensor(out=sp2[rows, 0, :], in0=Ebuf[rows, 1, :, cin], in1=CBn[rows], op=ALU.mult), reads=[hk + "X", "CB"], writes=[hk + "p2a"])
                            P.op(eng, lambda e: e.tensor_tensor(out=sp2[rows, 1, :], in0=Ebuf[rows, 0, :, cin], in1=CBp[rows], op=ALU.mult), reads=[hk + "X", "CB"], writes=[hk + "p2b"])
                            P.op(eng, lambda e: e.tensor_tensor(out=sq_[rows], in0=sp1[rows], in1=S_, op=ALU.add), reads=[hk + "p1", "Sall"], writes=[hk + "q"])
                            P.op(eng, lambda e: e.tensor_tensor(out=Xo, in0=sq_[rows], in1=sp2[rows], op=ALU.add), reads=[hk + "q", hk + "p2a", hk + "p2b"], writes=[hk + "X"])

                        steps = scan_steps()
                        pos = [0, 0]

                        def advance_scan(n):
                            for (half, eng, rows, hk, seq) in steps:
                                for _ in range(n):
                                    if pos[half] < len(seq):
                                        emit_scan_step(half, eng, rows, hk, seq[pos[half]])
                                        pos[half] += 1

                        cvb = [sb(st, "cvb%d" % i, [128, 4, 512], BF16) for i in range(2)]
                        csq = [sb(st, "csq%d" % i, [128, 512], BF16) for i in range(1)]
                        mean = sb(st, "mean", [128, 512], F32)
                        rsd = sb(st, "rsd", [128, 512], F32)
                        ctm = [sb(st, "ctm%d" % i, [128, 512], F32) for i in range(1)]
                        cob = [sb(st, "cob%d" % i, [128, 512], BF16) for i in range(1)]
                        DgAll = sb(st, "DgAll", [128, 4, 31, 128], BF16)
                        psC = [pst(st, "psC%d" % i, [128, 512], F32) for i in range(2)]
                        psM = [pst(st, "psM%d" % i, [128, 512], F32) for i in range(2)]
                        psQ = [pst(st, "psQ%d" % i, [128, 512], F32) for i in range(2)]
                        for j in range(4):
                            P.op("pool", lambda e, j=j: e.tensor_tensor(out=DgAll[:, j, :, :], in0=ident.unsqueeze(1).to_broadcast([128, 31, 128]),
                                                                        in1=cw[:, j, :].unsqueeze(2).to_broadcast([128, 31, 128]), op=ALU.mult),
                                 reads=["cst", "cw"], writes=["Dg%d" % j])
                        cc = [0]

                        def conv_X(kb):
                            cvt, cvk = cvb[kb % 2], "cvb%d" % (kb % 2)
                            pm, pmk = psM[kb % 2], "psM%d" % (kb % 2)
                            pq, pqk = psQ[kb % 2], "psQ%d" % (kb % 2)

                            def stats(j):
                                cs_, csk = csq[0], "csq0"
                                P.op("pe", lambda e: e.matmul(pm[:], lhsT=onesb[:], rhs=cvt[:, j, :], start=(j == 0), stop=(j == 3)),
                                     reads=[cvk + "_%d" % j, "onesb"], writes=[pmk])
                                P.op("pe", lambda e: e.matmul(pq[:], lhsT=onesb[:], rhs=cs_[:], start=(j == 0), stop=(j == 3)),
                                     reads=[csk, "onesb"], writes=[pqk])

                            def mm_part(j):
                                pc, pck = psC[cc[0] % 2], "psC%d" % (cc[0] % 2)
                                cc[0] += 1

                                def mmc(e):
                                    ins = None
                                    taps = [15] + [k for k in range(31) if k != 15]
                                    todo = []
                                    for k in taps:
                                        dl = 64 * (k - 15)
                                        lo = max(512 * kb, -dl)
                                        hi = min(512 * kb + 512, L - dl)
                                        if lo < hi:
                                            todo.append((k, lo, hi, dl))
                                    for n_, (k, lo, hi, dl) in enumerate(todo):
                                        ins = e.matmul(pc[:, lo - 512 * kb:hi - 512 * kb], lhsT=DgAll[:, j, k, :], rhs=hcT[:, j, lo + dl:hi + dl],
                                                       start=(n_ == 0), stop=(n_ == len(todo) - 1))
                                    return ins
                                P.op("pe", mmc, reads=["Dg%d" % j], writes=[pck])
                                return pc, pck

                            def act_part(j, pc, pck):
                                P.op("act", lambda e: e.activation(out=cvt[:, j, :], in_=pc[:], func=AF.Identity, bias=cb[:, j:j + 1], scale=1.0),
                                     reads=[pck], writes=[cvk + "_%d" % j])
                                cs_, csk = csq[0], "csq0"
                                P.op("act", lambda e: e.activation(out=cs_[:], in_=cvt[:, j, :], func=AF.Square), reads=[cvk + "_%d" % j], writes=[csk])
                            for j in range(4):
                                pc, pck = mm_part(j)
                                if j >= 1:
                                    stats(j - 1)
                                act_part(j, pc, pck)
                                advance_scan(3)
                            stats(3)

                        def conv_Y(kb):
                            cvt, cvk = cvb[kb % 2], "cvb%d" % (kb % 2)
                            pm, pmk = psM[kb % 2], "psM%d" % (kb % 2)
                            pq, pqk = psQ[kb % 2], "psQ%d" % (kb % 2)
                            P.op("dve", lambda e: e.tensor_scalar(out=mean[:], in0=pm[:], scalar1=1.0 / 512.0, scalar2=None, op0=ALU.mult), reads=[pmk], writes=["mean"])
                            P.op("dve", lambda e: e.tensor_tensor(out=rsd[:], in0=mean[:], in1=mean[:], op=ALU.mult), reads=["mean"], writes=["rsd"])
                            P.op("dve", lambda e: e.scalar_tensor_tensor(out=rsd[:], in0=pq[:], scalar=1.0 / 512.0, in1=rsd[:], op0=ALU.mult, op1=ALU.subtract),
                                 reads=[pqk, "rsd"], writes=["rsd"])
                            P.op("act", lambda e: e.activation(out=rsd[:], in_=rsd[:], func=AF.Sqrt, bias=cst[:, C_MK + 4:C_MK + 5], scale=1.0), reads=["rsd"], writes=["rsd"])
                            P.op("dve", lambda e: e.reciprocal(out=rsd[:], in_=rsd[:]), reads=["rsd"], writes=["rsd"])

                            def ln(j):
                                ct_, ctk = ctm[0], "ctm0"
                                co_, cok = cob[0], "cob0"
                                P.op("dve", lambda e: e.tensor_tensor(out=ct_[:], in0=cvt[:, j, :], in1=mean[:], op=ALU.subtract),
                                     reads=[cvk + "_%d" % j, "mean"], writes=[ctk])
                                P.op("dve", lambda e: e.tensor_tensor(out=ct_[:], in0=ct_[:], in1=rsd[:], op=ALU.mult), reads=[ctk, "rsd"], writes=[ctk])
                                P.op("act", lambda e: e.activation(out=co_[:], in_=ct_[:], func=AF.Silu, bias=lnb[:, j:j + 1], scale=lng[:, j:j + 1]),
                                     reads=[ctk], writes=[cok])
                                P.dma("sp", mix_d[512 + j * 128:512 + (j + 1) * 128, kb * 512:(kb + 1) * 512], co_[:], reads=[cok], writes=["mixd_c%d_%d" % (j, kb)])
                            for j in range(4):
                                ln(j)
                                advance_scan(3)

                        conv_X(0)
                        for kb in range(8):
                            if kb + 1 < 8:
                                conv_X(kb + 1)
                            conv_Y(kb)
                        advance_scan(10000)
                        for ri in range(2):
                            P.op("act", lambda e, ri=ri: e.activation(out=Ebf[0:64, :, ri, 0:NCH], in_=E32[0:64, ri, :, 0:NCH], func=AF.Copy), reads=["scan0X"], writes=["Sall"])
                            P.op("act", lambda e, ri=ri: e.activation(out=Ebf[64:128, :, ri, 0:NCH], in_=E32[64:128, ri, :, 1:NCH + 1], func=AF.Copy), reads=["scan1X"], writes=["Sall"])
                    P.barrier()
                    if stop == "B3":
                        return finish([("Sall", Sall[:], [128, 32, 2, NCE], BF16)])
                with contextlib.ExitStack() as st:
                    W3 = [sb(st, "W3_%d" % i, [128, 8, 2, 32, 16], BF16) for i in range(2)]
                    wglu = sb(st, "wglu", [128, 4, 512], BF16)
                    P.dma("pool", wglu[:], wglu_d.rearrange("(k p) n -> p k n", p=128), writes=["wglu"])
                    wa = [sb(st, "wa%d" % i, [128, 4, 32, 16], F32) for i in range(2)]
                    wb_ = [sb(st, "wb%d" % i, [128, 4, 32, 16], F32) for i in range(2)]
                    Yc = [sb(st, "Yc%d" % i, [128, 32, 8, 16], BF16) for i in range(2)]
                    ytm = [sb(st, "ytm%d" % i, [128, 8, 128], F32) for i in range(2)]
                    sgl = [sb(st, "sgl%d" % i, [128, 512], BF16) for i in range(2)]
                    msb = [sb(st, "msb%d" % i, [128, 512], BF16) for i in range(2)]
                    psY = [pst(st, "psYr%d" % i, [128, 512], F32) for i in range(2)]
                    psT = [pst(st, "psTr%d" % i, [128, 8, 128], BF16) for i in range(2)]
                    psL = [pst(st, "psL%d" % i, [128, 512], F32) for i in range(2)]
                    yc = 0
                    tcn = 0
                    def build_w3(j):
                        g0 = 8 * j
                        W3j, w3k = W3[j % 2], "W3_%d" % (j % 2)
                        for ri in range(2):
                            for hf in range(2):
                                eng = "pool" if (ri == 1 and hf == 1) else "dve"
                                t1, t2 = (wb_ if eng == "pool" else wa)
                                k1, k2 = ("wb0", "wb1") if eng == "pool" else ("wa0", "wa1")
                                gs = slice(g0 + 4 * hf, g0 + 4 * hf + 4)
                                p3r = pw_r[:, gs, 32:64].unsqueeze(3).to_broadcast([128, 4, 32, 16])
                                p3i = pw_i[:, gs, 32:64].unsqueeze(3).to_broadcast([128, 4, 32, 16])
                                if ri == 0:
                                    c1 = Cr[:, gs, :].unsqueeze(2).to_broadcast([128, 4, 32, 16])
                                    c2 = nCi[:, gs, :].unsqueeze(2).to_broadcast([128, 4, 32, 16])
                                    pa_, pb_ = p3r, p3i
                                else:
                                    c1 = nCr[:, gs, :].unsqueeze(2).to_broadcast([128, 4, 32, 16])
                                    c2 = nCi[:, gs, :].unsqueeze(2).to_broadcast([128, 4, 32, 16])
                                    pa_, pb_ = p3i, p3r
                                P.op(eng, lambda e, t1=t1, c1=c1, pa_=pa_: e.tensor_tensor(out=t1[:], in0=c1, in1=pa_, op=ALU.mult), reads=["pw", "C"], writes=[k1])
                                P.op(eng, lambda e, t2=t2, c2=c2, pb_=pb_: e.tensor_tensor(out=t2[:], in0=c2, in1=pb_, op=ALU.mult), reads=["pw", "C"], writes=[k2])
                                P.op(eng, lambda e, t1=t1, t2=t2, W3j=W3j, hf=hf, ri=ri: e.tensor_tensor(out=W3j[:, 4 * hf:4 * hf + 4, ri, :, :], in0=t1[:], in1=t2[:], op=ALU.add),
                                     reads=[k1, k2], writes=[w3k + "_%d%d" % (ri, hf)])

                    build_w3(0)
                    for j in range(4):
                        g0 = 8 * j
                        W3j, w3k = W3[j % 2], "W3_%d" % (j % 2)
                        Ycj, yck = Yc[j % 2], "Yc%d" % (j % 2)
                        if j + 1 < 4:
                            build_w3(j + 1)
                        w3keys = [w3k + "_%d%d" % (ri, hf) for ri in range(2) for hf in range(2)]
                        for gl in range(8):
                            py, pyk = psY[yc % 2], "psYr%d" % (yc % 2)
                            yc += 1

                            def mmr(e, py=py, g=g0 + gl, gl=gl, W3j=W3j):
                                e.matmul(py[:], lhsT=Ebf[:, g, 0, 0:NCH], rhs=W3j[:, gl, 0, :, :].rearrange("p r h -> p (r h)"), start=True, stop=False)
                                return e.matmul(py[:], lhsT=Ebf[:, g, 1, 0:NCH], rhs=W3j[:, gl, 1, :, :].rearrange("p r h -> p (r h)"), start=False, stop=True)
                            P.op("pe", mmr, reads=["Ebf"] + w3keys, writes=[pyk])
                            P.op("act", lambda e, py=py, Ycj=Ycj, gl=gl: e.activation(out=Ycj[:, :, gl, :], in_=py[:].rearrange("p (r h) -> p r h", h=16), func=AF.Copy), reads=[pyk], writes=[yck + "_%d" % gl])
                        ykeys = [yck + "_%d" % gl for gl in range(8)]
                        uv = uT[:, j, 0:L].rearrange("p (c r) -> p r c", r=32)
                        for r8 in range(4):
                            pt, ptk = psT[tcn % 2], "psTr%d" % (tcn % 2)
                            ym, ymk = ytm[tcn % 2], "ytm%d" % (tcn % 2)
                            tcn += 1

                            def trr(e, pt=pt, Ycj=Ycj, r8=r8):
                                ins = None
                                for rr in range(8):
                                    r = r8 * 8 + rr
                                    ins = e.transpose(out=pt[:, rr, :], in_=Ycj[:, r, :, :].rearrange("p g h -> p (g h)"), identity=identb[:])
                                return ins
                            P.op("pe", trr, reads=ykeys + ["identb"], writes=[ptk])
                            ukeys = ["uT%d_%d" % (j, kb) for kb in range(8)]
                            P.op("dve", lambda e, pt=pt, ym=ym, uv=uv, r8=r8: e.tensor_tensor(out=ym[:], in0=pt[:], in1=uv[:, r8 * 8:(r8 + 1) * 8, :], op=ALU.add),
                                 reads=[ptk] + ukeys, writes=[ymk])
                            P.op("act", lambda e, ym=ym, uv=uv, r8=r8: e.activation(out=uv[:, r8 * 8:(r8 + 1) * 8, :], in_=ym[:], func=AF.Gelu_apprx_tanh),
                                 reads=[ymk], writes=ukeys)
                    lc = 0
                    for kb in range(8):
                        for jo in range(4):
                            pl, plk = psL[lc % 2], "psL%d" % (lc % 2)
                            sg, sgk = sgl[lc % 2], "sgl%d" % (lc % 2)
                            ms, msk = msb[lc % 2], "msb%d" % (lc % 2)
                            lc += 1

                            def mml(e, pl=pl, jo=jo, kb=kb):
                                ins = None
                                for k in range(4):
                                    ins = e.matmul(pl[:], lhsT=wglu[:, k, jo * 128:(jo + 1) * 128], rhs=uT[:, k, kb * 512:(kb + 1) * 512], start=(k == 0), stop=(k == 3))
                                return ins
                            P.op("pe", mml, reads=["wglu"] + ["uT%d_%d" % (k, kb) for k in range(4)], writes=[plk])
                            P.op("act", lambda e, pl=pl, sg=sg: e.activation(out=sg[:], in_=pl[:], func=AF.Sigmoid), reads=[plk], writes=[sgk])
                            P.op("dve", lambda e, sg=sg, ms=ms, jo=jo, kb=kb: e.tensor_tensor(out=ms[:], in0=uT[:, jo, kb * 512:(kb + 1) * 512], in1=sg[:], op=ALU.mult),
                                 reads=[sgk, "uT%d_%d" % (jo, kb)], writes=[msk])
                            P.dma("sp", mix_d[jo * 128:(jo + 1) * 128, kb * 512:(kb + 1) * 512], ms[:], reads=[msk], writes=["mixd_s%d_%d" % (jo, kb)])
                P.barrier()
        if stop == "B4":
            return finish([("modT", modT[:], [128, 48, 2], F32)])
        with contextlib.ExitStack() as st:
            w1b = sb(st, "w1b", [128, 8, 4 * D], BF16)
            G1b = sb(st, "G1b", [128, D], F32)
            G2b = sb(st, "G2b", [128, D], F32)
            FGb = sb(st, "FGb", [128, D], F32)
            dg = [sb(st, "dg%d" % i, [128, 128], F32) for i in range(2)]
            if stop != "C00b":
                P.dma("sp", FGb[:], fg_d[:, :], writes=["FGb"])
            wout = sb(st, "wout", [128, 8, D], BF16)
            if stop != "C00a":
                P.dma("pool", wout[:], wout_d.rearrange("(k p) n -> p k n", p=128), writes=["wout"])
            w2b = sb(st, "w2b", [128, 32, D], BF16)
            mixb = [sb(st, "mixb%d" % i, [128, 8, 256], BF16) for i in range(2)]
            xh = [sb(st, "xh%d" % i, [128, D], F32) for i in range(4)]
            tmpG = [sb(st, "tmpG%d" % i, [128, 512], F32) for i in range(1)]
            xs = [sb(st, "xsc%d" % i, [128, D], BF16) for i in range(1)]
            junkc = xs[0]
            a2T = [sb(st, "a2T%d" % i, [128, 8, 256], BF16) for i in range(2)]
            ssq = [sb(st, "ssc%d" % i, [128, 8], F32) for i in range(4)]
            rT = [sb(st, "rT%d" % i, [128, 256], BF16) for i in range(3)]
            hT = [sb(st, "hT%d" % i, [128, 256], BF16) for i in range(3)]
            h2 = [sb(st, "h2_%d" % i, [128, D], F32) for i in range(2)]
            pso = [pst(st, "pso%d" % i, [128, 512], F32) for i in range(4)]
            psf = [pst(st, "psf%d" % i, [128, 256], F32) for i in range(3)]
            psx = pst(st, "psx", [128, 512], F32)
            psxT = psx.bitcast(BF16).rearrange("p (a m) -> p a m", m=128)

            if stop in ("C00", "C00a", "C00b"):
                return finish([("modT", modT[:], [128, 48, 2], F32)])
            for gi, (base, Gb) in enumerate(((16, G1b), (40, G2b))):
                for j in range(8):
                    d_ = dg[j % 2]
                    dk = "dg%d" % (j % 2)
                    P.op("dve", lambda e, d_=d_, base=base, j=j: e.tensor_scalar(out=d_[:], in0=ident, scalar1=modT[:, base + j, 0:1], scalar2=None, op0=ALU.mult),
                         reads=["modT", "cst"], writes=[dk])
                    pg = pso[gi * 2 + j // 4]
                    pk = "pso%d" % (gi * 2 + j // 4)
                    P.op("pe", lambda e, pg=pg, d_=d_, j=j: e.matmul(pg[:, (j % 4) * 128:(j % 4 + 1) * 128], lhsT=onesf, rhs=d_[:], start=True, stop=True),
                         reads=[dk, "cst"], writes=[pk + "_%d" % (j % 4)])
                for h in range(2):
                    pg = pso[gi * 2 + h]
                    pk = "pso%d" % (gi * 2 + h)
                    P.op("act", lambda e, pg=pg, Gb=Gb, h=h: e.activation(out=Gb[:, h * 512:(h + 1) * 512], in_=pg[:], func=AF.Copy),
                         reads=[pk + "_%d" % q for q in range(4)], writes=["G%d" % gi])
            P.barrier()
            if stop == "C0":
                return finish([("G1b", G1b[:], [128, D], F32)])
            w1v = w1_d.rearrange("(k p) n -> p k n", p=128)
            for k in range(8):
                P.dma("pool", w1b[:, k, :], w1v[:, k, :], writes=["w1b%d" % k])
            w2v = w2_d.rearrange("(f p) n -> p f n", p=128)
            for f4 in range(8):
                P.dma("pool", w2b[:, f4 * 4:(f4 + 1) * 4, :], w2v[:, f4 * 4:(f4 + 1) * 4, :], writes=["w2b%d" % f4])
            w1keys = ["w1b%d" % k for k in range(8)]
            finals = []
            NBLK = L // 256
            mixv = mix_d.rearrange("(k p) t -> p k t", p=128)

            def c_load(blk):
                t0 = blk * 256
                mb, mbk = mixb[blk % 2], "mixb%d" % (blk % 2)
                P.dma("sp", mb[:], mixv[:, :, t0:t0 + 256], writes=[mbk])
                for tt in range(2):
                    xi = (blk * 2 + tt) % 4
                    P.dma("sp", xh[xi][:], x_d[t0 + tt * 128:t0 + (tt + 1) * 128, :], writes=["xh%d" % xi])

            def c_wout(blk, tt, hf):
                mb, mbk = mixb[blk % 2], "mixb%d" % (blk % 2)
                xi = (blk * 2 + tt) % 4
                xb, xk = xh[xi], "xh%d" % xi
                cs = slice(hf * 512, (hf + 1) * 512)
                tg, tgk = tmpG[0], "tmpG0"

                def mmo(e):
                    ins = None
                    for k in range(8):
                        ins = e.matmul(psx[:], lhsT=mb[:, k, tt * 128:(tt + 1) * 128], rhs=wout[:, k, cs], start=(k == 0), stop=(k == 7))
                    return ins
                P.op("pe", mmo, reads=[mbk, "wout"], writes=["psx"])
                P.op("dve", lambda e: e.tensor_tensor(out=tg[:], in0=psx[:], in1=G1b[:, cs], op=ALU.mult), reads=["psx"], writes=[tgk])
                P.op("dve", lambda e: e.tensor_tensor(out=xb[:, cs], in0=xb[:, cs], in1=tg[:], op=ALU.add), reads=[tgk, xk], writes=[xk])

            def c_rms_act(blk, tt):
                xi = (blk * 2 + tt) % 4
                xb, xk = xh[xi], "xh%d" % xi
                sq, sk = ssq[xi], "ssc%d" % xi
                P.op("act", lambda e: e.activation(out=junkc[:], in_=xb[:], func=AF.Square, accum_out=sq[:, 0:1]), reads=[xk], writes=["xsc0", sk])
                P.op("act", lambda e: e.activation(out=sq[:, 2:3], in_=sq[:, 0:1], func=AF.Sqrt, bias=cst[:, C_MK + 3:C_MK + 4], scale=1.0 / D), reads=[sk], writes=[sk])

            def c_rms_dve(blk, tt):
                xi = (blk * 2 + tt) % 4
                xb, xk = xh[xi], "xh%d" % xi
                sq, sk = ssq[xi], "ssc%d" % xi
                xsb, xsk = xs[0], "xsc0"
                P.op("dve", lambda e: e.reciprocal(out=sq[:, 3:4], in_=sq[:, 2:3]), reads=[sk], writes=[sk])
                P.op("dve", lambda e: e.tensor_scalar(out=xsb[:], in0=xb[:], scalar1=sq[:, 3:4], scalar2=None, op0=ALU.mult), reads=[xk, sk], writes=[xsk])

            def c_tr(blk, tt):
                xsb, xsk = xs[0], "xsc0"

                def tr(e):
                    ins = None
                    for j in range(8):
                        ins = e.transpose(out=psxT[:, j, :], in_=xsb[:, j * 128:(j + 1) * 128], identity=identb[:])
                    return ins
                P.op("pe", tr, reads=[xsk], writes=["psx"])

            def c_ev(blk, tt):
                a2, a2k = a2T[blk % 2], "a2T%d" % (blk % 2)

                def ev(e):
                    ins = None
                    for j in range(8):
                        ins = e.activation(out=a2[:, j, tt * 128:(tt + 1) * 128], in_=psxT[:, j, :], func=AF.Identity,
                                           bias=modT[:, 24 + j, 0:1], scale=s2[:, j:j + 1])
                    return ins
                P.op("act", ev, reads=["psx"], writes=[a2k + "_%d" % tt])

            def prologue_steps(blk):
                st_ = []
                for tt in range(2):
                    st_.append(lambda tt=tt: c_wout(blk, tt, 0))
                    st_.append(lambda tt=tt: c_wout(blk, tt, 1))
                    st_.append(lambda tt=tt: c_rms_act(blk, tt))
                    st_.append(lambda tt=tt: c_rms_dve(blk, tt))
                    st_.append(lambda tt=tt: c_tr(blk, tt))
                    st_.append(lambda tt=tt: c_ev(blk, tt))
                return st_

            fcn = [0]

            def emit_w1(blk, f):
                a2, a2k = a2T[blk % 2], "a2T%d" % (blk % 2)
                i_ = fcn[0] % 3
                fcn[0] += 1
                pf, pfk = psf[i_], "psf%d" % i_
                r_, rk = rT[i_], "rT%d" % i_
                h_, hk_ = hT[i_], "hT%d" % i_

                def mm1(e):
                    ins = None
                    for k in range(8):
                        ins = e.matmul(pf[:], lhsT=w1b[:, k, f * 128:(f + 1) * 128], rhs=a2[:, k, :], start=(k == 0), stop=(k == 7))
                    return ins
                P.op("pe", mm1, reads=[a2k + "_0", a2k + "_1"] + w1keys, writes=[pfk])
                P.op("act", lambda e: e.activation(out=r_[:], in_=pf[:], func=AF.Relu), reads=[pfk], writes=[rk])
                P.op("pool", lambda e: e.tensor_tensor(out=h_[:], in0=r_[:], in1=r_[:], op=ALU.mult), reads=[rk], writes=[hk_])
                return h_, hk_

            def emit_w2(f, h_, hk_):
                def mm2(e):
                    ins = None
                    for tt in range(2):
                        for hf in range(2):
                            ins = e.matmul(pso[tt * 2 + hf][:], lhsT=h_[:, tt * 128:(tt + 1) * 128], rhs=w2b[:, f, hf * 512:(hf + 1) * 512], start=(f == 0), stop=(f == 31))
                    return ins
                P.op("pe", mm2, reads=[hk_, "w2b%d" % (f // 4)], writes=["pso%d" % i for i in range(4)])

            def epilogue(blk):
                for tt in range(2):
                    epi_mult(blk, tt)
                for tt in range(2):
                    epi_tile(blk, tt)

            def epi_mult(blk, tt):
                h2b, h2k = h2[tt], "h2_%d" % tt
                for hf in range(2):
                    cs = slice(hf * 512, (hf + 1) * 512)
                    P.op("dve", lambda e, hf=hf, cs=cs: e.tensor_tensor(out=h2b[:, cs], in0=pso[tt * 2 + hf][:], in1=G2b[:, cs], op=ALU.mult),
                         reads=["pso%d" % (tt * 2 + hf)], writes=[h2k + "_%d" % hf])

            def epi_tile(blk, tt):
                t0 = blk * 256
                if True:
                    xi = (blk * 2 + tt) % 4
                    xb, xk = xh[xi], "xh%d" % xi
                    h2b, h2k = h2[tt], "h2_%d" % tt
                    sq, sk = ssq[xi], "ssc%d" % xi
                    for hf in range(2):
                        cs = slice(hf * 512, (hf + 1) * 512)
                        P.op("dve", lambda e, cs=cs: e.tensor_tensor(out=h2b[:, cs], in0=h2b[:, cs], in1=xb[:, cs], op=ALU.add),
                             reads=[xk, h2k + "_%d" % hf], writes=[h2k + "_%d" % hf])
                    hk3 = [h2k + "_0", h2k + "_1"]
                    P.op("act", lambda e: e.activation(out=junkc[:], in_=h2b[:], func=AF.Square, accum_out=sq[:, 4:5]), reads=hk3, writes=["xsc0", sk])
                    P.op("act", lambda e: e.activation(out=sq[:, 6:7], in_=sq[:, 4:5], func=AF.Sqrt, bias=cst[:, C_MK + 3:C_MK + 4], scale=1.0 / D), reads=[sk], writes=[sk])
                    P.op("dve", lambda e: e.reciprocal(out=sq[:, 7:8], in_=sq[:, 6:7]), reads=[sk], writes=[sk])
                    P.op("dve", lambda e: e.scalar_tensor_tensor(out=h2b[:], in0=h2b[:], scalar=sq[:, 7:8], in1=FGb[:], op0=ALU.mult, op1=ALU.mult),
                         reads=hk3 + [sk], writes=hk3)
                    finals.append(P.dma("sp", out_d[t0 + tt * 128:t0 + (tt + 1) * 128, :], h2b[:], reads=hk3, writes=["out%d_%d" % (blk, tt)]))

            c_load(0)
            for stp in prologue_steps(0):
                stp()
            for blk in range(NBLK):
                nxt_steps = []
                if blk + 1 < NBLK:
                    c_load(blk + 1)
                    nxt_steps = prologue_steps(blk + 1)
                sched = {}
                for n_, stp in enumerate(nxt_steps):
                    sched[4 + 2 * n_] = stp
                pend = [emit_w1(blk, 0), emit_w1(blk, 1)]
                for f in range(32):
                    if f + 2 < 32:
                        pend.append(emit_w1(blk, f + 2))
                    h_, hk_ = pend.pop(0)
                    emit_w2(f, h_, hk_)
                    if f in sched:
                        sched[f]()
                epilogue(blk)
            P.emit(finals)
    return nc


_PROG = None


def _consts():
    c = np.zeros((128, NCST), np.float32)
    c[:, C_ID:C_ID + 128] = np.eye(128, dtype=np.float32)
    c[:, C_ONE:C_ONE + 128] = 1.0
    p = np.arange(128)
    c[:, C_BD:C_BD + 128] = (p[:, None] // 16 == p[None, :] // 16).astype(np.float32)
    s = np.arange(32, dtype=np.float32)
    ke = np.zeros((128, 98), np.float32)
    ke[:, 66:98] = np.arange(32, dtype=np.float32)[None, :]
    ke[:64, 0:32] = 31.0 - s
    ke[64:, 0:32] = s
    ke[:64, 32:64] = s + 1.0
    ke[64:, 32:64] = 32.0 - s
    ke[:, 64] = 32.0
    ke[:, 65] = 1.0
    c[:, C_KE:C_KE + 98] = ke
    c[:, C_MK] = ((p // 16) % 2 == 0).astype(np.float32)
    c[:, C_MK + 1] = ((p // 16) % 2 == 1).astype(np.float32)
    c[:, C_MK + 2] = (p >= 96).astype(np.float32)
    c[:, C_MK + 3] = 1e-6
    c[:, C_MK + 4] = 1e-5
    return c


def kernel(x, c, ctx, c_ctx, ada_w, ada_b, norm1_g, w_in, s5_lam_re, s5_lam_im, s5_log_dt,
           s5_b_re, s5_b_im, s5_c_re, s5_c_im, s5_d, s5_w_glu, conv_w, conv_b, conv_ln_g,
           conv_ln_b, w_out, norm2_g, mlp_w1, mlp_w2, final_g):
    global _PROG
    f = lambda a: np.ascontiguousarray(np.asarray(a, dtype=np.float32))
    x, c, ctx, c_ctx = f(x), f(c), f(ctx), f(c_ctx)
    nb = x.shape[0]
    if _PROG is None:
        _PROG = build_program()
    nc = _PROG
    col8 = lambda v: f(np.asarray(v).reshape(8, 128).T)
    col4 = lambda v: f(np.asarray(v).reshape(4, 128).T)
    shared = {
        "ada_w": f(ada_w[0]),
        "ada_b": f(np.asarray(ada_b[0]).reshape(48, 128).T),
        "n1g": col8(norm1_g[0]), "n2g": col8(norm2_g[0]),
        "fgb": f(np.broadcast_to(np.asarray(final_g)[None, :], (128, D))),
        "w_in": f(w_in[0]), "w_out": f(w_out[0]), "w1": f(mlp_w1[0]), "w2": f(mlp_w2[0]),
        "w_glu": f(s5_w_glu[0]),
        "lam_re": f(np.asarray(s5_lam_re[0]).transpose(0, 2, 1).reshape(128, 32)),
        "lam_im": f(np.asarray(s5_lam_im[0]).transpose(0, 2, 1).reshape(128, 32)),
        "log_dt": f(np.repeat(np.asarray(s5_log_dt[0])[:, None, :], 64, axis=1).reshape(128, 32)),
        "b_re": f(np.asarray(s5_b_re[0]).transpose(0, 2, 1, 3).reshape(128, 32, 16)),
        "b_im": f(np.asarray(s5_b_im[0]).transpose(0, 2, 1, 3).reshape(128, 32, 16)),
        "c_re": f(np.asarray(s5_c_re[0]).transpose(0, 3, 1, 2).reshape(128, 32, 16)),
        "c_im": f(np.asarray(s5_c_im[0]).transpose(0, 3, 1, 2).reshape(128, 32, 16)),
        "d_skip": col4(np.asarray(s5_d[0]).reshape(512)),
        "conv_w": f(np.asarray(conv_w[0]).T.reshape(4, 128, 31).transpose(1, 0, 2)),
        "conv_b": col4(conv_b[0]), "ln_g": col4(conv_ln_g[0]), "ln_b": col4(conv_ln_b[0]),
        "cst": _consts(),
    }
    in_maps = []
    for b in range(nb):
        m = dict(shared)
        m["x"] = x[b]
        m["ctx"] = ctx[b]
        m["cvec"] = f(np.stack([c[b].reshape(8, 128).T, c_ctx.reshape(8, 128).T], axis=-1))
        in_maps.append(m)
    res = run_bass_kernel_spmd(nc, in_maps, core_ids=list(range(nb)))
    return np.stack([np.asarray(r["out"], dtype=np.float32) for r in res.results], axis=0)
```

```python
import contextlib
import math
import numpy as np
import concourse.bass as bass
import concourse.mybir as mybir
from concourse.bass_utils import run_bass_kernel_spmd

F32 = mybir.dt.float32
BF16 = mybir.dt.bfloat16
AF = mybir.ActivationFunctionType
ALU = mybir.AluOpType

L = 4096
NCTX = 256
LE = L + NCTX
D = 1024
TCH = 32
NCH = L // TCH
NCC = NCTX // TCH
NCE = NCH + NCC
MAGIC = 12582912.0
TWO_PI = 2.0 * math.pi


class _Op:
    __slots__ = ("eng", "fn", "deps", "sig", "count", "dma", "dsem", "dval")

    def __init__(self, eng, fn):
        self.eng = eng
        self.fn = fn
        self.deps = []
        self.sig = False
        self.count = None
        self.dma = False
        self.dsem = None
        self.dval = None


class Prog:
    ENGS = ("pe", "act", "dve", "pool", "sp")
    NDMA = {"sp": 8, "pool": 4}

    def __init__(self, nc):
        self.nc = nc
        self.q = {e: [] for e in self.ENGS}
        self.lastw = {}
        self.readers = {}
        self.ndma = {e: 0 for e in self.NDMA}
        self.dma_hist = {e: [] for e in self.NDMA}
        self.pending = {e: [] for e in self.ENGS}

    def _add_dep(self, op, d):
        if d is op:
            return
        for x in op.deps:
            if x is d:
                return
        op.deps.append(d)
        if not d.dma:
            d.sig = True

    def _hazards(self, op, reads, writes):
        deps = []
        for k in reads:
            w = self.lastw.get(k)
            if w is not None:
                deps.append(w)
        for k in writes:
            w = self.lastw.get(k)
            if w is not None:
                deps.append(w)
            deps.extend(self.readers.get(k, ()))
        for k in reads:
            self.readers.setdefault(k, []).append(op)
        for k in writes:
            self.lastw[k] = op
            self.readers[k] = []
        for d in deps:
            self._add_dep(op, d)
        pend = self.pending[op.eng]
        if pend:
            for d in pend:
                self._add_dep(op, d)
            self.pending[op.eng] = []

    def op(self, eng, fn, reads=(), writes=()):
        o = _Op(eng, fn)
        self._hazards(o, reads, writes)
        self.q[eng].append(o)
        return o

    def dma(self, eng, out, in_, reads=(), writes=()):
        o = _Op(eng, lambda e: e.dma_start(out=out, in_=in_))
        o.dma = True
        n = self.ndma[eng]
        k = self.NDMA[eng]
        o.dsem = (eng, n % k)
        o.dval = 16 * (n // k + 1)
        self.ndma[eng] = n + 1
        hist = self.dma_hist[eng]
        if n >= k:
            o.deps.append(hist[n - k])
        hist.append(o)
        self._hazards(o, reads, writes)
        self.q[eng].append(o)
        return o

    def barrier(self):
        lasts = []
        for e in self.ENGS:
            for o in reversed(self.q[e]):
                if not o.dma:
                    lasts.append(o)
                    break
        for e, hist in self.dma_hist.items():
            lasts.extend(hist[-self.NDMA[e]:])
        for e in self.ENGS:
            self.pending[e] = list(lasts)
        self.lastw = {}
        self.readers = {}

    def emit(self, final_waits):
        nc = self.nc
        for e in self.ENGS:
            c = 0
            for o in self.q[e]:
                if o.sig and not o.dma:
                    c += 1
                    o.count = c
        with contextlib.ExitStack() as st:
            esem = {e: st.enter_context(nc.semaphore("s_" + e)) for e in ("pe", "act", "dve", "pool")}
            dsem = {}
            for e, k in self.NDMA.items():
                for i in range(k):
                    dsem[(e, i)] = st.enter_context(nc.semaphore("d_%s%d" % (e, i)))
            block = st.enter_context(nc.Block())

            def run(eng_name, engine):
                waited = {}

                def wait_for(d):
                    if d.dma:
                        key, sem, val = d.dsem, dsem[d.dsem], d.dval
                    else:
                        key, sem, val = d.eng, esem[d.eng], d.count
                    if waited.get(key, 0) >= val:
                        return
                    waited[key] = val
                    engine.wait_ge(sem, val)

                for o in self.q[eng_name]:
                    for d in o.deps:
                        wait_for(d)
                    ins = o.fn(engine)
                    if o.dma:
                        ins.then_inc(dsem[o.dsem], 16)
                    elif o.sig:
                        ins.then_inc(esem[eng_name], 1)
                if eng_name == "sp":
                    for d in final_waits:
                        wait_for(d)

            @block.tensor
            def _(eng):
                run("pe", eng)

            @block.scalar
            def _(eng):
                run("act", eng)

            @block.vector
            def _(eng):
                run("dve", eng)

            @block.gpsimd
            def _(eng):
                run("pool", eng)

            @block.sync
            def _(eng):
                run("sp", eng)


NCST = 128 + 128 + 128 + 98 + 5
C_ID, C_ONE, C_BD, C_KE, C_MK = 0, 128, 256, 384, 482


def build_program(stop=None):
    nc = bass.Bass("TRN2", target_bir_lowering=False)

    def din(name, shape, dt=F32):
        return nc.dram_tensor(name, list(shape), dt, kind="ExternalInput").ap()

    x_d = din("x", [L, D])
    ctx_d = din("ctx", [NCTX, D])
    cv_d = din("cvec", [128, 8, 2])
    adaw_d = din("ada_w", [D, 6 * D])
    adab_d = din("ada_b", [128, 48])
    n1g_d = din("n1g", [128, 8])
    n2g_d = din("n2g", [128, 8])
    fg_d = din("fgb", [128, D])
    win_d = din("w_in", [D, 1536])
    wout_d = din("w_out", [D, D])
    w1_d = din("w1", [D, 4 * D])
    w2_d = din("w2", [4 * D, D])
    wglu_d = din("w_glu", [512, 512])
    lre_d = din("lam_re", [128, 32])
    lim_d = din("lam_im", [128, 32])
    ldt_d = din("log_dt", [128, 32])
    bre_d = din("b_re", [128, 32, 16])
    bim_d = din("b_im", [128, 32, 16])
    cre_d = din("c_re", [128, 32, 16])
    cim_d = din("c_im", [128, 32, 16])
    dsk_d = din("d_skip", [128, 4])
    cw_d = din("conv_w", [128, 4, 31])
    cb_d = din("conv_b", [128, 4])
    lng_d = din("ln_g", [128, 4])
    lnb_d = din("ln_b", [128, 4])
    cst_d = din("cst", [128, NCST])
    out_d = nc.dram_tensor("out", [L, D], F32, kind="ExternalOutput").ap()
    mix_d = nc.dram_tensor("mixd", [D, L], BF16, kind=("ExternalOutput" if stop else "Internal")).ap()

    P = Prog(nc)

    def sb(st, name, shape, dt):
        return st.enter_context(nc.sbuf_tensor("sb_" + name, list(shape), dt))

    def pst(st, name, shape, dt):
        full = 512 if dt == F32 else 1024
        nfree = 1
        for d_ in shape[1:]:
            nfree *= d_
        assert nfree <= full
        t = st.enter_context(nc.psum_tensor("ps_" + name, [128, full], dt))
        ap = t[:, 0:nfree]
        if len(shape) == 3:
            ap = ap.rearrange("p (a b) -> p a b", b=shape[2])
        return ap

    dbg_fins = []

    def finish(dumps):
        P.barrier()
        fins = []
        for name, ap, shape, dt in dumps:
            d_ = nc.dram_tensor("dbg_" + name, list(shape), dt, kind="ExternalOutput").ap()
            fins.append(P.dma("sp", d_, ap))
        P.emit(fins)
        return nc

    with contextlib.ExitStack() as glob:
        cst = sb(glob, "cst", [128, NCST], F32)
        identb = sb(glob, "identb", [128, 128], BF16)
        onesb = sb(glob, "onesb", [128, 128], BF16)
        modT = sb(glob, "modT", [128, 48, 2], F32)
        s1 = sb(glob, "s1", [128, 8], F32)
        s1c = sb(glob, "s1c", [128, 8], F32)
        s2 = sb(glob, "s2", [128, 8], F32)
        n1g = sb(glob, "n1g", [128, 8], F32)
        n2g = sb(glob, "n2g", [128, 8], F32)
        dsk = sb(glob, "dsk", [128, 4], F32)
        cw = sb(glob, "cw", [128, 4, 31], F32)
        cb = sb(glob, "cb", [128, 4], F32)
        lng = sb(glob, "lng", [128, 4], F32)
        lnb = sb(glob, "lnb", [128, 4], F32)

        ident = cst[:, C_ID:C_ID + 128]
        onesf = cst[:, C_ONE:C_ONE + 128]
        bdmask = cst[:, C_BD:C_BD + 128]
        kexp = cst[:, C_KE:C_KE + 98]
        mk_even = cst[:, C_MK:C_MK + 1]
        mk_odd = cst[:, C_MK + 1:C_MK + 2]
        mk_hi = cst[:, C_MK + 2:C_MK + 3]

        P.dma("sp", cst[:], cst_d[:, :], writes=["cst"])
        P.dma("sp", n1g[:], n1g_d[:, :], writes=["n1g"])
        P.dma("sp", n2g[:], n2g_d[:, :], writes=["n2g"])
        P.dma("sp", dsk[:], dsk_d[:, :], writes=["dsk"])
        P.dma("sp", cw[:], cw_d[:, :, :], writes=["cw"])
        P.dma("sp", cb[:], cb_d[:, :], writes=["cb"])
        P.dma("sp", lng[:], lng_d[:, :], writes=["lng"])
        P.dma("sp", lnb[:], lnb_d[:, :], writes=["lnb"])
        P.op("dve", lambda e: e.tensor_copy(out=identb[:], in_=ident), reads=["cst"], writes=["identb"])
        P.op("dve", lambda e: e.tensor_copy(out=onesb[:], in_=onesf), reads=["cst"], writes=["onesb"])

        with contextlib.ExitStack() as stB:
            uT = sb(stB, "uT", [128, 4, LE], BF16)
            Sall = sb(stB, "Sall", [128, 32, 2, NCE], BF16)
            Ebf = Sall
            with contextlib.ExitStack() as stB2:
                pw_r = sb(stB2, "pw_r", [128, 32, 98], F32)
                pw_i = sb(stB2, "pw_i", [128, 32, 98], F32)
                bbr = sb(stB2, "bbr", [128, 32, 16], F32)
                bbi = sb(stB2, "bbi", [128, 32, 16], F32)
                Cr = sb(stB2, "Cr", [128, 32, 16], F32)
                Ci = sb(stB2, "Ci", [128, 32, 16], F32)
                nCr = sb(stB2, "nCr", [128, 32, 16], F32)
                nCi = sb(stB2, "nCi", [128, 32, 16], F32)
                Crb = sb(stB2, "Crb", [128, 32, 16], BF16)
                nCib = sb(stB2, "nCib", [128, 32, 16], BF16)
                CA = sb(stB2, "CA", [128, 2, 32], F32)
                CBn = sb(stB2, "CBn", [128, 32], F32)
                CBp = sb(stB2, "CBp", [128, 32], F32)
                with contextlib.ExitStack() as stH:
                    hcT = sb(stH, "hcT", [128, 4, L], BF16)
                    stP = contextlib.ExitStack()
                    cvs = sb(stP, "cvs", [128, 8, 2], F32)
                    csl = sb(stP, "csl", [128, 8, 2], F32)
                    adb = sb(stP, "adb", [128, 48], F32)
                    awb = [sb(stP, "awb%d" % i, [128, 8, 256], F32) for i in range(2)]
                    rowb = [sb(stP, "rowb%d" % i, [128, 256], F32) for i in range(2)]
                    ps_mod = pst(stP, "ps_mod", [128, 48, 2], F32)
                    psR = [pst(stP, "psR%d" % i, [128, 512], F32) for i in range(1)]
                    with contextlib.ExitStack() as st:
                        lre = sb(st, "lre", [128, 32], F32)
                        lim = sb(st, "lim", [128, 32], F32)
                        ldt = sb(st, "ldt", [128, 32], F32)
                        bre = sb(st, "bre", [128, 32, 16], F32)
                        bim = sb(st, "bim", [128, 32, 16], F32)
                        dtt = sb(st, "dtt", [128, 32], F32)
                        lr = sb(st, "lr", [128, 32], F32)
                        li = sb(st, "li", [128, 32], F32)
                        arg = sb(st, "arg", [128, 32, 98], F32)
                        mg = sb(st, "mg", [128, 32, 98], F32)
                        tn = sb(st, "tn", [128, 32, 98], F32)
                        tr_ = sb(st, "tr_", [128, 32, 98], F32)
                        fa = sb(st, "fa", [128, 32], F32)
                        fb = sb(st, "fb", [128, 32], F32)
                        fc = sb(st, "fc", [128, 32], F32)
                        den = sb(st, "den", [128, 32], F32)
                        fre = sb(st, "fre", [128, 32], F32)
                        fim = sb(st, "fim", [128, 32], F32)
                        tb1 = sb(st, "tb1", [128, 32, 16], F32)
                        tb2 = sb(st, "tb2", [128, 32, 16], F32)

                        P.dma("sp", cvs[:], cv_d[:, :, :], writes=["cvs"])
                        P.dma("sp", adb[:], adab_d[:, :], writes=["adb"])
                        P.dma("sp", lre[:], lre_d[:, :], writes=["lre"])
                        P.dma("sp", lim[:], lim_d[:, :], writes=["lim"])
                        P.dma("sp", ldt[:], ldt_d[:, :], writes=["ldt"])
                        P.dma("sp", bre[:], bre_d[:, :, :], writes=["bre"])
                        P.dma("sp", bim[:], bim_d[:, :, :], writes=["bim"])
                        P.dma("sp", Cr[:], cre_d[:, :, :], writes=["Cr"])
                        P.dma("sp", Ci[:], cim_d[:, :, :], writes=["Ci"])
                        P.op("act", lambda e: e.activation(out=csl[:], in_=cvs[:], func=AF.Silu), reads=["cvs"], writes=["csl"])

                        def emit_modT_tr(blk):
                            rb, rbk = rowb[blk % 2], "rowb%d" % (blk % 2)

                            def trm(e):
                                ins = None
                                for ctl in range(2):
                                    ins = e.transpose(out=ps_mod[:, blk * 2 + ctl, :], in_=rb[0:2, ctl * 128:(ctl + 1) * 128], identity=ident[0:2, 0:2])
                                return ins
                            P.op("pe", trm, reads=[rbk, "cst"], writes=["ps_mod"])
                        awv = adaw_d.rearrange("(k p) n -> p k n", p=128)
                        def ada_dma(blk):
                            buf = awb[blk % 2]
                            key = "awb%d" % (blk % 2)
                            P.dma("sp", buf[:], awv[:, :, blk * 256:(blk + 1) * 256], writes=[key])

                        def ada_mm(blk):
                            buf = awb[blk % 2]
                            key = "awb%d" % (blk % 2)
                            pr, prk = psR[0], "psR0"
                            rb, rbk = rowb[blk % 2], "rowb%d" % (blk % 2)

                            def mm(e):
                                ins = None
                                for k in range(8):
                                    ins = e.matmul(pr[0:2, 0:256], lhsT=csl[:, k, :], rhs=buf[:, k, :], start=(k == 0), stop=(k == 7))
                                return ins
                            P.op("pe", mm, reads=[key, "csl"], writes=[prk])
                            P.op("act", lambda e: e.activation(out=rb[0:2, :], in_=pr[0:2, 0:256], func=AF.Copy), reads=[prk], writes=[rbk])
                        ada_dma(0)
                        for blk in range(8):
                            if blk + 1 < 8:
                                ada_dma(blk + 1)
                            ada_mm(blk)
                            if blk >= 1:
                                emit_modT_tr(blk - 1)
                        emit_modT_tr(7)


                        V = lambda fn, r, w: P.op("dve", fn, reads=r, writes=w)
                        A_ = lambda fn, r, w: P.op("act", fn, reads=r, writes=w)
                        A_(lambda e: e.activation(out=dtt[:], in_=ldt[:], func=AF.Exp), ["ldt"], ["dtt"])
                        V(lambda e: e.tensor_tensor(out=lr[:], in0=lre[:], in1=dtt[:], op=ALU.mult), ["lre", "dtt"], ["lr"])
                        V(lambda e: e.tensor_tensor(out=li[:], in0=lim[:], in1=dtt[:], op=ALU.mult), ["lim", "dtt"], ["li"])
                        kb3 = kexp.unsqueeze(1).to_broadcast([128, 32, 98])
                        V(lambda e: e.tensor_tensor(out=mg[:], in0=kb3, in1=lr[:].unsqueeze(2).to_broadcast([128, 32, 98]), op=ALU.mult), ["cst", "lr"], ["mg"])
                        A_(lambda e: e.activation(out=mg[:], in_=mg[:], func=AF.Exp), ["mg"], ["mg"])
                        V(lambda e: e.tensor_tensor(out=arg[:], in0=kb3, in1=li[:].unsqueeze(2).to_broadcast([128, 32, 98]), op=ALU.mult), ["cst", "li"], ["arg"])

                        def trig(dst, shift, key):
                            if shift != 0.0:
                                V(lambda e: e.tensor_scalar(out=tr_[:], in0=arg[:], scalar1=shift, scalar2=None, op0=ALU.add), ["arg"], ["tr_"])
                                srcv = tr_
                            else:
                                V(lambda e: e.tensor_copy(out=tr_[:], in_=arg[:]), ["arg"], ["tr_"])
                                srcv = tr_
                            V(lambda e: e.tensor_scalar(out=tn[:], in0=srcv[:], scalar1=1.0 / TWO_PI, scalar2=MAGIC, op0=ALU.mult, op1=ALU.add), ["tr_"], ["tn"])
                            V(lambda e: e.tensor_scalar(out=tn[:], in0=tn[:], scalar1=-MAGIC, scalar2=None, op0=ALU.add), ["tn"], ["tn"])
                            V(lambda e: e.scalar_tensor_tensor(out=tr_[:], in0=tn[:], scalar=-TWO_PI, in1=srcv[:], op0=ALU.mult, op1=ALU.add), ["tn", "tr_"], ["tr_"])
                            V(lambda e: e.tensor_scalar(out=tr_[:], in0=tr_[:], scalar1=-3.1415925, scalar2=3.1415925, op0=ALU.max, op1=ALU.min), ["tr_"], ["tr_"])
                            A_(lambda e: e.activation(out=tr_[:], in_=tr_[:], func=AF.Sin), ["tr_"], ["tr_"])
                            V(lambda e: e.tensor_tensor(out=dst[:], in0=tr_[:], in1=mg[:], op=ALU.mult), ["tr_", "mg"], [key])
                        trig(pw_i, 0.0, "pw_i")
                        trig(pw_r, math.pi / 2.0, "pw_r")
                        V(lambda e: e.tensor_scalar(out=fa[:], in0=pw_r[:, :, 65], scalar1=-1.0, scalar2=None, op0=ALU.add), ["pw_r"], ["fa"])
                        V(lambda e: e.tensor_tensor(out=den[:], in0=lre[:], in1=lre[:], op=ALU.mult), ["lre"], ["den"])
                        V(lambda e: e.tensor_tensor(out=fb[:], in0=lim[:], in1=lim[:], op=ALU.mult), ["lim"], ["fb"])
                        V(lambda e: e.tensor_tensor(out=den[:], in0=den[:], in1=fb[:], op=ALU.add), ["den", "fb"], ["den"])
                        V(lambda e: e.reciprocal(out=den[:], in_=den[:]), ["den"], ["den"])
                        V(lambda e: e.tensor_tensor(out=fb[:], in0=fa[:], in1=lre[:], op=ALU.mult), ["fa", "lre"], ["fb"])
                        V(lambda e: e.tensor_tensor(out=fc[:], in0=pw_i[:, :, 65], in1=lim[:], op=ALU.mult), ["pw_i", "lim"], ["fc"])
                        V(lambda e: e.tensor_tensor(out=fb[:], in0=fb[:], in1=fc[:], op=ALU.add), ["fb", "fc"], ["fb"])
                        V(lambda e: e.tensor_tensor(out=fre[:], in0=fb[:], in1=den[:], op=ALU.mult), ["fb", "den"], ["fre"])
                        V(lambda e: e.tensor_tensor(out=fb[:], in0=pw_i[:, :, 65], in1=lre[:], op=ALU.mult), ["pw_i", "lre", "fre"], ["fb"])
                        V(lambda e: e.tensor_tensor(out=fc[:], in0=fa[:], in1=lim[:], op=ALU.mult), ["fa", "lim"], ["fc"])
                        V(lambda e: e.tensor_tensor(out=fb[:], in0=fb[:], in1=fc[:], op=ALU.subtract), ["fb", "fc"], ["fb"])
                        V(lambda e: e.tensor_tensor(out=fim[:], in0=fb[:], in1=den[:], op=ALU.mult), ["fb", "den"], ["fim"])
                        frb = fre[:].unsqueeze(2).to_broadcast([128, 32, 16])
                        fib = fim[:].unsqueeze(2).to_broadcast([128, 32, 16])
                        V(lambda e: e.tensor_tensor(out=tb1[:], in0=bre[:], in1=frb, op=ALU.mult), ["bre", "fre"], ["tb1"])
                        V(lambda e: e.tensor_tensor(out=tb2[:], in0=bim[:], in1=fib, op=ALU.mult), ["bim", "fim"], ["tb2"])
                        V(lambda e: e.tensor_tensor(out=bbr[:], in0=tb1[:], in1=tb2[:], op=ALU.subtract), ["tb1", "tb2"], ["bbr"])
                        V(lambda e: e.tensor_tensor(out=tb1[:], in0=bim[:], in1=frb, op=ALU.mult), ["bim", "fre", "bbr"], ["tb1"])
                        V(lambda e: e.tensor_tensor(out=tb2[:], in0=bre[:], in1=fib, op=ALU.mult), ["bre", "fim", "bbr"], ["tb2"])
                        V(lambda e: e.tensor_tensor(out=bbi[:], in0=tb1[:], in1=tb2[:], op=ALU.add), ["tb1", "tb2"], ["bbi"])
                        V(lambda e: e.tensor_scalar(out=nCr[:], in0=Cr[:], scalar1=-1.0, scalar2=None, op0=ALU.mult), ["Cr"], ["nCr"])
                        V(lambda e: e.tensor_scalar(out=nCi[:], in0=Ci[:], scalar1=-1.0, scalar2=None, op0=ALU.mult), ["Ci"], ["nCi"])
                        V(lambda e: e.tensor_copy(out=Crb[:], in_=Cr[:]), ["Cr"], ["Crb"])
                        V(lambda e: e.tensor_copy(out=nCib[:], in_=nCi[:]), ["nCi"], ["nCib"])
                        V(lambda e: e.tensor_copy(out=CA[:, 0, :], in_=pw_r[:, :, 64]), ["pw_r"], ["CA"])
                        V(lambda e: e.tensor_copy(out=CA[:, 1, :], in_=pw_r[:, :, 64]), ["pw_r", "CA"], ["CA"])
                        V(lambda e: e.tensor_copy(out=CBp[:], in_=pw_i[:, :, 64]), ["pw_i"], ["CBp"])
                        V(lambda e: e.tensor_scalar(out=CBn[:], in0=pw_i[:, :, 64], scalar1=-1.0, scalar2=None, op0=ALU.mult), ["pw_i"], ["CBn"])

                        P.op("dve", lambda e: e.tensor_tensor(out=modT[:, 0:16, :], in0=ps_mod[:, 0:16, :], in1=adb[:, 0:16].unsqueeze(2).to_broadcast([128, 16, 2]), op=ALU.add),
                             reads=["ps_mod", "adb"], writes=["modT"])
                        P.op("dve", lambda e: e.scalar_tensor_tensor(out=s1[:], in0=modT[:, 8:16, 0], scalar=1.0, in1=n1g[:], op0=ALU.add, op1=ALU.mult),
                             reads=["modT", "n1g"], writes=["s1"])
                        P.op("dve", lambda e: e.scalar_tensor_tensor(out=s1c[:], in0=modT[:, 8:16, 1], scalar=1.0, in1=n1g[:], op0=ALU.add, op1=ALU.mult),
                             reads=["modT", "n1g"], writes=["s1c"])
                    P.barrier()
                    if stop == "B1":
                        return finish([("pw_r", pw_r[:], [128, 32, 98], F32), ("pw_i", pw_i[:], [128, 32, 98], F32), ("bbr", bbr[:], [128, 32, 16], F32), ("bbi", bbi[:], [128, 32, 16], F32)])
                    with contextlib.ExitStack() as st:
                        winb = sb(st, "winb", [128, 8, 1536], BF16)
                        xt = [sb(st, "xt%d" % i, [128, D], F32) for i in range(3)]
                        junk = sb(st, "junk", [128, D], BF16)
                        ssq4 = [sb(st, "ssqa%d" % i, [128, 4], F32) for i in range(4)]
                        xs = [sb(st, "xs%d" % i, [128, D], BF16) for i in range(2)]
                        aT = [sb(st, "aT%d" % i, [128, 8, 512], BF16) for i in range(2)]
                        ssq = [sb(st, "ssq%d" % i, [128, 4], F32) for i in range(3)]
                        sig = [sb(st, "sig%d" % i, [128, 512], F32) for i in range(2)]
                        psT = [pst(st, "psTA%d" % i, [128, 8, 128], BF16) for i in range(2)]
                        psz = [pst(st, "psz%d" % i, [128, 512], F32) for i in range(4)]

                        for k in range(8):
                            P.dma("pool", winb[:, k, :], win_d[k * 128:(k + 1) * 128, :], writes=["winb%d" % k])
                        winkeys = ["winb%d" % k for k in range(8)]

                        tiles = []
                        for kb in range(L // 512):
                            for tt in range(4):
                                tiles.append((kb, tt, False))
                        for tt in range(2):
                            tiles.append((L // 512, tt, True))
                        NT = len(tiles)
                        xt4 = xt
                        zc = [0]

                        def a_s1a(t):
                            kb, tt, is_ctx = tiles[t]
                            xi = t % 3
                            xb, xk = xt4[xi], "xt%d" % xi
                            sq, sk = ssq4[t % 4], "ssq%d" % (t % 4)
                            src = ctx_d[tt * 128:(tt + 1) * 128, :] if is_ctx else x_d[kb * 512 + tt * 128: kb * 512 + (tt + 1) * 128, :]
                            P.dma("sp", xb[:], src, writes=[xk])
                            P.op("act", lambda e: e.activation(out=junk[:], in_=xb[:], func=AF.Square, accum_out=sq[:, 0:1]), reads=[xk], writes=["junk", sk])
                            P.op("act", lambda e: e.activation(out=sq[:, 2:3], in_=sq[:, 0:1], func=AF.Sqrt, bias=cst[:, C_MK + 3:C_MK + 4], scale=1.0 / D), reads=[sk], writes=[sk])

                        def a_s1b(t):
                            xi = t % 3
                            xb, xk = xt4[xi], "xt%d" % xi
                            sq, sk = ssq4[t % 4], "ssq%d" % (t % 4)
                            xsb, xsk = xs[t % 2], "xs%d" % (t % 2)
                            P.op("dve", lambda e: e.reciprocal(out=sq[:, 3:4], in_=sq[:, 2:3]), reads=[sk], writes=[sk])
                            P.op("dve", lambda e: e.tensor_scalar(out=xsb[:], in0=xb[:], scalar1=sq[:, 3:4], scalar2=None, op0=ALU.mult), reads=[xk, sk], writes=[xsk])

                        def a_s2(t):
                            kb, tt, is_ctx = tiles[t]
                            xsb, xsk = xs[t % 2], "xs%d" % (t % 2)
                            pt, ptk = psT[t % 2], "psTA%d" % (t % 2)
                            a, ak = aT[kb % 2], "aT%d" % (kb % 2)
                            sc_t = s1c if is_ctx else s1
                            shcol = 1 if is_ctx else 0

                            def tr(e):
                                ins = None
                                for j in range(8):
                                    ins = e.transpose(out=pt[:, j, :], in_=xsb[:, j * 128:(j + 1) * 128], identity=identb[:])
                                return ins
                            P.op("pe", tr, reads=[xsk], writes=[ptk])

                            def ev(e):
                                ins = None
                                for j in range(8):
                                    ins = e.activation(out=a[:, j, tt * 128:(tt + 1) * 128], in_=pt[:, j, :], func=AF.Identity,
                                                       bias=modT[:, j, shcol:shcol + 1], scale=sc_t[:, j:j + 1])
                                return ins
                            P.op("act", ev, reads=[ptk], writes=[ak + "_%d" % tt])

                        def a_block(kb):
                            is_ctx = kb == L // 512
                            ntile = 2 if is_ctx else 4
                            ntok = ntile * 128
                            a, ak = aT[kb % 2], "aT%d" % (kb % 2)
                            akeys = [ak + "_%d" % tt for tt in range(ntile)]
                            tok0 = L if is_ctx else kb * 512

                            def mm_cols(pz, c0):
                                def f_(e):
                                    ins = None
                                    for k in range(8):
                                        ins = e.matmul(pz[:, 0:ntok], lhsT=winb[:, k, c0:c0 + 128], rhs=a[:, k, 0:ntok], start=(k == 0), stop=(k == 7))
                                    return ins
                                return f_
                            for ct in range(4):
                                pz, pzk = psz[zc[0] % 4], "psz%d" % (zc[0] % 4)
                                zc[0] += 1
                                P.op("pe", mm_cols(pz, ct * 128), reads=akeys + winkeys, writes=[pzk])
                                P.op("act", lambda e, pz=pz, ct=ct: e.activation(out=uT[:, ct, tok0:tok0 + ntok], in_=pz[:, 0:ntok], func=AF.Copy),
                                     reads=[pzk], writes=["uT%d_%d" % (ct, kb)])
                            if is_ctx:
                                return
                            for ct in range(4):
                                pv, pvk = psz[zc[0] % 4], "psz%d" % (zc[0] % 4)
                                zc[0] += 1
                                pg, pgk = psz[zc[0] % 4], "psz%d" % (zc[0] % 4)
                                zc[0] += 1
                                P.op("pe", mm_cols(pg, 1024 + ct * 128), reads=akeys + winkeys, writes=[pgk])
                                P.op("pe", mm_cols(pv, 512 + ct * 128), reads=akeys + winkeys, writes=[pvk])
                                sg, sgk = sig[ct % 2], "sig%d" % (ct % 2)
                                P.op("act", lambda e, sg=sg, pg=pg: e.activation(out=sg[:], in_=pg[:], func=AF.Sigmoid), reads=[pgk], writes=[sgk])
                                P.op("dve", lambda e, sg=sg, pv=pv, ct=ct: e.tensor_tensor(out=hcT[:, ct, kb * 512:(kb + 1) * 512], in0=pv[:], in1=sg[:], op=ALU.mult),
                                     reads=[pvk, sgk], writes=["hcT%d_%d" % (ct, kb)])

                        ada_dma(8)
                        a_s1a(0)
                        a_s1a(1)
                        a_s1b(0)
                        for t in range(NT):
                            if t + 2 < NT:
                                a_s1a(t + 2)
                            if t + 1 < NT:
                                a_s1b(t + 1)
                            a_s2(t)
                            if t < 16:
                                if t + 1 < 16:
                                    ada_dma(8 + t + 1)
                                ada_mm(8 + t)
                                if t >= 1:
                                    emit_modT_tr(8 + t - 1)
                            if t == 16:
                                emit_modT_tr(23)
                                P.op("dve", lambda e: e.tensor_tensor(out=modT[:, 16:48, :], in0=ps_mod[:, 16:48, :], in1=adb[:, 16:48].unsqueeze(2).to_broadcast([128, 32, 2]), op=ALU.add),
                                     reads=["ps_mod", "adb"], writes=["modT2"])
                                P.op("dve", lambda e: e.scalar_tensor_tensor(out=s2[:], in0=modT[:, 32:40, 0], scalar=1.0, in1=n2g[:], op0=ALU.add, op1=ALU.mult),
                                     reads=["modT2", "n2g"], writes=["s2"])
                            kb, tt, is_ctx = tiles[t]
                            if tt == (1 if is_ctx else 3):
                                a_block(kb)
                    P.barrier()
                    stP.close()
                    if stop == "A":
                        return finish([("uT", uT[:], [128, 4, LE], BF16), ("hcT", hcT[:], [128, 4, L], BF16), ("modT", modT[:], [128, 48, 2], F32), ("G1b", G1b[:], [128, D], F32)])
                    with contextlib.ExitStack() as st:
                        ABr = sb(st, "ABr", [128, 8, 32, 16], BF16)
                        ABi = sb(st, "ABi", [128, 8, 32, 16], BF16)
                        CAr = sb(st, "CAr", [128, 8, 32, 16], BF16)
                        CAi = sb(st, "CAi", [128, 8, 32, 16], BF16)
                        tA = [sb(st, "tA%d" % i, [128, 8, 2, 16], F32) for i in range(2)]
                        tB = [sb(st, "tB%d" % i, [128, 8, 2, 16], F32) for i in range(2)]
                        Wp = sb(st, "Wp", [128, 8, 4, 2, 128], BF16)
                        W2 = sb(st, "W2", [128, 2, 32, 128], BF16)
                        Zc = sb(st, "Zc", [128, 1, 8, 8, 16], BF16)
                        Up = sb(st, "Up", [128, 8, LE // 8], BF16)
                        Bbrb = sb(st, "Bbrb", [128, 32, 16], BF16)
                        nBbib = sb(st, "nBbib", [128, 32, 16], BF16)
                        psT1s = [pst(st, "psT1_%d" % i, [128, 8, 128], BF16) for i in range(2)]
                        tcnt = [0]

                        def next_pt():
                            i_ = tcnt[0] % 2
                            tcnt[0] += 1
                            return psT1s[i_], "psT1_%d" % i_
                        psS = [pst(st, "psS%d" % i, [128, 2, NCE], F32) for i in range(2)]
                        psKs = [pst(st, "psK%d" % i, [128, 4, 128], F32) for i in range(2)]
                        psY = [pst(st, "psYt%d" % i, [128, 16, 32], F32) for i in range(2)]
                        P.op("dve", lambda e: e.tensor_copy(out=Bbrb[:], in_=bbr[:]), reads=["bb"], writes=["Bbrb"])
                        P.op("dve", lambda e: e.tensor_scalar(out=nBbib[:], in0=bbi[:], scalar1=-1.0, scalar2=None, op0=ALU.mult), reads=["bb"], writes=["nBbib"])
                        yc = 0
                        sc_ = [0]
                        ycc = [0]

                        def prep_up(j, sp_list):
                            g0 = 8 * j
                            ukeys = ["uT%d_%d" % (j, kb) for kb in range(9)]
                            upkeys = ["Up%d" % sp for sp in range(5)]
                            abkeys = ["ABr%d" % q for q in range(16)] + ["ABi%d" % q for q in range(16)]
                            cakeys = ["CAr%d" % q for q in range(16)] + ["CAi%d" % q for q in range(16)]
                            for sp in sp_list:
                                ncol = 128 if sp < 4 else 32

                                psT1, ptk = next_pt()

                                def tr1(e, sp=sp, ncol=ncol, j=j, psT1=psT1):
                                    ins = None
                                    for s8 in range(8):
                                        ins = e.transpose(out=psT1[0:ncol, s8, :], in_=uT[:, j, 1024 * sp + s8:1024 * sp + 8 * ncol:8], identity=identb[:])
                                    return ins
                                P.op("pe", tr1, reads=ukeys, writes=[ptk])
                                P.op("act", lambda e, sp=sp, ncol=ncol, psT1=psT1: e.activation(out=Zc[0:ncol, 0, :, :, :].rearrange("p g s h -> p s g h"),
                                                                                   in_=psT1[0:ncol].rearrange("p s (g h) -> p s g h", h=16), func=AF.Copy),
                                     reads=[ptk], writes=["Zc0"])
                                psT1b, ptkb = next_pt()

                                def tr2(e, sp=sp, ncol=ncol, psT1=psT1b):
                                    ins = None
                                    for gl in range(8):
                                        ins = e.transpose(out=psT1[:, gl, 0:ncol], in_=Zc[0:ncol, 0, gl, :, :].rearrange("p s h -> p (s h)"), identity=identb[0:ncol, 0:ncol])
                                    return ins
                                P.op("pe", tr2, reads=["Zc0"], writes=[ptkb])
                                P.op("dve", lambda e, sp=sp, ncol=ncol, psT1=psT1b: e.tensor_copy(out=Up[:, :, 128 * sp:128 * sp + ncol], in_=psT1[:, :, 0:ncol]),
                                     reads=[ptkb], writes=["Up%d" % sp])

                        def prep_ab(j, sh_list, which="all"):
                            g0 = 8 * j
                            ukeys = ["uT%d_%d" % (j, kb) for kb in range(9)]
                            upkeys = ["Up%d" % sp for sp in range(5)]
                            abkeys = ["ABr%d" % q for q in range(16)] + ["ABi%d" % q for q in range(16)]
                            cakeys = ["CAr%d" % q for q in range(16)] + ["CAi%d" % q for q in range(16)]
                            for sh in sh_list:
                                s0 = sh * 2
                                bsh = [128, 8, 2, 16]
                                p1r = pw_r[:, g0:g0 + 8, s0:s0 + 2].unsqueeze(3).to_broadcast(bsh)
                                p1i = pw_i[:, g0:g0 + 8, s0:s0 + 2].unsqueeze(3).to_broadcast(bsh)
                                ptr = pw_r[:, g0:g0 + 8, 66 + s0:66 + s0 + 2].unsqueeze(3).to_broadcast(bsh)
                                pti = pw_i[:, g0:g0 + 8, 66 + s0:66 + s0 + 2].unsqueeze(3).to_broadcast(bsh)
                                br_ = bbr[:, g0:g0 + 8, :].unsqueeze(2).to_broadcast(bsh)
                                bi_ = bbi[:, g0:g0 + 8, :].unsqueeze(2).to_broadcast(bsh)
                                cr_ = Cr[:, g0:g0 + 8, :].unsqueeze(2).to_broadcast(bsh)
                                ci_ = Ci[:, g0:g0 + 8, :].unsqueeze(2).to_broadcast(bsh)

                                def cplx(eng, ta, tb, kA, kB, a1, b1, a2, b2, op, dst, dkey):
                                    P.op(eng, lambda e: e.tensor_tensor(out=ta[:], in0=a1, in1=b1, op=ALU.mult), reads=["pw", "bb"], writes=[kA])
                                    P.op(eng, lambda e: e.tensor_tensor(out=tb[:], in0=a2, in1=b2, op=ALU.mult), reads=["pw", "bb"], writes=[kB])
                                    P.op(eng, lambda e: e.tensor_tensor(out=dst, in0=ta[:], in1=tb[:], op=op), reads=[kA, kB], writes=[dkey])
                                if which in ("all", "ab"):
                                    cplx("dve", tA[0], tA[1], "tA0", "tA1", p1r, br_, p1i, bi_, ALU.subtract, ABr[:, :, s0:s0 + 2, :], "ABr%d" % sh)
                                    cplx("pool", tB[0], tB[1], "tB0", "tB1", p1r, bi_, p1i, br_, ALU.add, ABi[:, :, s0:s0 + 2, :], "ABi%d" % sh)
                                if which in ("all", "ca"):
                                    cplx("dve", tA[0], tA[1], "tA0", "tA1", ptr, cr_, pti, ci_, ALU.subtract, CAr[:, :, s0:s0 + 2, :], "CAr%d" % sh)
                                    cplx("pool", tB[0], tB[1], "tB0", "tB1", pti, cr_, ptr, ci_, ALU.add, CAi[:, :, s0:s0 + 2, :], "CAi%d" % sh)

                        def main_pre(j):
                            g0 = 8 * j
                            ukeys = ["uT%d_%d" % (j, kb) for kb in range(9)]
                            upkeys = ["Up%d" % sp for sp in range(5)]
                            abkeys = ["ABr%d" % q for q in range(16)] + ["ABi%d" % q for q in range(16)]
                            cakeys = ["CAr%d" % q for q in range(16)] + ["CAi%d" % q for q in range(16)]
                            for gl in range(8):
                                psT1, ptk = next_pt()

                                def trw(e, gl=gl, psT1=psT1):
                                    ins = None
                                    for q in range(4):
                                        for ri, AB in enumerate((ABr, ABi)):
                                            ins = e.transpose(out=psT1[:, q * 2 + ri, :], in_=AB[:, gl, 8 * q:8 * q + 8, :].rearrange("p s h -> p (s h)"), identity=identb[:])
                                    return ins
                                P.op("pe", trw, reads=abkeys, writes=[ptk])

                                def evw(e, gl=gl, psT1=psT1):
                                    o_ = Wp[:, gl, :, :, :].rearrange("p q r m -> p (q r m)")
                                    i_ = psT1[:].rearrange("p a m -> p (a m)")
                                    e.activation(out=o_[:, 0:512], in_=i_[:, 0:512], func=AF.Copy)
                                    return e.activation(out=o_[:, 512:1024], in_=i_[:, 512:1024], func=AF.Copy)
                                P.op("act", evw, reads=[ptk], writes=["Wp%d" % gl])
                            for gl in range(8):
                                for ri in range(2):
                                    ps_, psk = psS[sc_[0] % 2], "psS%d" % (sc_[0] % 2)
                                    sc_[0] += 1

                                    def mms(e, gl=gl, ri=ri, ps_=ps_):
                                        ins = None
                                        for q in range(4):
                                            ins = e.matmul(ps_[:, 0, :], lhsT=Wp[:, gl, q, ri, :], rhs=Up[:, gl, q:LE // 8:4], start=(q == 0), stop=(q == 3))
                                        return ins
                                    P.op("pe", mms, reads=["Wp%d" % gl] + upkeys, writes=[psk])
                                    P.op("act", lambda e, gl=gl, ri=ri, ps_=ps_, g0=g0: e.activation(out=Sall[:, g0 + gl, ri, :], in_=ps_[:, 0, :], func=AF.Copy),
                                         reads=[psk], writes=["Sall"])
                            for dr in range(2):
                                rows = slice(64 * dr, 64 * dr + 64)
                                for t4 in range(8):
                                    psK, pkk = psKs[(dr * 8 + t4) % 2], "psK%d" % ((dr * 8 + t4) % 2)

                                    def mmk(e, rows=rows, t4=t4, g0=g0, psK=psK):
                                        ins = None
                                        for q in range(4):
                                            tau = t4 * 4 + q
                                            o_ = psK[:, q, :].rearrange("p (g h) -> p g h", h=16)
                                            e.matmul(o_, lhsT=Bbrb[rows, g0:g0 + 8, :].rearrange("p g h -> p (g h)"), rhs=CAr[rows, :, tau, :], start=True, stop=False)
                                            ins = e.matmul(o_, lhsT=nBbib[rows, g0:g0 + 8, :].rearrange("p g h -> p (g h)"), rhs=CAi[rows, :, tau, :], start=False, stop=True)
                                        return ins
                                    P.op("pe", mmk, reads=cakeys + ["Bbrb", "nBbib"], writes=[pkk])
                                    P.op("dve", lambda e, dr=dr, t4=t4, psK=psK: e.tensor_tensor(out=W2[:, dr, t4 * 4:(t4 + 1) * 4, :], in0=psK[:], in1=bdmask.unsqueeze(1).to_broadcast([128, 4, 128]), op=ALU.mult),
                                         reads=[pkk, "cst"], writes=["W2_%d_%d" % (dr, t4)])

                        def taps(j, kb):
                            g0 = 8 * j
                            ukeys = ["uT%d_%d" % (j, kb) for kb in range(9)]
                            upkeys = ["Up%d" % sp for sp in range(5)]
                            abkeys = ["ABr%d" % q for q in range(16)] + ["ABi%d" % q for q in range(16)]
                            cakeys = ["CAr%d" % q for q in range(16)] + ["CAi%d" % q for q in range(16)]
                            if True:
                                py, pyk = psY[ycc[0] % 2], "psYt%d" % (ycc[0] % 2)
                                ycc[0] += 1
                                u3 = uT[:, j, kb * 512:(kb + 1) * 512].rearrange("p (c r) -> p c r", r=32)

                                def mmt(e, py=py, u3=u3):
                                    ins = None
                                    first = True
                                    for dr in range(2):
                                        for tau in range(32):
                                            if dr == 0:
                                                o_ = py[:, :, tau:32]
                                                r_ = u3[:, :, 0:32 - tau]
                                            else:
                                                o_ = py[:, :, 0:32 - tau]
                                                r_ = u3[:, :, tau:32]
                                            ins = e.matmul(o_, lhsT=W2[:, dr, tau, :], rhs=r_, start=first, stop=(dr == 1 and tau == 31))
                                            first = False
                                    return ins
                                ukey = "uT%d_%d" % (j, kb)
                                P.op("pe", mmt, reads=["W2_%d_%d" % (d_, t_) for d_ in range(2) for t_ in range(8)] + [ukey], writes=[pyk])
                                P.op("dve", lambda e, py=py, u3=u3, j=j: e.scalar_tensor_tensor(out=u3, in0=u3, scalar=dsk[:, j:j + 1], in1=py[:], op0=ALU.mult, op1=ALU.add),
                                     reads=[pyk, "dsk"], writes=[ukey])

                        prep_ab(0, range(16), "ab")
                        prep_up(0, range(5))
                        prep_ab(0, range(16), "ca")
                        for j in range(4):
                            if j > 0:
                                prep_up(j, range(5))
                            main_pre(j)
                            for kb in range(8):
                                taps(j, kb)
                                if j + 1 < 4:
                                    prep_ab(j + 1, [2 * kb, 2 * kb + 1])
                    P.barrier()
                    if stop == "B2":
                        return finish([("uT", uT[:], [128, 4, LE], BF16), ("Sall", Sall[:], [128, 32, 2, NCE], BF16)])
                    with contextlib.ExitStack() as st:
                        E32 = sb(st, "E32", [128, 2, 32, NCH + 1], F32)
                        Ec32 = sb(st, "Ec32", [128, 2, 32, NCC + 1], F32)
                        sp1 = sb(st, "sp1", [128, 2, 32], F32)
                        sp2 = sb(st, "sp2", [128, 2, 32], F32)
                        sq_ = sb(st, "sq_", [128, 2, 32], F32)
                        def scan_steps():
                            steps = []
                            Sv = lambda rows, c: Sall[rows, :, :, c].rearrange("p g r -> p r g")
                            for half, eng in ((0, "dve"), (1, "pool")):
                                rows = slice(64 * half, 64 * half + 64)
                                hk = "scan%d" % half
                                seq = []
                                c_init = 0 if half == 0 else NCC
                                seq.append(("init", None))
                                order_c = list(range(NCC)) if half == 0 else list(range(NCC - 1, -1, -1))
                                for c in order_c:
                                    seq.append(("ctx", c))
                                seq.append(("seed", None))
                                order_m = list(range(NCH)) if half == 0 else list(range(NCH - 1, -1, -1))
                                for c in order_m:
                                    seq.append(("main", c))
                                steps.append((half, eng, rows, hk, seq))
                            return steps

                        def emit_scan_step(half, eng, rows, hk, item):
                            kind, c = item
                            if kind == "init":
                                ci = 0 if half == 0 else NCC
                                P.op(eng, lambda e: e.memset(Ec32[rows, :, :, ci], 0.0), reads=[], writes=[hk + "X"])
                                return
                            if kind == "seed":
                                src = Ec32[rows, :, :, NCC] if half == 0 else Ec32[rows, :, :, 0]
                                dst = E32[rows, :, :, 0] if half == 0 else E32[rows, :, :, NCH]
                                P.op(eng, lambda e: e.tensor_copy(out=dst, in_=src), reads=[hk + "X"], writes=[hk + "X"])
                                return
                            Ebuf = Ec32 if kind == "ctx" else E32
                            scol = NCH + c if kind == "ctx" else c
                            if half == 0:
                                cin, cout = c, c + 1
                            else:
                                cin, cout = c + 1, c
                            X = Ebuf[rows, :, :, cin]
                            Xo = Ebuf[rows, :, :, cout]
                            S_ = Sall[rows, :, :, scol].rearrange("p g r -> p r g")
                            P.op(eng, lambda e: e.tensor_tensor(out=sp1[rows], in0=X, in1=CA[rows], op=ALU.mult), reads=[hk + "X", "CA"], writes=[hk + "p1"])
                            P.op(eng, lambda e: e.tensor_tensor(out=sp2[rows, 0, :], in0=Ebuf[rows, 1, :, cin], in1=CBn[rows], op=ALU.mult), reads=[hk + "X", "CB"], writes=[hk + "p2a"])
                            P.op(eng, lambda e: e.tensor_tensor(out=sp2[rows, 1, :], in0=Ebuf[rows, 0, :, cin], in1=CBp[rows], op=ALU.mult), reads=[hk + "X", "CB"], writes=[hk + "p2b"])
                            P.op(eng, lambda e: e.tensor_tensor(out=sq_[rows], in0=sp1[rows], in1=S_, op=ALU.add), reads=[hk + "p1", "Sall"], writes=[hk + "q"])
                            P.op(eng, lambda e: e.tensor_tensor(out=Xo, in0=sq_[rows], in1=sp2[rows], op=ALU.add), reads=[hk + "q", hk + "p2a", hk + "p2b"], writes=[hk + "X"])

                        steps = scan_steps()
                        pos = [0, 0]

                        def advance_scan(n):
                            for (half, eng, rows, hk, seq) in steps:
                                for _ in range(n):
                                    if pos[half] < len(seq):
                                        emit_scan_step(half, eng, rows, hk, seq[pos[half]])
                                        pos[half] += 1

                        cvb = [sb(st, "cvb%d" % i, [128, 4, 512], BF16) for i in range(2)]
                        csq = [sb(st, "csq%d" % i, [128, 512], BF16) for i in range(1)]
                        mean = sb(st, "mean", [128, 512], F32)
                        rsd = sb(st, "rsd", [128, 512], F32)
                        ctm = [sb(st, "ctm%d" % i, [128, 512], F32) for i in range(1)]
                        cob = [sb(st, "cob%d" % i, [128, 512], BF16) for i in range(1)]
                        DgAll = sb(st, "DgAll", [128, 4, 31, 128], BF16)
                        psC = [pst(st, "psC%d" % i, [128, 512], F32) for i in range(2)]
                        psM = [pst(st, "psM%d" % i, [128, 512], F32) for i in range(2)]
                        psQ = [pst(st, "psQ%d" % i, [128, 512], F32) for i in range(2)]
                        for j in range(4):
                            P.op("pool", lambda e, j=j: e.tensor_tensor(out=DgAll[:, j, :, :], in0=ident.unsqueeze(1).to_broadcast([128, 31, 128]),
                                                                        in1=cw[:, j, :].unsqueeze(2).to_broadcast([128, 31, 128]), op=ALU.mult),
                                 reads=["cst", "cw"], writes=["Dg%d" % j])
                        cc = [0]

                        def conv_X(kb):
                            cvt, cvk = cvb[kb % 2], "cvb%d" % (kb % 2)
                            pm, pmk = psM[kb % 2], "psM%d" % (kb % 2)
                            pq, pqk = psQ[kb % 2], "psQ%d" % (kb % 2)

                            def stats(j):
                                cs_, csk = csq[0], "csq0"
                                P.op("pe", lambda e: e.matmul(pm[:], lhsT=onesb[:], rhs=cvt[:, j, :], start=(j == 0), stop=(j == 3)),
                                     reads=[cvk + "_%d" % j, "onesb"], writes=[pmk])
                                P.op("pe", lambda e: e.matmul(pq[:], lhsT=onesb[:], rhs=cs_[:], start=(j == 0), stop=(j == 3)),
                                     reads=[csk, "onesb"], writes=[pqk])

                            def mm_part(j):
                                pc, pck = psC[cc[0] % 2], "psC%d" % (cc[0] % 2)
                                cc[0] += 1

                                def mmc(e):
                                    ins = None
                                    taps = [15] + [k for k in range(31) if k != 15]
                                    todo = []
                                    for k in taps:
                                        dl = 64 * (k - 15)
                                        lo = max(512 * kb, -dl)
                                        hi = min(512 * kb + 512, L - dl)
                                        if lo < hi:
                                            todo.append((k, lo, hi, dl))
                                    for n_, (k, lo, hi, dl) in enumerate(todo):
                                        ins = e.matmul(pc[:, lo - 512 * kb:hi - 512 * kb], lhsT=DgAll[:, j, k, :], rhs=hcT[:, j, lo + dl:hi + dl],
                                                       start=(n_ == 0), stop=(n_ == len(todo) - 1))
                                    return ins
                                P.op("pe", mmc, reads=["Dg%d" % j], writes=[pck])
                                return pc, pck

                            def act_part(j, pc, pck):
                                P.op("act", lambda e: e.activation(out=cvt[:, j, :], in_=pc[:], func=AF.Identity, bias=cb[:, j:j + 1], scale=1.0),
                                     reads=[pck], writes=[cvk + "_%d" % j])
                                cs_, csk = csq[0], "csq0"
                                P.op("act", lambda e: e.activation(out=cs_[:], in_=cvt[:, j, :], func=AF.Square), reads=[cvk + "_%d" % j], writes=[csk])
                            for j in range(4):
                                pc, pck = mm_part(j)
                                if j >= 1:
                                    stats(j - 1)
                                act_part(j, pc, pck)
                                advance_scan(3)
                            stats(3)

                        def conv_Y(kb):
                            cvt, cvk = cvb[kb % 2], "cvb%d" % (kb % 2)
                            pm, pmk = psM[kb % 2], "psM%d" % (kb % 2)
                            pq, pqk = psQ[kb % 2], "psQ%d" % (kb % 2)
                            P.op("dve", lambda e: e.tensor_scalar(out=mean[:], in0=pm[:], scalar1=1.0 / 512.0, scalar2=None, op0=ALU.mult), reads=[pmk], writes=["mean"])
                            P.op("dve", lambda e: e.tensor_tensor(out=rsd[:], in0=mean[:], in1=mean[:], op=ALU.mult), reads=["mean"], writes=["rsd"])
                            P.op("dve", lambda e: e.scalar_tensor_tensor(out=rsd[:], in0=pq[:], scalar=1.0 / 512.0, in1=rsd[:], op0=ALU.mult, op1=ALU.subtract),
                                 reads=[pqk, "rsd"], writes=["rsd"])
                            P.op("act", lambda e: e.activation(out=rsd[:], in_=rsd[:], func=AF.Sqrt, bias=cst[:, C_MK + 4:C_MK + 5], scale=1.0), reads=["rsd"], writes=["rsd"])
                            P.op("dve", lambda e: e.reciprocal(out=rsd[:], in_=rsd[:]), reads=["rsd"], writes=["rsd"])

                            def ln(j):
                                ct_, ctk = ctm[0], "ctm0"
                                co_, cok = cob[0], "cob0"
                                P.op("dve", lambda e: e.tensor_tensor(out=ct_[:], in0=cvt[:, j, :], in1=mean[:], op=ALU.subtract),
                                     reads=[cvk + "_%d" % j, "mean"], writes=[ctk])
                                P.op("dve", lambda e: e.tensor_tensor(out=ct_[:], in0=ct_[:], in1=rsd[:], op=ALU.mult), reads=[ctk, "rsd"], writes=[ctk])
                                P.op("act", lambda e: e.activation(out=co_[:], in_=ct_[:], func=AF.Silu, bias=lnb[:, j:j + 1], scale=lng[:, j:j + 1]),
                                     reads=[ctk], writes=[cok])
                                P.dma("sp", mix_d[512 + j * 128:512 + (j + 1) * 128, kb * 512:(kb + 1) * 512], co_[:], reads=[cok], writes=["mixd_c%d_%d" % (j, kb)])
                            for j in range(4):
                                ln(j)
                                advance_scan(3)

                        conv_X(0)
                        for kb in range(8):
                            if kb + 1 < 8:
                                conv_X(kb + 1)
                            conv_Y(kb)
                        advance_scan(10000)
                        for ri in range(2):
                            P.op("act", lambda e, ri=ri: e.activation(out=Ebf[0:64, :, ri, 0:NCH], in_=E32[0:64, ri, :, 0:NCH], func=AF.Copy), reads=["scan0X"], writes=["Sall"])
                            P.op("act", lambda e, ri=ri: e.activation(out=Ebf[64:128, :, ri, 0:NCH], in_=E32[64:128, ri, :, 1:NCH + 1], func=AF.Copy), reads=["scan1X"], writes=["Sall"])
                    P.barrier()
                    if stop == "B3":
                        return finish([("Sall", Sall[:], [128, 32, 2, NCE], BF16)])
                with contextlib.ExitStack() as st:
                    W3 = [sb(st, "W3_%d" % i, [128, 8, 2, 32, 16], BF16) for i in range(2)]
                    wglu = sb(st, "wglu", [128, 4, 512], BF16)
                    P.dma("pool", wglu[:], wglu_d.rearrange("(k p) n -> p k n", p=128), writes=["wglu"])
                    wa = [sb(st, "wa%d" % i, [128, 4, 32, 16], F32) for i in range(2)]
                    wb_ = [sb(st, "wb%d" % i, [128, 4, 32, 16], F32) for i in range(2)]
                    Yc = [sb(st, "Yc%d" % i, [128, 32, 8, 16], BF16) for i in range(2)]
                    ytm = [sb(st, "ytm%d" % i, [128, 8, 128], F32) for i in range(2)]
                    sgl = [sb(st, "sgl%d" % i, [128, 512], BF16) for i in range(2)]
                    msb = [sb(st, "msb%d" % i, [128, 512], BF16) for i in range(2)]
                    psY = [pst(st, "psYr%d" % i, [128, 512], F32) for i in range(2)]
                    psT = [pst(st, "psTr%d" % i, [128, 8, 128], BF16) for i in range(2)]
                    psL = [pst(st, "psL%d" % i, [128, 512], F32) for i in range(2)]
                    yc = 0
                    tcn = 0
                    def build_w3(j):
                        g0 = 8 * j
                        W3j, w3k = W3[j % 2], "W3_%d" % (j % 2)
                        for ri in range(2):
                            for hf in range(2):
                                eng = "pool" if (ri == 1 and hf == 1) else "dve"
                                t1, t2 = (wb_ if eng == "pool" else wa)
                                k1, k2 = ("wb0", "wb1") if eng == "pool" else ("wa0", "wa1")
                                gs = slice(g0 + 4 * hf, g0 + 4 * hf + 4)
                                p3r = pw_r[:, gs, 32:64].unsqueeze(3).to_broadcast([128, 4, 32, 16])
                                p3i = pw_i[:, gs, 32:64].unsqueeze(3).to_broadcast([128, 4, 32, 16])
                                if ri == 0:
                                    c1 = Cr[:, gs, :].unsqueeze(2).to_broadcast([128, 4, 32, 16])
                                    c2 = nCi[:, gs, :].unsqueeze(2).to_broadcast([128, 4, 32, 16])
                                    pa_, pb_ = p3r, p3i
                                else:
                                    c1 = nCr[:, gs, :].unsqueeze(2).to_broadcast([128, 4, 32, 16])
                                    c2 = nCi[:, gs, :].unsqueeze(2).to_broadcast([128, 4, 32, 16])
                                    pa_, pb_ = p3i, p3r
                                P.op(eng, lambda e, t1=t1, c1=c1, pa_=pa_: e.tensor_tensor(out=t1[:], in0=c1, in1=pa_, op=ALU.mult), reads=["pw", "C"], writes=[k1])
                                P.op(eng, lambda e, t2=t2, c2=c2, pb_=pb_: e.tensor_tensor(out=t2[:], in0=c2, in1=pb_, op=ALU.mult), reads=["pw", "C"], writes=[k2])
                                P.op(eng, lambda e, t1=t1, t2=t2, W3j=W3j, hf=hf, ri=ri: e.tensor_tensor(out=W3j[:, 4 * hf:4 * hf + 4, ri, :, :], in0=t1[:], in1=t2[:], op=ALU.add),
                                     reads=[k1, k2], writes=[w3k + "_%d%d" % (ri, hf)])

                    build_w3(0)
                    for j in range(4):
                        g0 = 8 * j
                        W3j, w3k = W3[j % 2], "W3_%d" % (j % 2)
                        Ycj, yck = Yc[j % 2], "Yc%d" % (j % 2)
                        if j + 1 < 4:
                            build_w3(j + 1)
                        w3keys = [w3k + "_%d%d" % (ri, hf) for ri in range(2) for hf in range(2)]
                        for gl in range(8):
                            py, pyk = psY[yc % 2], "psYr%d" % (yc % 2)
                            yc += 1

                            def mmr(e, py=py, g=g0 + gl, gl=gl, W3j=W3j):
                                e.matmul(py[:], lhsT=Ebf[:, g, 0, 0:NCH], rhs=W3j[:, gl, 0, :, :].rearrange("p r h -> p (r h)"), start=True, stop=False)
                                return e.matmul(py[:], lhsT=Ebf[:, g, 1, 0:NCH], rhs=W3j[:, gl, 1, :, :].rearrange("p r h -> p (r h)"), start=False, stop=True)
                            P.op("pe", mmr, reads=["Ebf"] + w3keys, writes=[pyk])
                            P.op("act", lambda e, py=py, Ycj=Ycj, gl=gl: e.activation(out=Ycj[:, :, gl, :], in_=py[:].rearrange("p (r h) -> p r h", h=16), func=AF.Copy), reads=[pyk], writes=[yck + "_%d" % gl])
                        ykeys = [yck + "_%d" % gl for gl in range(8)]
                        uv = uT[:, j, 0:L].rearrange("p (c r) -> p r c", r=32)
                        for r8 in range(4):
                            pt, ptk = psT[tcn % 2], "psTr%d" % (tcn % 2)
                            ym, ymk = ytm[tcn % 2], "ytm%d" % (tcn % 2)
                            tcn += 1

                            def trr(e, pt=pt, Ycj=Ycj, r8=r8):
                                ins = None
                                for rr in range(8):
                                    r = r8 * 8 + rr
                                    ins = e.transpose(out=pt[:, rr, :], in_=Ycj[:, r, :, :].rearrange("p g h -> p (g h)"), identity=identb[:])
                                return ins
                            P.op("pe", trr, reads=ykeys + ["identb"], writes=[ptk])
                            ukeys = ["uT%d_%d" % (j, kb) for kb in range(8)]
                            P.op("dve", lambda e, pt=pt, ym=ym, uv=uv, r8=r8: e.tensor_tensor(out=ym[:], in0=pt[:], in1=uv[:, r8 * 8:(r8 + 1) * 8, :], op=ALU.add),
                                 reads=[ptk] + ukeys, writes=[ymk])
                            P.op("act", lambda e, ym=ym, uv=uv, r8=r8: e.activation(out=uv[:, r8 * 8:(r8 + 1) * 8, :], in_=ym[:], func=AF.Gelu_apprx_tanh),
                                 reads=[ymk], writes=ukeys)
                    lc = 0
                    for kb in range(8):
                        for jo in range(4):
                            pl, plk = psL[lc % 2], "psL%d" % (lc % 2)
                            sg, sgk = sgl[lc % 2], "sgl%d" % (lc % 2)
                            ms, msk = msb[lc % 2], "msb%d" % (lc % 2)
                            lc += 1

                            def mml(e, pl=pl, jo=jo, kb=kb):
                                ins = None
                                for k in range(4):
                                    ins = e.matmul(pl[:], lhsT=wglu[:, k, jo * 128:(jo + 1) * 128], rhs=uT[:, k, kb * 512:(kb + 1) * 512], start=(k == 0), stop=(k == 3))
                                return ins
                            P.op("pe", mml, reads=["wglu"] + ["uT%d_%d" % (k, kb) for k in range(4)], writes=[plk])
                            P.op("act", lambda e, pl=pl, sg=sg: e.activation(out=sg[:], in_=pl[:], func=AF.Sigmoid), reads=[plk], writes=[sgk])
                            P.op("dve", lambda e, sg=sg, ms=ms, jo=jo, kb=kb: e.tensor_tensor(out=ms[:], in0=uT[:, jo, kb * 512:(kb + 1) * 512], in1=sg[:], op=ALU.mult),
                                 reads=[sgk, "uT%d_%d" % (jo, kb)], writes=[msk])
                            P.dma("sp", mix_d[jo * 128:(jo + 1) * 128, kb * 512:(kb + 1) * 512], ms[:], reads=[msk], writes=["mixd_s%d_%d" % (jo, kb)])
                P.barrier()
        if stop == "B4":
            return finish([("modT", modT[:], [128, 48, 2], F32)])
        with contextlib.ExitStack() as st:
            w1b = sb(st, "w1b", [128, 8, 4 * D], BF16)
            G1b = sb(st, "G1b", [128, D], F32)
            G2b = sb(st, "G2b", [128, D], F32)
            FGb = sb(st, "FGb", [128, D], F32)
            dg = [sb(st, "dg%d" % i, [128, 128], F32) for i in range(2)]
            if stop != "C00b":
                P.dma("sp", FGb[:], fg_d[:, :], writes=["FGb"])
            wout = sb(st, "wout", [128, 8, D], BF16)
            if stop != "C00a":
                P.dma("pool", wout[:], wout_d.rearrange("(k p) n -> p k n", p=128), writes=["wout"])
            w2b = sb(st, "w2b", [128, 32, D], BF16)
            mixb = [sb(st, "mixb%d" % i, [128, 8, 256], BF16) for i in range(2)]
            xh = [sb(st, "xh%d" % i, [128, D], F32) for i in range(4)]
            tmpG = [sb(st, "tmpG%d" % i, [128, 512], F32) for i in range(1)]
            xs = [sb(st, "xsc%d" % i, [128, D], BF16) for i in range(1)]
            junkc = xs[0]
            a2T = [sb(st, "a2T%d" % i, [128, 8, 256], BF16) for i in range(2)]
            ssq = [sb(st, "ssc%d" % i, [128, 8], F32) for i in range(4)]
            rT = [sb(st, "rT%d" % i, [128, 256], BF16) for i in range(3)]
            hT = [sb(st, "hT%d" % i, [128, 256], BF16) for i in range(3)]
            h2 = [sb(st, "h2_%d" % i, [128, D], F32) for i in range(2)]
            pso = [pst(st, "pso%d" % i, [128, 512], F32) for i in range(4)]
            psf = [pst(st, "psf%d" % i, [128, 256], F32) for i in range(3)]
            psx = pst(st, "psx", [128, 512], F32)
            psxT = psx.bitcast(BF16).rearrange("p (a m) -> p a m", m=128)

            if stop in ("C00", "C00a", "C00b"):
                return finish([("modT", modT[:], [128, 48, 2], F32)])
            for gi, (base, Gb) in enumerate(((16, G1b), (40, G2b))):
                for j in range(8):
                    d_ = dg[j % 2]
                    dk = "dg%d" % (j % 2)
                    P.op("dve", lambda e, d_=d_, base=base, j=j: e.tensor_scalar(out=d_[:], in0=ident, scalar1=modT[:, base + j, 0:1], scalar2=None, op0=ALU.mult),
                         reads=["modT", "cst"], writes=[dk])
                    pg = pso[gi * 2 + j // 4]
                    pk = "pso%d" % (gi * 2 + j // 4)
                    P.op("pe", lambda e, pg=pg, d_=d_, j=j: e.matmul(pg[:, (j % 4) * 128:(j % 4 + 1) * 128], lhsT=onesf, rhs=d_[:], start=True, stop=True),
                         reads=[dk, "cst"], writes=[pk + "_%d" % (j % 4)])
                for h in range(2):
                    pg = pso[gi * 2 + h]
                    pk = "pso%d" % (gi * 2 + h)
                    P.op("act", lambda e, pg=pg, Gb=Gb, h=h: e.activation(out=Gb[:, h * 512:(h + 1) * 512], in_=pg[:], func=AF.Copy),
                         reads=[pk + "_%d" % q for q in range(4)], writes=["G%d" % gi])
            P.barrier()
            if stop == "C0":
                return finish([("G1b", G1b[:], [128, D], F32)])
            w1v = w1_d.rearrange("(k p) n -> p k n", p=128)
            for k in range(8):
                P.dma("pool", w1b[:, k, :], w1v[:, k, :], writes=["w1b%d" % k])
            w2v = w2_d.rearrange("(f p) n -> p f n", p=128)
            for f4 in range(8):
                P.dma("pool", w2b[:, f4 * 4:(f4 + 1) * 4, :], w2v[:, f4 * 4:(f4 + 1) * 4, :], writes=["w2b%d" % f4])
            w1keys = ["w1b%d" % k for k in range(8)]
            finals = []
            NBLK = L // 256
            mixv = mix_d.rearrange("(k p) t -> p k t", p=128)

            def c_load(blk):
                t0 = blk * 256
                mb, mbk = mixb[blk % 2], "mixb%d" % (blk % 2)
                P.dma("sp", mb[:], mixv[:, :, t0:t0 + 256], writes=[mbk])
                for tt in range(2):
                    xi = (blk * 2 + tt) % 4
                    P.dma("sp", xh[xi][:], x_d[t0 + tt * 128:t0 + (tt + 1) * 128, :], writes=["xh%d" % xi])

            def c_wout(blk, tt, hf):
                mb, mbk = mixb[blk % 2], "mixb%d" % (blk % 2)
                xi = (blk * 2 + tt) % 4
                xb, xk = xh[xi], "xh%d" % xi
                cs = slice(hf * 512, (hf + 1) * 512)
                tg, tgk = tmpG[0], "tmpG0"

                def mmo(e):
                    ins = None
                    for k in range(8):
                        ins = e.matmul(psx[:], lhsT=mb[:, k, tt * 128:(tt + 1) * 128], rhs=wout[:, k, cs], start=(k == 0), stop=(k == 7))
                    return ins
                P.op("pe", mmo, reads=[mbk, "wout"], writes=["psx"])
                P.op("dve", lambda e: e.tensor_tensor(out=tg[:], in0=psx[:], in1=G1b[:, cs], op=ALU.mult), reads=["psx"], writes=[tgk])
                P.op("dve", lambda e: e.tensor_tensor(out=xb[:, cs], in0=xb[:, cs], in1=tg[:], op=ALU.add), reads=[tgk, xk], writes=[xk])

            def c_rms_act(blk, tt):
                xi = (blk * 2 + tt) % 4
                xb, xk = xh[xi], "xh%d" % xi
                sq, sk = ssq[xi], "ssc%d" % xi
                P.op("act", lambda e: e.activation(out=junkc[:], in_=xb[:], func=AF.Square, accum_out=sq[:, 0:1]), reads=[xk], writes=["xsc0", sk])
                P.op("act", lambda e: e.activation(out=sq[:, 2:3], in_=sq[:, 0:1], func=AF.Sqrt, bias=cst[:, C_MK + 3:C_MK + 4], scale=1.0 / D), reads=[sk], writes=[sk])

            def c_rms_dve(blk, tt):
                xi = (blk * 2 + tt) % 4
                xb, xk = xh[xi], "xh%d" % xi
                sq, sk = ssq[xi], "ssc%d" % xi
                xsb, xsk = xs[0], "xsc0"
                P.op("dve", lambda e: e.reciprocal(out=sq[:, 3:4], in_=sq[:, 2:3]), reads=[sk], writes=[sk])
                P.op("dve", lambda e: e.tensor_scalar(out=xsb[:], in0=xb[:], scalar1=sq[:, 3:4], scalar2=None, op0=ALU.mult), reads=[xk, sk], writes=[xsk])

            def c_tr(blk, tt):
                xsb, xsk = xs[0], "xsc0"

                def tr(e):
                    ins = None
                    for j in range(8):
                        ins = e.transpose(out=psxT[:, j, :], in_=xsb[:, j * 128:(j + 1) * 128], identity=identb[:])
                    return ins
                P.op("pe", tr, reads=[xsk], writes=["psx"])

            def c_ev(blk, tt):
                a2, a2k = a2T[blk % 2], "a2T%d" % (blk % 2)

                def ev(e):
                    ins = None
                    for j in range(8):
                        ins = e.activation(out=a2[:, j, tt * 128:(tt + 1) * 128], in_=psxT[:, j, :], func=AF.Identity,
                                           bias=modT[:, 24 + j, 0:1], scale=s2[:, j:j + 1])
                    return ins
                P.op("act", ev, reads=["psx"], writes=[a2k + "_%d" % tt])

            def prologue_steps(blk):
                st_ = []
                for tt in range(2):
                    st_.append(lambda tt=tt: c_wout(blk, tt, 0))
                    st_.append(lambda tt=tt: c_wout(blk, tt, 1))
                    st_.append(lambda tt=tt: c_rms_act(blk, tt))
                    st_.append(lambda tt=tt: c_rms_dve(blk, tt))
                    st_.append(lambda tt=tt: c_tr(blk, tt))
                    st_.append(lambda tt=tt: c_ev(blk, tt))
                return st_

            fcn = [0]

            def emit_w1(blk, f):
                a2, a2k = a2T[blk % 2], "a2T%d" % (blk % 2)
                i_ = fcn[0] % 3
                fcn[0] += 1
                pf, pfk = psf[i_], "psf%d" % i_
                r_, rk = rT[i_], "rT%d" % i_
                h_, hk_ = hT[i_], "hT%d" % i_

                def mm1(e):
                    ins = None
                    for k in range(8):
                        ins = e.matmul(pf[:], lhsT=w1b[:, k, f * 128:(f + 1) * 128], rhs=a2[:, k, :], start=(k == 0), stop=(k == 7))
                    return ins
                P.op("pe", mm1, reads=[a2k + "_0", a2k + "_1"] + w1keys, writes=[pfk])
                P.op("act", lambda e: e.activation(out=r_[:], in_=pf[:], func=AF.Relu), reads=[pfk], writes=[rk])
                P.op("pool", lambda e: e.tensor_tensor(out=h_[:], in0=r_[:], in1=r_[:], op=ALU.mult), reads=[rk], writes=[hk_])
                return h_, hk_

            def emit_w2(f, h_, hk_):
                def mm2(e):
                    ins = None
                    for tt in range(2):
                        for hf in range(2):
                            ins = e.matmul(pso[tt * 2 + hf][:], lhsT=h_[:, tt * 128:(tt + 1) * 128], rhs=w2b[:, f, hf * 512:(hf + 1) * 512], start=(f == 0), stop=(f == 31))
                    return ins
                P.op("pe", mm2, reads=[hk_, "w2b%d" % (f // 4)], writes=["pso%d" % i for i in range(4)])

            def epilogue(blk):
                for tt in range(2):
                    epi_mult(blk, tt)
                for tt in range(2):
                    epi_tile(blk, tt)

            def epi_mult(blk, tt):
                h2b, h2k = h2[tt], "h2_%d" % tt
                for hf in range(2):
                    cs = slice(hf * 512, (hf + 1) * 512)
                    P.op("dve", lambda e, hf=hf, cs=cs: e.tensor_tensor(out=h2b[:, cs], in0=pso[tt * 2 + hf][:], in1=G2b[:, cs], op=ALU.mult),
                         reads=["pso%d" % (tt * 2 + hf)], writes=[h2k + "_%d" % hf])

            def epi_tile(blk, tt):
                t0 = blk * 256
                if True:
                    xi = (blk * 2 + tt) % 4
                    xb, xk = xh[xi], "xh%d" % xi
                    h2b, h2k = h2[tt], "h2_%d" % tt
                    sq, sk = ssq[xi], "ssc%d" % xi
                    for hf in range(2):
                        cs = slice(hf * 512, (hf + 1) * 512)
                        P.op("dve", lambda e, cs=cs: e.tensor_tensor(out=h2b[:, cs], in0=h2b[:, cs], in1=xb[:, cs], op=ALU.add),
                             reads=[xk, h2k + "_%d" % hf], writes=[h2k + "_%d" % hf])
                    hk3 = [h2k + "_0", h2k + "_1"]
                    P.op("act", lambda e: e.activation(out=junkc[:], in_=h2b[:], func=AF.Square, accum_out=sq[:, 4:5]), reads=hk3, writes=["xsc0", sk])
                    P.op("act", lambda e: e.activation(out=sq[:, 6:7], in_=sq[:, 4:5], func=AF.Sqrt, bias=cst[:, C_MK + 3:C_MK + 4], scale=1.0 / D), reads=[sk], writes=[sk])
                    P.op("dve", lambda e: e.reciprocal(out=sq[:, 7:8], in_=sq[:, 6:7]), reads=[sk], writes=[sk])
                    P.op("dve", lambda e: e.scalar_tensor_tensor(out=h2b[:], in0=h2b[:], scalar=sq[:, 7:8], in1=FGb[:], op0=ALU.mult, op1=ALU.mult),
                         reads=hk3 + [sk], writes=hk3)
                    finals.append(P.dma("sp", out_d[t0 + tt * 128:t0 + (tt + 1) * 128, :], h2b[:], reads=hk3, writes=["out%d_%d" % (blk, tt)]))

            c_load(0)
            for stp in prologue_steps(0):
                stp()
            for blk in range(NBLK):
                nxt_steps = []
                if blk + 1 < NBLK:
                    c_load(blk + 1)
                    nxt_steps = prologue_steps(blk + 1)
                sched = {}
                for n_, stp in enumerate(nxt_steps):
                    sched[4 + 2 * n_] = stp
                pend = [emit_w1(blk, 0), emit_w1(blk, 1)]
                for f in range(32):
                    if f + 2 < 32:
                        pend.append(emit_w1(blk, f + 2))
                    h_, hk_ = pend.pop(0)
                    emit_w2(f, h_, hk_)
                    if f in sched:
                        sched[f]()
                epilogue(blk)
            P.emit(finals)
    return nc


_PROG = None


def _consts():
    c = np.zeros((128, NCST), np.float32)
    c[:, C_ID:C_ID + 128] = np.eye(128, dtype=np.float32)
    c[:, C_ONE:C_ONE + 128] = 1.0
    p = np.arange(128)
    c[:, C_BD:C_BD + 128] = (p[:, None] // 16 == p[None, :] // 16).astype(np.float32)
    s = np.arange(32, dtype=np.float32)
    ke = np.zeros((128, 98), np.float32)
    ke[:, 66:98] = np.arange(32, dtype=np.float32)[None, :]
    ke[:64, 0:32] = 31.0 - s
    ke[64:, 0:32] = s
    ke[:64, 32:64] = s + 1.0
    ke[64:, 32:64] = 32.0 - s
    ke[:, 64] = 32.0
    ke[:, 65] = 1.0
    c[:, C_KE:C_KE + 98] = ke
    c[:, C_MK] = ((p // 16) % 2 == 0).astype(np.float32)
    c[:, C_MK + 1] = ((p // 16) % 2 == 1).astype(np.float32)
    c[:, C_MK + 2] = (p >= 96).astype(np.float32)
    c[:, C_MK + 3] = 1e-6
    c[:, C_MK + 4] = 1e-5
    return c


def kernel(x, c, ctx, c_ctx, ada_w, ada_b, norm1_g, w_in, s5_lam_re, s5_lam_im, s5_log_dt,
           s5_b_re, s5_b_im, s5_c_re, s5_c_im, s5_d, s5_w_glu, conv_w, conv_b, conv_ln_g,
           conv_ln_b, w_out, norm2_g, mlp_w1, mlp_w2, final_g):
    global _PROG
    f = lambda a: np.ascontiguousarray(np.asarray(a, dtype=np.float32))
    x, c, ctx, c_ctx = f(x), f(c), f(ctx), f(c_ctx)
    nb = x.shape[0]
    if _PROG is None:
        _PROG = build_program()
    nc = _PROG
    col8 = lambda v: f(np.asarray(v).reshape(8, 128).T)
    col4 = lambda v: f(np.asarray(v).reshape(4, 128).T)
    shared = {
        "ada_w": f(ada_w[0]),
        "ada_b": f(np.asarray(ada_b[0]).reshape(48, 128).T),
        "n1g": col8(norm1_g[0]), "n2g": col8(norm2_g[0]),
        "fgb": f(np.broadcast_to(np.asarray(final_g)[None, :], (128, D))),
        "w_in": f(w_in[0]), "w_out": f(w_out[0]), "w1": f(mlp_w1[0]), "w2": f(mlp_w2[0]),
        "w_glu": f(s5_w_glu[0]),
        "lam_re": f(np.asarray(s5_lam_re[0]).transpose(0, 2, 1).reshape(128, 32)),
        "lam_im": f(np.asarray(s5_lam_im[0]).transpose(0, 2, 1).reshape(128, 32)),
        "log_dt": f(np.repeat(np.asarray(s5_log_dt[0])[:, None, :], 64, axis=1).reshape(128, 32)),
        "b_re": f(np.asarray(s5_b_re[0]).transpose(0, 2, 1, 3).reshape(128, 32, 16)),
        "b_im": f(np.asarray(s5_b_im[0]).transpose(0, 2, 1, 3).reshape(128, 32, 16)),
        "c_re": f(np.asarray(s5_c_re[0]).transpose(0, 3, 1, 2).reshape(128, 32, 16)),
        "c_im": f(np.asarray(s5_c_im[0]).transpose(0, 3, 1, 2).reshape(128, 32, 16)),
        "d_skip": col4(np.asarray(s5_d[0]).reshape(512)),
        "conv_w": f(np.asarray(conv_w[0]).T.reshape(4, 128, 31).transpose(1, 0, 2)),
        "conv_b": col4(conv_b[0]), "ln_g": col4(conv_ln_g[0]), "ln_b": col4(conv_ln_b[0]),
        "cst": _consts(),
    }
    in_maps = []
    for b in range(nb):
        m = dict(shared)
        m["x"] = x[b]
        m["ctx"] = ctx[b]
        m["cvec"] = f(np.stack([c[b].reshape(8, 128).T, c_ctx.reshape(8, 128).T], axis=-1))
        in_maps.append(m)
    res = run_bass_kernel_spmd(nc, in_maps, core_ids=list(range(nb)))
    return np.stack([np.asarray(r["out"], dtype=np.float32) for r in res.results], axis=0)
```
